# Optimizing a Trainium2 kernel written in Bass

```python
import math, functools
import jax, jax.numpy as jnp
from jax import lax
import numpy as np

D_MODEL = 1024
BATCH = 4
SEQ = 4096
DEPTH = 1
DEC_BATCH = 128
DEC_SEQ = 1
PAST_LEN = 2048
PAGE_SIZE = 128

SSM_WIDTH = D_MODEL // 2
SSM_GROUP = 16
SSM_GROUPS = SSM_WIDTH // SSM_GROUP
SSM_STATE = 64
HEAD_DIM = 64
N_HEADS = D_MODEL // (2 * HEAD_DIM)
QK_DIM = 2 * HEAD_DIM
V_DIM = 2 * HEAD_DIM
ATTN_WIDTH = N_HEADS * QK_DIM
ROPE_DIM = HEAD_DIM // 4
ROPE_THETA = 500000.0
Q_BLOCK = 128
D_FF = -(-8 * D_MODEL // (3 * 256)) * 256
IN_WIDTH = SSM_WIDTH + 3 * ATTN_WIDTH + 2 * D_MODEL
EPS = 1e-6
F32 = jnp.float32

kernel_name = "hybrid_s5_diffattn_step"


def rmsnorm(x, g):
    x32 = x.astype(F32)
    r = x32 * lax.rsqrt(jnp.mean(x32 * x32, axis=-1, keepdims=True) + EPS)
    return (r * g.astype(F32)).astype(x.dtype)


def rope_partial(x, pos):
    half = ROPE_DIM // 2
    inv = ROPE_THETA ** (-jnp.arange(half, dtype=F32) * 2.0 / ROPE_DIM)
    ang = pos.astype(F32)[:, None] * inv[None, :]
    cos = jnp.cos(ang)[:, None, None, :]
    sin = jnp.sin(ang)[:, None, None, :]
    xr = x[..., :ROPE_DIM].astype(F32)
    x1, x2 = xr[..., :half], xr[..., half:]
    rot = jnp.concatenate([x1 * cos - x2 * sin, x2 * cos + x1 * sin], axis=-1).astype(x.dtype)
    return jnp.concatenate([rot, x[..., ROPE_DIM:]], axis=-1)


def diff_attn_core(q, k, v, q_pos, k_pos, lam):
    s = jnp.einsum("bqhcd,bkhcd->bhcqk", q, k, preferred_element_type=F32) * (HEAD_DIM ** -0.5)
    mask = k_pos[None, :] <= q_pos[:, None]
    s = jnp.where(mask, s, -jnp.inf)
    p = jax.nn.softmax(s, axis=-1)
    w = p[:, :, 0] - lam * p[:, :, 1]
    return jnp.einsum("bhqk,bkhe->bqhe", w.astype(v.dtype), v)


def prompt_attention(q, k, v, pos, lam):
    outs = []
    for s0 in range(0, q.shape[1], Q_BLOCK):
        e = s0 + Q_BLOCK
        outs.append(diff_attn_core(q[:, s0:e], k[:, :e], v[:, :e], pos[s0:e], pos[:e], lam))
    return jnp.concatenate(outs, axis=1)


def cached_attention(q, k, v, pos, lam, k_past, v_past):
    past = k_past.shape[1]
    k_all = jnp.concatenate([k_past.astype(k.dtype), k], axis=1)
    v_all = jnp.concatenate([v_past.astype(v.dtype), v], axis=1)
    k_pos = jnp.concatenate([jnp.arange(past), pos])
    return diff_attn_core(q, k_all, v_all, pos, k_pos, lam)


def ssm_branch(u, h0_re, h0_im, lam_re, lam_im, log_dt, b_re, b_im, c_re, c_im, d_skip):
    lam_re, lam_im = lam_re.astype(F32), lam_im.astype(F32)
    b_re, b_im = b_re.astype(F32), b_im.astype(F32)
    dt = jnp.exp(log_dt.astype(F32))[:, None]
    mag = jnp.exp(lam_re * dt)
    ar, ai = mag * jnp.cos(lam_im * dt), mag * jnp.sin(lam_im * dt)
    den = lam_re * lam_re + lam_im * lam_im
    fr = ((ar - 1.0) * lam_re + ai * lam_im) / den
    fi = (ai * lam_re - (ar - 1.0) * lam_im) / den
    bb_re = fr[..., None] * b_re - fi[..., None] * b_im
    bb_im = fr[..., None] * b_im + fi[..., None] * b_re
    B, T, _ = u.shape
    u32 = u.astype(F32)
    ug = u32.reshape(B, T, SSM_GROUPS, SSM_GROUP)
    bu_re = jnp.einsum("gpc,btgc->btgp", bb_re, ug)
    bu_im = jnp.einsum("gpc,btgc->btgp", bb_im, ug)
    a_re = jnp.broadcast_to(ar, bu_re.shape)
    a_im = jnp.broadcast_to(ai, bu_re.shape)

    def combine(e1, e2):
        a1r, a1i, b1r, b1i = e1
        a2r, a2i, b2r, b2i = e2
        return (a2r * a1r - a2i * a1i, a2r * a1i + a2i * a1r,
                a2r * b1r - a2i * b1i + b2r, a2r * b1i + a2i * b1r + b2i)

    Ar, Ai, Hr, Hi = lax.associative_scan(combine, (a_re, a_im, bu_re, bu_im), axis=1)
    h0r = h0_re.astype(F32)[:, None]
    h0i = h0_im.astype(F32)[:, None]
    hr = Ar * h0r - Ai * h0i + Hr
    hi = Ar * h0i + Ai * h0r + Hi
    y = (jnp.einsum("gcp,btgp->btgc", c_re.astype(F32), hr)
         - jnp.einsum("gcp,btgp->btgc", c_im.astype(F32), hi))
    y = y.reshape(B, T, SSM_WIDTH) + d_skip.astype(F32) * u32
    return y, hr[:, -1], hi[:, -1]


def trunk_layer(x, pos, h0_re, h0_im, attend, lam_init,
                norm_pre_mix, w_in, ssm_lambda_re, ssm_lambda_im, ssm_log_dt, ssm_b_re, ssm_b_im,
                ssm_c_re, ssm_c_im, ssm_d, w_glu_a, w_glu_b, lambda_q1, lambda_k1, lambda_q2,
                lambda_k2, subln_gain, w_o, norm_post_mix, norm_pre_ffn, w_gate, w_up, w_down,
                norm_post_ffn):
    B, T, _ = x.shape
    h = rmsnorm(x, norm_pre_mix)
    z = h @ w_in
    o0 = SSM_WIDTH
    o1 = o0 + ATTN_WIDTH
    o2 = o1 + ATTN_WIDTH
    o3 = o2 + ATTN_WIDTH
    o4 = o3 + D_MODEL
    u, q, k, v = z[..., :o0], z[..., o0:o1], z[..., o1:o2], z[..., o2:o3]
    g_s, g_a = z[..., o3:o4], z[..., o4:]
    y_s, hT_re, hT_im = ssm_branch(u, h0_re, h0_im, ssm_lambda_re, ssm_lambda_im, ssm_log_dt,
                                   ssm_b_re, ssm_b_im, ssm_c_re, ssm_c_im, ssm_d)
    y_s = jax.nn.gelu(y_s).astype(x.dtype)
    y_s = (y_s @ w_glu_a) * jax.nn.sigmoid(y_s @ w_glu_b)
    q = rope_partial(q.reshape(B, T, N_HEADS, 2, HEAD_DIM), pos)
    k = rope_partial(k.reshape(B, T, N_HEADS, 2, HEAD_DIM), pos)
    v = v.reshape(B, T, N_HEADS, V_DIM)
    lam = (jnp.exp(jnp.sum(lambda_q1.astype(F32) * lambda_k1.astype(F32)))
           - jnp.exp(jnp.sum(lambda_q2.astype(F32) * lambda_k2.astype(F32))) + lam_init)
    o = attend(q, k, v, pos, lam)
    o = (rmsnorm(o, subln_gain) * (1.0 - lam_init)).astype(x.dtype).reshape(B, T, ATTN_WIDTH)
    mixed = jax.nn.sigmoid(g_s) * y_s + jax.nn.sigmoid(g_a) * o
    x = x + rmsnorm(mixed @ w_o, norm_post_mix)
    hf = rmsnorm(x, norm_pre_ffn)
    f = (jax.nn.silu(hf @ w_gate) * (hf @ w_up)) @ w_down
    x = x + rmsnorm(f, norm_post_ffn)
    k_rows = k.reshape(B, T, N_HEADS, QK_DIM)
    return x, k_rows, v, hT_re, hT_im


def setup_inputs(seed: int = 0) -> dict:
    key = jax.random.key(seed)
    ks = jax.random.split(key, 32)
    n_pages = PAST_LEN // PAGE_SIZE
    n_used = DEC_BATCH * n_pages
    n_phys = (5 * n_used + 3) // 4
    nrm = lambda k, shape, s: jax.random.normal(k, shape, F32) * s
    page_table = jax.random.permutation(ks[0], n_phys)[:n_used].reshape(DEC_BATCH, n_pages).astype(jnp.int32)
    n_idx = jnp.arange(SSM_STATE, dtype=F32)
    lam_re = -0.5 + nrm(ks[1], (DEPTH, SSM_GROUPS, SSM_STATE), 0.01)
    lam_im = math.pi * n_idx[None, None, :] + nrm(ks[2], (DEPTH, SSM_GROUPS, SSM_STATE), 0.01)
    log_dt = jax.random.uniform(ks[3], (DEPTH, SSM_GROUPS), F32, math.log(1e-3), math.log(1e-1))
    gain = lambda k, n: 1.0 + nrm(k, (DEPTH, n), 0.02)
    return {
        "x_prompt": nrm(ks[4], (BATCH, SEQ, D_MODEL), 1.0),
        "x_sample": nrm(ks[5], (DEC_BATCH, DEC_SEQ, D_MODEL), 1.0),
        "cache_k": nrm(ks[6], (DEPTH, n_phys, PAGE_SIZE, N_HEADS, QK_DIM), 1.0),
        "cache_v": nrm(ks[7], (DEPTH, n_phys, PAGE_SIZE, N_HEADS, V_DIM), 1.0),
        "state_ssm_re": nrm(ks[8], (DEPTH, DEC_BATCH, SSM_GROUPS, SSM_STATE), 0.1),
        "state_ssm_im": nrm(ks[9], (DEPTH, DEC_BATCH, SSM_GROUPS, SSM_STATE), 0.1),
        "page_table": page_table,
        "norm_pre_mix": gain(ks[10], D_MODEL),
        "w_in": nrm(ks[11], (DEPTH, D_MODEL, IN_WIDTH), D_MODEL ** -0.5),
        "ssm_lambda_re": lam_re,
        "ssm_lambda_im": lam_im,
        "ssm_log_dt": log_dt,
        "ssm_b_re": nrm(ks[12], (DEPTH, SSM_GROUPS, SSM_STATE, SSM_GROUP), (2 * SSM_GROUP) ** -0.5),
        "ssm_b_im": nrm(ks[13], (DEPTH, SSM_GROUPS, SSM_STATE, SSM_GROUP), (2 * SSM_GROUP) ** -0.5),
        "ssm_c_re": nrm(ks[14], (DEPTH, SSM_GROUPS, SSM_GROUP, SSM_STATE), (2 * SSM_STATE) ** -0.5),
        "ssm_c_im": nrm(ks[15], (DEPTH, SSM_GROUPS, SSM_GROUP, SSM_STATE), (2 * SSM_STATE) ** -0.5),
        "ssm_d": nrm(ks[16], (DEPTH, SSM_WIDTH), 1.0),
        "w_glu_a": nrm(ks[17], (DEPTH, SSM_WIDTH, D_MODEL), SSM_WIDTH ** -0.5),
        "w_glu_b": nrm(ks[18], (DEPTH, SSM_WIDTH, D_MODEL), SSM_WIDTH ** -0.5),
        "lambda_q1": nrm(ks[19], (DEPTH, HEAD_DIM), 0.1),
        "lambda_k1": nrm(ks[20], (DEPTH, HEAD_DIM), 0.1),
        "lambda_q2": nrm(ks[21], (DEPTH, HEAD_DIM), 0.1),
        "lambda_k2": nrm(ks[22], (DEPTH, HEAD_DIM), 0.1),
        "subln_gain": gain(ks[23], V_DIM),
        "w_o": nrm(ks[24], (DEPTH, D_MODEL, D_MODEL), D_MODEL ** -0.5),
        "norm_post_mix": gain(ks[25], D_MODEL),
        "norm_pre_ffn": gain(ks[26], D_MODEL),
        "w_gate": nrm(ks[27], (DEPTH, D_MODEL, D_FF), D_MODEL ** -0.5),
        "w_up": nrm(ks[28], (DEPTH, D_MODEL, D_FF), D_MODEL ** -0.5),
        "w_down": nrm(ks[29], (DEPTH, D_FF, D_MODEL), D_FF ** -0.5),
        "norm_post_ffn": gain(ks[30], D_MODEL),
    }


def reference(x_prompt, x_sample, cache_k, cache_v, state_ssm_re, state_ssm_im, page_table,
              norm_pre_mix, w_in, ssm_lambda_re, ssm_lambda_im, ssm_log_dt, ssm_b_re, ssm_b_im,
              ssm_c_re, ssm_c_im, ssm_d, w_glu_a, w_glu_b, lambda_q1, lambda_k1, lambda_q2,
              lambda_k2, subln_gain, w_o, norm_post_mix, norm_pre_ffn, w_gate, w_up, w_down,
              norm_post_ffn):
    n_pages = PAST_LEN // PAGE_SIZE
    bp, tp = x_prompt.shape[0], x_prompt.shape[1]
    bs, ts = x_sample.shape[0], x_sample.shape[1]
    pos_p = jnp.arange(tp)
    pos_s = PAST_LEN + jnp.arange(ts)
    zeros = jnp.zeros((bp, SSM_GROUPS, SSM_STATE), F32)
    yp, ys = x_prompt, x_sample
    kp_l, vp_l, hpr_l, hpi_l, ks_l, vs_l, hsr_l, hsi_l = [], [], [], [], [], [], [], []
    for l in range(DEPTH):
        lam_init = 0.8 - 0.6 * math.exp(-0.3 * l)
        lw = (norm_pre_mix[l], w_in[l], ssm_lambda_re[l], ssm_lambda_im[l], ssm_log_dt[l],
              ssm_b_re[l], ssm_b_im[l], ssm_c_re[l], ssm_c_im[l], ssm_d[l], w_glu_a[l], w_glu_b[l],
              lambda_q1[l], lambda_k1[l], lambda_q2[l], lambda_k2[l], subln_gain[l], w_o[l],
              norm_post_mix[l], norm_pre_ffn[l], w_gate[l], w_up[l], w_down[l], norm_post_ffn[l])
        yp, kp, vp, hpr, hpi = trunk_layer(yp, pos_p, zeros, zeros, prompt_attention, lam_init, *lw)
        k_past = cache_k[l][page_table].reshape(bs, n_pages * PAGE_SIZE, N_HEADS, 2, HEAD_DIM)
        v_past = cache_v[l][page_table].reshape(bs, n_pages * PAGE_SIZE, N_HEADS, V_DIM)
        attend_s = functools.partial(cached_attention, k_past=k_past, v_past=v_past)
        ys, k_s, v_s, hsr, hsi = trunk_layer(ys, pos_s, state_ssm_re[l], state_ssm_im[l], attend_s,
                                             lam_init, *lw)
        kp_l.append(kp); vp_l.append(vp); hpr_l.append(hpr); hpi_l.append(hpi)
        ks_l.append(k_s); vs_l.append(v_s); hsr_l.append(hsr); hsi_l.append(hsi)
    return (yp, ys,
            jnp.stack(kp_l), jnp.stack(vp_l), jnp.stack(hpr_l), jnp.stack(hpi_l),
            jnp.stack(ks_l), jnp.stack(vs_l), jnp.stack(hsr_l), jnp.stack(hsi_l))
```

```python
import contextlib
import math
import numpy as np
import concourse.bass as bass
import concourse.mybir as mybir
from concourse.bass_utils import run_bass_kernel_spmd

F32 = mybir.dt.float32
BF16 = mybir.dt.bfloat16
I32 = mybir.dt.int32
AF = mybir.ActivationFunctionType
ALU = mybir.AluOpType
AX = mybir.AxisListType

D = 1024
NB = 32
NOWN = 16
EPS = 1e-6
LAM_INIT = 0.2


class T:
    __slots__ = ("ap", "w", "r", "name", "psum")

    def __init__(self, ap, name="", psum=False):
        self.psum = psum
        self.ap = ap
        self.w = None
        self.r = {}
        self.name = name

    def __getitem__(self, idx):
        return self.ap[idx]


class Eng:
    def __init__(self, fw, name, raw, selfsync=True):
        self.name = name
        self.raw = raw
        self.sem = fw.new_sem("e_" + name)
        self.n = 0
        self.seen = {}
        self.ops = []
        self.pending = []
        self.selfsync = selfsync


class FW:
    NDMA = 40

    def __init__(self, nc, stack):
        self.nc = nc
        self.stack = stack
        self.sems = {}
        self.pe = Eng(self, "pe", nc.tensor, selfsync=False)
        self.act = Eng(self, "act", nc.scalar)
        self.dve = Eng(self, "dve", nc.vector)
        self.pool = Eng(self, "pool", nc.gpsimd)
        self.sp = Eng(self, "sp", nc.sync)
        self.dma_sems = [self.new_sem("d%d" % i) for i in range(self.NDMA)]
        self.dma_cnt = [0] * self.NDMA
        self.dma_rr = 0
        self.out_events = []

    def new_sem(self, name):
        s = self.stack.enter_context(self.nc.semaphore(name))
        self.sems[id(s)] = s
        return s

    def sb(self, name, shape, dt, stack=None):
        st = stack or self.stack
        self.uid = getattr(self, 'uid', 0) + 1
        name = '%s_%d' % (name, self.uid)
        return T(st.enter_context(self.nc.sbuf_tensor(name, list(shape), dt)), name)

    def ps(self, name, shape, dt=F32):
        return T(self.stack.enter_context(self.nc.psum_tensor(name, list(shape), dt)), name, psum=True)

    def _deps(self, eng, reads, writes):
        need = {}
        for t in reads:
            if t.w is not None and need.get(t.w[0], 0) < t.w[1]:
                need[t.w[0]] = t.w[1]
            if t.psum:
                for s, v in t.r.items():
                    if s != id(eng.sem) and need.get(s, 0) < v:
                        need[s] = v
        for t in writes:
            if t.w is not None and need.get(t.w[0], 0) < t.w[1]:
                need[t.w[0]] = t.w[1]
            for s, v in t.r.items():
                if need.get(s, 0) < v:
                    need[s] = v
        waits = []
        for s, v in need.items():
            if s == id(eng.sem) and not eng.selfsync:
                continue
            if eng.seen.get(s, 0) >= v:
                continue
            eng.seen[s] = v
            waits.append((self.sems[s], v))
        if eng.pending:
            waits = eng.pending + waits
            eng.pending = []
        return waits

    def _mark(self, ev, reads, writes):
        for t in reads:
            if t.r.get(ev[0], 0) < ev[1]:
                t.r[ev[0]] = ev[1]
        for t in writes:
            t.w = ev
            t.r = {}

    stopped = False

    def barrier(self):
        engs = [self.pe, self.act, self.dve, self.pool, self.sp]
        for e in engs:
            for o in engs:
                if o is not e and o.n > 0 and e.seen.get(id(o.sem), 0) < o.n:
                    e.seen[id(o.sem)] = o.n
                    e.pending.append((o.sem, o.n))
            for k, sem in enumerate(self.dma_sems):
                v = self.dma_cnt[k]
                if v and e.seen.get(id(sem), 0) < v:
                    e.seen[id(sem)] = v
                    e.pending.append((sem, v))

    def ckpt(self, k, stop_after):
        if stop_after <= k:
            self.stopped = True

    def op(self, eng, fn, reads=(), writes=()):
        if self.stopped:
            return None
        waits = self._deps(eng, reads, writes)
        eng.n += 1
        ev = (id(eng.sem), eng.n)
        eng.ops.append((waits, fn, (eng.sem, 1)))
        self._mark(ev, reads, writes)
        return ev

    def dma(self, eng, out, in_, reads=(), writes=(), is_out=False, indirect=None):
        if self.stopped:
            return None
        k = self.dma_rr
        self.dma_rr = (k + 1) % self.NDMA
        sem = self.dma_sems[k]
        waits = self._deps(eng, reads, writes)
        prev = self.dma_cnt[k]
        if prev and eng.seen.get(id(sem), 0) < prev:
            eng.seen[id(sem)] = prev
            waits.append((sem, prev))
        self.dma_cnt[k] = prev + 16
        ev = (id(sem), prev + 16)
        if indirect is None:
            fn = lambda e: e.dma_start(out=out, in_=in_)
        else:
            fn = lambda e: e.indirect_dma_start(out=out, out_offset=None, in_=in_,
                                                in_offset=bass.IndirectOffsetOnAxis(ap=indirect, axis=0))
        eng.ops.append((waits, fn, (sem, 16)))
        self._mark(ev, reads, writes)
        if is_out:
            self.out_events.append(ev)
        return ev

    def mm(self, out, lhsT, rhs, start, stop, reads, writes):
        return self.op(self.pe, lambda e: e.matmul(out, lhsT=lhsT, rhs=rhs, start=start, stop=stop,
                                                   skip_group_check=True), reads, writes)

    def tr(self, out, in_, ident, reads, writes):
        return self.op(self.pe, lambda e: e.transpose(out, in_, ident), reads, writes)

    def actv(self, out, in_, func, reads, writes, scale=1.0, bias=None, accum=None, eng=None):
        kw = {}
        if bias is not None:
            kw["bias"] = bias
        if accum is not None:
            kw["accum_out"] = accum
        return self.op(self.act, lambda e: e.activation(out=out, in_=in_, func=func, scale=scale, **kw),
                       reads, writes)

    def tt(self, eng, out, in0, in1, op, reads, writes):
        return self.op(eng, lambda e: e.tensor_tensor(out=out, in0=in0, in1=in1, op=op), reads, writes)

    def stt(self, eng, out, in0, scalar, in1, op0, op1, reads, writes):
        return self.op(eng, lambda e: e.scalar_tensor_tensor(out=out, in0=in0, scalar=scalar, in1=in1,
                                                             op0=op0, op1=op1), reads, writes)

    def ts(self, eng, out, in0, s1, s2, op0, op1, reads, writes):
        return self.op(eng, lambda e: e.tensor_scalar(out=out, in0=in0, scalar1=s1, scalar2=s2, op0=op0,
                                                      op1=op1), reads, writes)

    def cp(self, eng, out, in_, reads, writes):
        if eng is self.act:
            return self.op(eng, lambda e: e.copy(out=out, in_=in_), reads, writes)
        return self.op(eng, lambda e: e.tensor_copy(out=out, in_=in_), reads, writes)

    def red(self, eng, out, in_, reads, writes, op=ALU.add):
        return self.op(eng, lambda e: e.tensor_reduce(out=out, in_=in_, axis=AX.X, op=op), reads, writes)

    def mset(self, eng, out, val, writes):
        return self.op(eng, lambda e: e.memset(out, val), (), writes)

    def recip(self, out, in_, reads, writes):
        return self.op(self.dve, lambda e: e.reciprocal(out=out, in_=in_), reads, writes)

    def finish(self):
        final = {}
        for s, v in self.out_events:
            if final.get(s, 0) < v:
                final[s] = v
        nc = self.nc
        with nc.Block() as block:
            def emit(e, last=False):
                def body(raw):
                    for waits, fn, inc in e.ops:
                        for s, v in waits:
                            raw.wait_ge(s, v)
                        fn(raw).then_inc(inc[0], inc[1])
                    if last:
                        for s, v in final.items():
                            raw.wait_ge(self.sems[s], v)
                return body
            block.tensor(emit(self.pe))
            block.scalar(emit(self.act))
            block.vector(emit(self.dve))
            block.gpsimd(emit(self.pool))
            block.sync(emit(self.sp, True))


class Stop(Exception):
    pass


class Ring:
    def __init__(self, tiles):
        self.tiles = tiles
        self.i = 0

    def get(self):
        t = self.tiles[self.i % len(self.tiles)]
        self.i += 1
        return t


def build_nc(stop_after=99, cache_rows=2560 * 128, dbg=False):
    nc = bass.Bass("TRN2", target_bir_lowering=False)

    def din(name, shape, dt=F32):
        return nc.dram_tensor(name, list(shape), dt, kind="ExternalInput").ap()

    def dout(name, shape, dt=F32):
        return nc.dram_tensor(name, list(shape), dt, kind="ExternalOutput").ap()

    xa = din("xa", [33 * 128, D])
    xo = din("xo", [17 * 128, D])
    w_in = din("w_in", [D, 5632])
    wga = din("w_glu_a", [512, D]); wgb = din("w_glu_b", [512, D])
    w_o = din("w_o", [D, D])
    w_gate = din("w_gate", [D, 2816]); w_up = din("w_up", [D, 2816]); w_down = din("w_down", [2816, D])
    n_pre = din("n_pre", [1, D]); n_post = din("n_post", [1, D]); n_pf = din("n_pf", [1, D]); n_postf = din("n_postf", [1, D])
    subln = din("subln", [1, 128])
    lamq = din("lamq", [1, 256])
    lamre = din("lamre", [16, 128]); lamim = din("lamim", [16, 128]); logdt = din("logdt", [16, 2])
    bre = din("bre", [16, 128, 16]); bim = din("bim", [16, 128, 16])
    cre = din("cre", [4, 128, 64]); cim = din("cim", [4, 128, 64])
    dsk = din("dsk", [128, 4])
    cache_k = din("cache_k", [cache_rows, D]); cache_v = din("cache_v", [cache_rows, D])
    ptab = din("ptab", [1, 256], I32)
    s0re = din("s0re", [16, 2048]); s0im = din("s0im", [16, 2048])
    c_ident = din("c_ident", [128, 128]); c_tri = din("c_tri", [128, 128])
    c_maskA = din("c_maskA", [128, 128]); c_maskB = din("c_maskB", [128, 128])
    c_maskZ = din("c_maskZ", [128, 512]); c_maskC = din("c_maskC", [128, 512])
    c_sel = din("c_sel", [128, 2])
    c_cosA = din("c_cosA", [128, 33 * 8]); c_sinA = din("c_sinA", [128, 33 * 8])
    c_cosO = din("c_cosO", [128, 17 * 8]); c_sinO = din("c_sinO", [128, 17 * 8])
    c_selb = din("c_selb", [16, 16 * 128])
    c_iota = din("c_iota", [128, 1])

    o_y = dout("o_y", [17 * 128, D])
    o_k = dout("o_k", [33 * 128, D]); o_v = dout("o_v", [33 * 128, D])
    o_hre = dout("o_hre", [128, 16]); o_him = dout("o_him", [128, 16])
    o_sre = dout("o_sre", [128, 2048]); o_sim = dout("o_sim", [128, 2048])

    x1scr = nc.dram_tensor("x1scr", [17 * 128, D], F32, kind="Internal").ap()
    oscr = nc.dram_tensor("oscr", [17 * 128, D], BF16, kind="Internal").ap()

    with contextlib.ExitStack() as st:
        fw = FW(nc, st)
        sm = contextlib.ExitStack()
        pe, act, dve, pool, sp = fw.pe, fw.act, fw.dve, fw.pool, fw.sp
        PS = [fw.ps("bank%d" % i, [128, 512]) for i in range(8)]

        def psbf(t):
            return t.ap[:].bitcast(BF16)

        ident = fw.sb("ident", [128, 128], F32)
        identb = fw.sb("identb", [128, 128], BF16)
        trib = fw.sb("trib", [128, 128], BF16)
        maskA = fw.sb("maskA", [128, 128], BF16); maskB = fw.sb("maskB", [128, 128], BF16)
        sel = fw.sb("sel", [128, 2], F32)
        lam = fw.sb("lam", [128, 1], F32)
        xring = Ring([fw.sb("x%d" % i, [128, D], F32) for i in range(2)])
        hring = Ring([fw.sb("h%d" % i, [128, D], BF16) for i in range(2)])
        hTring = Ring([fw.sb("hT%d" % i, [128, 8, 128], BF16) for i in range(2)])
        junk = fw.sb("junk", [128, D], BF16)
        sring = Ring([fw.sb("st%d" % i, [128, 4], F32) for i in range(4)])

        gB = {}
        for nm, src in (("pf", n_pf), ("postf", n_postf), ("pre", n_pre), ("post", n_post)):
            gB[nm] = fw.sb("g_" + nm, [128, D], F32, sm if nm in ("pre", "post") else None)
            fw.dma(sp, gB[nm][:], src.partition_broadcast(128), writes=[gB[nm]])
        sublnB = fw.sb("sublnB", [128, 128], F32, sm)
        fw.dma(sp, sublnB[:], subln.partition_broadcast(128), writes=[sublnB])
        fw.dma(sp, ident[:], c_ident, writes=[ident])
        fw.dma(pool, identb[:], c_ident, writes=[identb])
        fw.dma(pool, trib[:], c_tri, writes=[trib])
        fw.dma(pool, maskA[:], c_maskA, writes=[maskA])
        fw.dma(pool, maskB[:], c_maskB, writes=[maskB])
        fw.dma(sp, sel[:], c_sel, writes=[sel])
        cosA = fw.sb("cosA", [128, 33, 8], F32, sm); sinA = fw.sb("sinA", [128, 33, 8], F32, sm)
        cosO = fw.sb("cosO", [128, 17, 8], F32, sm); sinO = fw.sb("sinO", [128, 17, 8], F32, sm)
        fw.dma(sp, cosA[:].rearrange("p a b -> p (a b)"), c_cosA, writes=[cosA])
        fw.dma(sp, sinA[:].rearrange("p a b -> p (a b)"), c_sinA, writes=[sinA])
        fw.dma(sp, cosO[:].rearrange("p a b -> p (a b)"), c_cosO, writes=[cosO])
        fw.dma(sp, sinO[:].rearrange("p a b -> p (a b)"), c_sinO, writes=[sinO])
        lq = fw.sb("lq", [128, 256], F32, sm)
        fw.dma(sp, lq[:], lamq.partition_broadcast(128), writes=[lq])
        lprod = fw.sb("lprod", [128, 2, 64], F32, sm)
        lsum = fw.sb("lsum", [128, 2], F32, sm)
        lqv = lq[:].rearrange("p (a b c) -> p a b c", a=2, b=2)
        fw.tt(dve, lprod[:], lqv[:, :, 0, :], lqv[:, :, 1, :], ALU.mult, [lq], [lprod])
        fw.red(dve, lsum[:], lprod[:], [lprod], [lsum])
        fw.actv(lsum[:], lsum[:], AF.Exp, [lsum], [lsum])
        fw.tt(dve, lam[:], lsum[:, 0:1], lsum[:, 1:2], ALU.subtract, [lsum], [lam])
        fw.ts(dve, lam[:], lam[:], LAM_INIT, None, ALU.add, ALU.bypass, [lam], [lam])

        ysown = fw.sb("ysown", [128, 17, 4, 128], BF16, sm)
        ks_t = fw.sb("ks_t", [128, D], F32, sm); vs_t = fw.sb("vs_t", [128, D], F32, sm); qs_t = fw.sb("qs_t", [128, D], F32, sm)

        def rstd_from(ssq_ap, stt, n):
            fw.ts(dve, stt[:, 1:2], ssq_ap, 1.0 / n, EPS, ALU.mult, ALU.add, [stt], [stt])
            fw.actv(stt[:, 2:3], stt[:, 1:2], AF.Sqrt, [stt], [stt])
            fw.recip(stt[:, 3:4], stt[:, 2:3], [stt], [stt])
            return stt[:, 3:4]

        def norm_h(xt, g):
            stt = sring.get()
            fw.actv(junk[:], xt[:], AF.Square, [xt], [junk, stt], accum=stt[:, 0:1])
            r = rstd_from(stt[:, 0:1], stt, D)
            h = hring.get()
            fw.stt(dve, h[:], xt[:], r, g[:], ALU.mult, ALU.mult, [xt, stt, g], [h])
            hT = hTring.get()
            pb = psbf(PS[0])
            for kt in range(8):
                fw.tr(pb[:, kt * 128:(kt + 1) * 128], h[:, kt * 128:(kt + 1) * 128], identb[:], [h, identb], [PS[0]])
            fw.cp(act, hT[:].rearrange("p a b -> p (a b)"), pb, [PS[0]], [hT])
            return hT

        def load_w(dst, src_ap, nk, c0, c1, rows0=0):
            for kt in range(nk):
                fw.dma(pool, dst[:, kt, :], src_ap[rows0 + kt * 128: rows0 + (kt + 1) * 128, c0:c1], writes=[dst])

        fw.ckpt(0, stop_after)
        with contextlib.ExitStack() as s1:
            Wu = fw.sb("Wu", [128, 8, 512], BF16, s1)
            load_w(Wu, w_in, 8, 0, 512)
            BbT = fw.sb("BbT", [128, 4, 1024], BF16, s1)
            Cpad = fw.sb("Cpad", [128, 16, 2, 128], BF16, s1)
            Wt = fw.sb("Wt", [128, 16, 2, 128], F32, s1)
            Atr = fw.sb("Atr", [128, 16, 128], F32, s1); Ati = fw.sb("Ati", [128, 16, 128], F32, s1)
            a_r = fw.sb("a_r", [128, 16], F32, s1); a_i = fw.sb("a_i", [128, 16], F32, s1)
            Dcol = fw.sb("Dcol", [128, 4], F32, s1)
            with contextlib.ExitStack() as s0:
                nat = fw.sb("nat", [16, 3, 128], F32, s0)
                fw.dma(sp, nat[:, 0, :], lamre, writes=[nat])
                fw.dma(sp, nat[:, 1, :], lamim, writes=[nat])
                ldt = fw.sb("ldt", [16, 2], F32, s0)
                fw.dma(sp, ldt[:], logdt, writes=[ldt])
                fw.cp(dve, nat[:, 2, :].rearrange("p (a b) -> p a b", a=2), ldt[:].unsqueeze(2).to_broadcast([16, 2, 64]), [ldt, nat], [nat])
                L = fw.sb("L", [128, 3, 16], F32, s0)
                for i in range(3):
                    fw.tr(PS[1][:, i * 16:(i + 1) * 16], nat[:, i, :], ident[0:16, 0:16], [nat, ident], [PS[1]])
                fw.cp(dve, L[:].rearrange("p a b -> p (a b)"), PS[1][:, 0:48], [PS[1]], [L])
                lr = L[:, 0, :]; li = L[:, 1, :]
                w = fw.sb("wk", [128, 12, 16], F32, s0)
                W = lambda i: w[:, i, :]
                def horner(dst, x, coefs, rd):
                    q = W(11)
                    fw.ts(dve, q, x, float(coefs[-1]), None, ALU.mult, ALU.bypass, rd + [w], [w])
                    for c in coefs[-2:0:-1]:
                        fw.stt(dve, q, q, float(c), x, ALU.add, ALU.mult, rd + [w], [w])
                    fw.ts(dve, dst, q, float(coefs[0]), None, ALU.add, ALU.bypass, [w], [w])
                ecoef = [1.0 / math.factorial(k) for k in range(10)]
                fw.ts(dve, W(2), L[:, 2, :], 1.0 / 16, None, ALU.mult, ALU.bypass, [L], [w])
                horner(W(0), W(2), ecoef, [])
                for _ in range(4):
                    fw.tt(dve, W(0), W(0), W(0), ALU.mult, [w], [w])
                fw.tt(dve, W(3), lr, W(0), ALU.mult, [L, w], [w])
                horner(W(1), W(3), ecoef, [])
                fw.ts(dve, W(3), W(3), -1.0, None, ALU.mult, ALU.bypass, [w], [w])
                horner(W(9), W(3), ecoef, [])
                fw.tt(dve, W(2), li, W(0), ALU.mult, [L, w], [w])
                ki = fw.sb("ki", [128, 16], I32, s0)
                C1 = 6.28125
                C2 = 2 * math.pi - C1
                fw.ts(dve, ki[:], W(2), 1.0 / (2 * math.pi), None, ALU.mult, ALU.bypass, [w], [ki])
                fw.cp(dve, W(3), ki[:], [ki], [w])
                fw.stt(dve, W(4), W(3), -C1, W(2), ALU.mult, ALU.add, [w], [w])
                fw.stt(dve, W(4), W(3), -C2, W(4), ALU.mult, ALU.add, [w], [w])
                fw.ts(dve, W(3), W(4), math.pi, -2 * math.pi, ALU.is_gt, ALU.mult, [w], [w])
                fw.tt(dve, W(4), W(4), W(3), ALU.add, [w], [w])
                fw.ts(dve, W(3), W(4), -math.pi, 2 * math.pi, ALU.is_lt, ALU.mult, [w], [w])
                fw.tt(dve, W(4), W(4), W(3), ALU.add, [w], [w])
                fw.tt(dve, W(5), W(4), W(4), ALU.mult, [w], [w])
                scoef = [(-1.0) ** k / math.factorial(2 * k + 1) for k in range(12)]
                ccoef = [(-1.0) ** k / math.factorial(2 * k) for k in range(13)]
                horner(W(6), W(5), ccoef, [])
                horner(W(3), W(5), scoef, [])
                fw.tt(dve, W(4), W(3), W(4), ALU.mult, [w], [w])
                fw.tt(dve, a_r[:], W(1), W(6), ALU.mult, [w], [a_r])
                fw.tt(dve, a_i[:], W(1), W(4), ALU.mult, [w], [a_i])
                fw.tt(dve, W(10), W(9), W(4), ALU.mult, [w], [w])
                fw.ts(dve, W(10), W(10), -1.0, None, ALU.mult, ALU.bypass, [w], [w])
                fw.tt(dve, W(9), W(9), W(6), ALU.mult, [w], [w])
                fw.tt(dve, W(0), lr, lr, ALU.mult, [L], [w])
                fw.tt(dve, W(1), li, li, ALU.mult, [L], [w])
                fw.tt(dve, W(0), W(0), W(1), ALU.add, [w], [w])
                fw.recip(W(0), W(0), [w], [w])
                fw.ts(dve, W(1), a_r[:], -1.0, None, ALU.add, ALU.bypass, [a_r], [w])
                fw.tt(dve, W(2), W(1), lr, ALU.mult, [w, L], [w])
                fw.tt(dve, W(3), a_i[:], li, ALU.mult, [a_i, L], [w])
                fw.tt(dve, W(2), W(2), W(3), ALU.add, [w], [w])
                fw.tt(dve, W(7), W(2), W(0), ALU.mult, [w], [w])
                fw.tt(dve, W(2), a_i[:], lr, ALU.mult, [a_i, L], [w])
                fw.tt(dve, W(3), W(1), li, ALU.mult, [w, L], [w])
                fw.tt(dve, W(2), W(2), W(3), ALU.subtract, [w], [w])
                fw.tt(dve, W(8), W(2), W(0), ALU.mult, [w], [w])
                bL = fw.sb("bL", [128, 2, 16, 16], F32, s0)
                fw.dma(sp, bL[:, 0, :, :], bre.rearrange("g q c -> q g c"), writes=[bL])
                fw.dma(sp, bL[:, 1, :, :], bim.rearrange("g q c -> q g c"), writes=[bL])
                bb = fw.sb("bb", [128, 2, 16, 16], F32, s0)
                tmpb = fw.sb("tmpb", [128, 16, 16], F32, s0)
                frb = W(7).unsqueeze(2).to_broadcast([128, 16, 16]); fib = W(8).unsqueeze(2).to_broadcast([128, 16, 16])
                fw.tt(dve, bb[:, 0], bL[:, 0], frb, ALU.mult, [bL, w], [bb])
                fw.tt(dve, tmpb[:], bL[:, 1], fib, ALU.mult, [bL, w], [tmpb])
                fw.tt(dve, bb[:, 0], bb[:, 0], tmpb[:], ALU.subtract, [bb, tmpb], [bb])
                fw.tt(dve, bb[:, 1], bL[:, 1], frb, ALU.mult, [bL, w], [bb])
                fw.tt(dve, tmpb[:], bL[:, 0], fib, ALU.mult, [bL, w], [tmpb])
                fw.tt(dve, bb[:, 1], bb[:, 1], tmpb[:], ALU.add, [bb, tmpb], [bb])
                mZ = fw.sb("mZ", [128, 4, 128], F32, s0); mC = fw.sb("mC", [128, 4, 128], F32, s0)
                fw.dma(sp, mZ[:].rearrange("p a b -> p (a b)"), c_maskZ, writes=[mZ])
                fw.dma(sp, mC[:].rearrange("p a b -> p (a b)"), c_maskC, writes=[mC])
                Zr = Ring([fw.sb("Z%d" % i, [128, 128], F32, s0) for i in range(2)])
                for ct in range(4):
                    for gpl in range(4):
                        gp = 4 * ct + gpl
                        for ri in range(2):
                            Z = Zr.get()
                            fw.tt(dve, Z[:].rearrange("p (a b) -> p a b", a=8), mZ[:, gpl, :].rearrange("p (a b) -> p a b", a=8),
                                  bb[:, ri, gp, :].unsqueeze(1).to_broadcast([128, 8, 16]), ALU.mult, [mZ, bb], [Z])
                            bk = PS[2 + (gp * 2 + ri) % 2]
                            fw.tr(bk[:, 0:128], Z[:], ident[:], [Z, ident], [bk])
                            fw.cp(act, BbT[:, ct, (gpl * 2 + ri) * 128:(gpl * 2 + ri + 1) * 128], bk[:, 0:128], [bk], [BbT])
                Cn = fw.sb("Cn", [128, 2, 4, 64], F32, s0)
                fw.dma(sp, Cn[:, 0], cre.rearrange("c q p -> q c p"), writes=[Cn])
                fw.dma(sp, Cn[:, 1], cim.rearrange("c q p -> q c p"), writes=[Cn])
                for ct in range(4):
                    for gpl in range(4):
                        gp = 4 * ct + gpl
                        for ri in range(2):
                            Z = Zr.get()
                            fw.tt(dve, Z[:].rearrange("p (a b) -> p a b", a=2), mC[:, gpl, :].rearrange("p (a b) -> p a b", a=2),
                                  Cn[:, ri, ct, :].unsqueeze(1).to_broadcast([128, 2, 64]), ALU.mult, [mC, Cn], [Z])
                            bk = PS[2 + (gp * 2 + ri) % 2]
                            fw.tr(bk[:, 0:128], Z[:], ident[:], [Z, ident], [bk])
                            fw.actv(Cpad[:, gp, ri, :], bk[:, 0:128], AF.Copy, [bk], [Cpad], scale=(1.0 if ri == 0 else -1.0))
                fw.dma(sp, Dcol[:], dsk, writes=[Dcol])
                Avr = fw.sb("Avr", [128, 16, 128], F32, s0); Avi = fw.sb("Avi", [128, 16, 128], F32, s0)
                pt1 = fw.sb("pt1", [128, 16, 64], F32, s0); pt2 = fw.sb("pt2", [128, 16, 64], F32, s0)

                def cpow(Pr, Pi, br, bi, rd):
                    fw.cp(dve, Pr[:, :, 0:1], br.unsqueeze(2), rd, [Pr])
                    fw.cp(dve, Pi[:, :, 0:1], bi.unsqueeze(2), rd, [Pi])
                    k = 1
                    while k < 128:
                        akr = Pr[:, :, k - 1:k].to_broadcast([128, 16, k]); aki = Pi[:, :, k - 1:k].to_broadcast([128, 16, k])
                        fw.tt(dve, pt1[:, :, 0:k], Pr[:, :, 0:k], akr, ALU.mult, [Pr], [pt1])
                        fw.tt(dve, pt2[:, :, 0:k], Pi[:, :, 0:k], aki, ALU.mult, [Pi], [pt2])
                        fw.tt(dve, Pr[:, :, k:2 * k], pt1[:, :, 0:k], pt2[:, :, 0:k], ALU.subtract, [pt1, pt2, Pr], [Pr])
                        fw.tt(dve, pt1[:, :, 0:k], Pr[:, :, 0:k], aki, ALU.mult, [Pr, Pi], [pt1])
                        fw.tt(dve, pt2[:, :, 0:k], Pi[:, :, 0:k], akr, ALU.mult, [Pr, Pi], [pt2])
                        fw.tt(dve, Pi[:, :, k:2 * k], pt1[:, :, 0:k], pt2[:, :, 0:k], ALU.add, [pt1, pt2, Pi], [Pi])
                        k *= 2
                cpow(Atr, Ati, a_r[:], a_i[:], [a_r, a_i])
                cpow(Avr, Avi, W(9), W(10), [w])
                for gp in range(16):
                    for ri, Av in ((0, Avr), (1, Avi)):
                        bk = PS[2 + (gp * 2 + ri) % 2]
                        fw.tr(bk[:, 0:128], Av[:, gp, :], ident[:], [Av, ident], [bk])
                        fw.cp(act, Wt[:, gp, ri, :], bk[:, 0:128], [bk], [Wt])

            fw.barrier()
            fw.ckpt(1, stop_after)
            uTb_r = Ring([fw.sb("uTb%d" % i, [128, 4, 128], BF16, s1) for i in range(2)])
            uTf_r = Ring([fw.sb("uTf%d" % i, [128, 4, 128], F32, s1) for i in range(2)])
            X_r = Ring([fw.sb("X%d" % i, [128, 4, 2, 128], BF16, s1) for i in range(2)])
            e1 = fw.sb("e1", [128, 4, 128], F32, s1); e2 = fw.sb("e2", [128, 4, 128], F32, s1)
            e3 = fw.sb("e3", [128, 4, 128], F32, s1); e4 = fw.sb("e4", [128, 4, 128], F32, s1)
            Gr = fw.sb("Gr", [128, 4, 128], F32, s1); Gi = fw.sb("Gi", [128, 4, 128], F32, s1)
            Hr = fw.sb("Hr", [128, 4, 128], F32, s1); Hi = fw.sb("Hi", [128, 4, 128], F32, s1)
            Hb_r = Ring([fw.sb("Hb%d" % i, [128, 2, 4, 128], BF16, s1) for i in range(2)])
            Hc = fw.sb("Hc", [128, 2, 16], F32, s1)
            fw.mset(dve, Hc[:], 0.0, [Hc])
            yall = fw.sb("yall", [128, 2, 4, 128], BF16, s1)
            yt1 = fw.sb("yt1", [128, 4, 128], F32, s1); yt2 = fw.sb("yt2", [128, 4, 128], F32, s1); yt3 = fw.sb("yt3", [128, 4, 128], F32, s1)
            S0 = fw.sb("S0", [128, 2048], F32, s1)
            H0 = fw.sb("H0", [128, 2, 16, 128], F32, s1)
            So = fw.sb("So", [128, 2048], F32, s1)

            for b in range(33):
                samp = (b == 32)
                xt = xring.get()
                fw.dma(sp, xt[:], xa[b * 128:(b + 1) * 128, :], writes=[xt])
                hT = norm_h(xt, gB["pre"])
                for ct in range(4):
                    for kt in range(8):
                        fw.mm(PS[1][:, ct * 128:(ct + 1) * 128], Wu[:, kt, ct * 128:(ct + 1) * 128], hT[:, kt, :], kt == 0, kt == 7, [Wu, hT], [PS[1]])
                uTb = uTb_r.get(); uTf = uTf_r.get()
                fw.cp(act, uTb[:].rearrange("p a b -> p (a b)"), PS[1][:], [PS[1]], [uTb])
                fw.cp(dve, uTf[:].rearrange("p a b -> p (a b)"), PS[1][:], [PS[1]], [uTf])
                if samp:
                    for ri, src in ((0, s0re), (1, s0im)):
                        fw.mset(dve, S0[:], 0.0, [S0])
                        fw.dma(sp, S0[0:16, :], src, writes=[S0])
                        for gp in range(16):
                            bk = PS[2 + gp % 2]
                            fw.tr(bk[:, 0:128], S0[:, gp * 128:(gp + 1) * 128], ident[:], [S0, ident], [bk])
                            fw.cp(act, H0[:, ri, gp, :], bk[:, 0:128], [bk], [H0])
                for ct in range(4):
                    gps = slice(4 * ct, 4 * ct + 4)
                    if not samp:
                        for hf in range(2):
                            fw.mm(PS[2 + hf][:], uTb[:, ct, :], BbT[:, ct, hf * 512:(hf + 1) * 512], True, True, [uTb, BbT], [PS[2 + hf]])
                        X = X_r.get()
                        for hf in range(2):
                            Bv = PS[2 + hf][:].rearrange("p (a r q) -> p a r q", a=2, r=2)
                            gsl = slice(4 * ct + 2 * hf, 4 * ct + 2 * hf + 2)
                            sl = slice(2 * hf, 2 * hf + 2)
                            fw.tt(dve, e1[:, sl, :], Bv[:, :, 0, :], Wt[:, gsl, 0, :], ALU.mult, [PS[2 + hf], Wt], [e1])
                            fw.tt(dve, e2[:, sl, :], Bv[:, :, 1, :], Wt[:, gsl, 1, :], ALU.mult, [PS[2 + hf], Wt], [e2])
                            fw.tt(dve, e3[:, sl, :], Bv[:, :, 0, :], Wt[:, gsl, 1, :], ALU.mult, [PS[2 + hf], Wt], [e3])
                            fw.tt(dve, e4[:, sl, :], Bv[:, :, 1, :], Wt[:, gsl, 0, :], ALU.mult, [PS[2 + hf], Wt], [e4])
                        fw.tt(pool, X[:, :, 0, :], e1[:], e2[:], ALU.subtract, [e1, e2], [X])
                        fw.tt(pool, X[:, :, 1, :], e3[:], e4[:], ALU.add, [e3, e4], [X])
                        for gpl in range(4):
                            for ri in range(2):
                                fw.mm(PS[4 + ri][:, gpl * 128:(gpl + 1) * 128], X[:, gpl, ri, :], trib[:], True, True, [X, trib], [PS[4 + ri]])
                        cr = PS[4][:].rearrange("p (a b) -> p a b", a=4); ci = PS[5][:].rearrange("p (a b) -> p a b", a=4)
                        fw.tt(dve, Gr[:], cr, Hc[:, 0, gps].unsqueeze(2).to_broadcast([128, 4, 128]), ALU.add, [PS[4], Hc], [Gr])
                        fw.tt(dve, Gi[:], ci, Hc[:, 1, gps].unsqueeze(2).to_broadcast([128, 4, 128]), ALU.add, [PS[5], Hc], [Gi])
                        fw.tt(pool, e1[:], Gr[:], Atr[:, gps, :], ALU.mult, [Gr, Atr], [e1])
                        fw.tt(pool, e2[:], Gi[:], Ati[:, gps, :], ALU.mult, [Gi, Ati], [e2])
                        fw.tt(dve, e3[:], Gr[:], Ati[:, gps, :], ALU.mult, [Gr, Ati], [e3])
                        fw.tt(dve, e4[:], Gi[:], Atr[:, gps, :], ALU.mult, [Gi, Atr], [e4])
                        fw.tt(pool, Hr[:], e1[:], e2[:], ALU.subtract, [e1, e2], [Hr])
                        fw.tt(dve, Hi[:], e3[:], e4[:], ALU.add, [e3, e4], [Hi])
                        fw.cp(dve, Hc[:, 0, gps], Hr[:, :, 127], [Hr], [Hc])
                        fw.cp(dve, Hc[:, 1, gps], Hi[:, :, 127], [Hi], [Hc])
                    else:
                        for gpl in range(4):
                            for ri in range(2):
                                fw.mm(PS[4 + ri][:, gpl * 128:(gpl + 1) * 128], BbT[:, ct, (gpl * 2 + ri) * 128:(gpl * 2 + ri + 1) * 128],
                                      uTb[:, ct, :], True, True, [uTb, BbT], [PS[4 + ri]])
                        cr = PS[4][:].rearrange("p (a b) -> p a b", a=4); ci = PS[5][:].rearrange("p (a b) -> p a b", a=4)
                        arb = a_r[:, gps].unsqueeze(2).to_broadcast([128, 4, 128]); aib = a_i[:, gps].unsqueeze(2).to_broadcast([128, 4, 128])
                        fw.tt(dve, e1[:], H0[:, 0, gps, :], arb, ALU.mult, [H0, a_r], [e1])
                        fw.tt(dve, e2[:], H0[:, 1, gps, :], aib, ALU.mult, [H0, a_i], [e2])
                        fw.tt(dve, e3[:], H0[:, 0, gps, :], aib, ALU.mult, [H0, a_i], [e3])
                        fw.tt(dve, e4[:], H0[:, 1, gps, :], arb, ALU.mult, [H0, a_r], [e4])
                        fw.tt(dve, e1[:], e1[:], e2[:], ALU.subtract, [e1, e2], [e1])
                        fw.tt(dve, e3[:], e3[:], e4[:], ALU.add, [e3, e4], [e3])
                        fw.tt(dve, Hr[:], e1[:], cr, ALU.add, [e1, PS[4]], [Hr])
                        fw.tt(dve, Hi[:], e3[:], ci, ALU.add, [e3, PS[5]], [Hi])
                    Hb = Hb_r.get()
                    fw.cp(act, Hb[:, 0], Hr[:], [Hr], [Hb])
                    fw.cp(act, Hb[:, 1], Hi[:], [Hi], [Hb])
                    if samp:
                        for ri, Hx in ((0, Hr), (1, Hi)):
                            for gpl in range(4):
                                fw.tr(PS[7][:, gpl * 128:(gpl + 1) * 128], Hx[:, gpl, :], ident[:], [Hx, ident], [PS[7]])
                            fw.cp(dve, H0[:, ri, gps, :].rearrange("p a b -> p (a b)"), PS[7][:], [PS[7]], [H0])
                    for gpl in range(4):
                        for ri in range(2):
                            fw.mm(PS[6][:, ct * 128:(ct + 1) * 128], Cpad[:, 4 * ct + gpl, ri, :], Hb[:, ri, gpl, :],
                                  gpl == 0 and ri == 0, gpl == 3 and ri == 1, [Cpad, Hb], [PS[6]])
                fw.tt(pool, yt1[:], uTf[:], Dcol[:].unsqueeze(2).to_broadcast([128, 4, 128]), ALU.mult, [uTf, Dcol], [yt1])
                fw.tt(dve, yt1[:], yt1[:], PS[6][:].rearrange("p (a b) -> p a b", a=4), ALU.add, [yt1, PS[6]], [yt1])
                fw.tt(pool, yt2[:], yt1[:], yt1[:], ALU.mult, [yt1], [yt2])
                fw.ts(pool, yt2[:], yt2[:], 0.044715, 1.0, ALU.mult, ALU.add, [yt2], [yt2])
                fw.tt(pool, yt2[:], yt2[:], yt1[:], ALU.mult, [yt2, yt1], [yt2])
                fw.actv(yt3[:], yt2[:], AF.Sigmoid, [yt2], [yt3], scale=2.0 * math.sqrt(2.0 / math.pi))
                if samp:
                    fw.tt(dve, ysown[:, 16], yt1[:], yt3[:], ALU.mult, [yt1, yt3], [ysown])
                    fw.dma(sp, o_sre, H0[:, 0].rearrange("p a b -> p (a b)"), reads=[H0], is_out=True)
                    fw.dma(sp, o_sim, H0[:, 1].rearrange("p a b -> p (a b)"), reads=[H0], is_out=True)
                else:
                    fw.tt(dve, yall[:, b % 2], yt1[:], yt3[:], ALU.mult, [yt1, yt3], [yall])
                    if b % 2 == 1:
                        m = b // 2
                        fw.ts(pool, yt2[:], yall[:, 0], sel[:, 0:1], None, ALU.mult, ALU.bypass, [yall, sel], [yt2])
                        fw.stt(dve, ysown[:, m], yall[:, 1], sel[:, 1:2], yt2[:], ALU.mult, ALU.add, [yall, sel, yt2], [ysown])
                fw.ckpt(2 + b, stop_after)
                if b == 31:
                    fw.dma(sp, o_hre, Hc[:, 0, :], reads=[Hc], is_out=True)
                    fw.dma(sp, o_him, Hc[:, 1, :], reads=[Hc], is_out=True)


        fw.barrier()
        fw.ckpt(40, stop_after)
        subB = fw.sb("subB", [128, 128], F32, sm)
        fw.ts(dve, subB[:], sublnB[:], 1.0 - LAM_INIT, None, ALU.mult, ALU.bypass, [sublnB], [subB])
        rt = [fw.sb("rt%d" % i, [128, 8, 8], F32, sm) for i in range(4)]

        def rope(Xf, cs, sn):
            v = Xf[:].rearrange("p (a d) -> p a d", a=8)
            x1 = v[:, :, 0:8]; x2 = v[:, :, 8:16]
            cb = cs.unsqueeze(1).to_broadcast([128, 8, 8]); sb_ = sn.unsqueeze(1).to_broadcast([128, 8, 8])
            fw.tt(dve, rt[0][:], x1, cb, ALU.mult, [Xf], [rt[0]])
            fw.tt(dve, rt[1][:], x2, sb_, ALU.mult, [Xf], [rt[1]])
            fw.tt(dve, rt[2][:], x2, cb, ALU.mult, [Xf], [rt[2]])
            fw.tt(dve, rt[3][:], x1, sb_, ALU.mult, [Xf], [rt[3]])
            fw.tt(dve, x1, rt[0][:], rt[1][:], ALU.subtract, [rt[0], rt[1]], [Xf])
            fw.tt(dve, x2, rt[2][:], rt[3][:], ALU.add, [rt[2], rt[3]], [Xf])

        for hg in range(2):
            with contextlib.ExitStack() as s2:
                Wq = fw.sb("Wq", [128, 8, 512], BF16, s2); Wk = fw.sb("Wk", [128, 8, 512], BF16, s2); Wv = fw.sb("Wv", [128, 8, 512], BF16, s2)
                load_w(Wk, w_in, 8, 1536 + hg * 512, 1536 + hg * 512 + 512)
                load_w(Wv, w_in, 8, 2560 + hg * 512, 2560 + hg * 512 + 512)
                load_w(Wq, w_in, 8, 512 + hg * 512, 512 + hg * 512 + 512)
                KT = fw.sb("KT", [128, 4, 4096], BF16, s2)
                V = fw.sb("V", [128, 32, 4, 130], BF16, s2)
                fw.mset(pool, V[:], 1.0, [V])
                Kf_r = Ring([fw.sb("Kf%d" % i, [128, 512], F32, s2) for i in range(2)])
                Vf_r = Ring([fw.sb("Vf%d" % i, [128, 512], F32, s2) for i in range(2)])
                Qf_r = Ring([fw.sb("Qf%d" % i, [128, 512], F32, s2) for i in range(2)])
                Kb_r = Ring([fw.sb("Kb%d" % i, [128, 512], BF16, s2) for i in range(2)])
                QT_r = Ring([fw.sb("QT%d" % i, [128, 4, 128], BF16, s2) for i in range(2)])
                PT_r = Ring([fw.sb("PT%d" % i, [128, 4, 128], BF16, s2) for i in range(3)])
                o1 = fw.sb("o1", [128, 128], F32, s2); oh = fw.sb("oh", [128, 128], F32, s2)
                ob_r = Ring([fw.sb("ob%d" % i, [128, 512], BF16, s2) for i in range(2)])
                cnt = 0
                for b in range(33):
                    samp = (b == 32)
                    xt = xring.get()
                    fw.dma(sp, xt[:], xa[b * 128:(b + 1) * 128, :], writes=[xt])
                    hT = norm_h(xt, gB["pre"])
                    for kt in range(8):
                        fw.mm(PS[1][:], hT[:, kt, :], Wk[:, kt, :], kt == 0, kt == 7, [hT, Wk], [PS[1]])
                    for kt in range(8):
                        fw.mm(PS[2][:], hT[:, kt, :], Wv[:, kt, :], kt == 0, kt == 7, [hT, Wv], [PS[2]])
                    Kf = Kf_r.get(); Vf = Vf_r.get()
                    fw.cp(act, Kf[:], PS[1][:], [PS[1]], [Kf])
                    rope(Kf, cosA[:, b, :], sinA[:, b, :])
                    fw.cp(act, Vf[:], PS[2][:], [PS[2]], [Vf])
                    fw.dma(sp, o_k[b * 128:(b + 1) * 128, hg * 512:(hg + 1) * 512], Kf[:], reads=[Kf], is_out=True)
                    fw.dma(sp, o_v[b * 128:(b + 1) * 128, hg * 512:(hg + 1) * 512], Vf[:], reads=[Vf], is_out=True)
                    if samp:
                        fw.cp(pool, ks_t[:, hg * 512:(hg + 1) * 512], Kf[:], [Kf], [ks_t])
                        fw.cp(pool, vs_t[:, hg * 512:(hg + 1) * 512], Vf[:], [Vf], [vs_t])
                    else:
                        Kb = Kb_r.get()
                        fw.cp(pool, Kb[:], Kf[:], [Kf], [Kb])
                        pb = psbf(PS[0])
                        for hl in range(4):
                            fw.tr(pb[:, hl * 128:(hl + 1) * 128], Kb[:, hl * 128:(hl + 1) * 128], identb[:], [Kb, identb], [PS[0]])
                        fw.cp(act, KT[:, :, b * 128:(b + 1) * 128], pb[:, 0:512].rearrange("p (a b) -> p a b", a=4), [PS[0]], [KT])
                        fw.cp(pool, V[:, b, :, 0:128], Vf[:].rearrange("p (a b) -> p a b", a=4), [Vf], [V])
                    if not (samp or b % 2 == 1):
                        continue
                    m = 16 if samp else b // 2
                    if samp:
                        hTo = hT
                    else:
                        xo_t = xring.get()
                        fw.dma(sp, xo_t[:], xo[m * 128:(m + 1) * 128, :], writes=[xo_t])
                        hTo = norm_h(xo_t, gB["pre"])
                    for kt in range(8):
                        fw.mm(PS[3][:], hTo[:, kt, :], Wq[:, kt, :], kt == 0, kt == 7, [hTo, Wq], [PS[3]])
                    Qf = Qf_r.get()
                    fw.cp(act, Qf[:], PS[3][:], [PS[3]], [Qf])
                    rope(Qf, cosO[:, m, :], sinO[:, m, :])
                    if samp:
                        fw.cp(pool, qs_t[:, hg * 512:(hg + 1) * 512], Qf[:], [Qf], [qs_t])
                        continue
                    Qb = Kb_r.get()
                    fw.cp(pool, Qb[:], Qf[:], [Qf], [Qb])
                    pb = psbf(PS[0])
                    for hl in range(4):
                        fw.tr(pb[:, hl * 128:(hl + 1) * 128], Qb[:, hl * 128:(hl + 1) * 128], identb[:], [Qb, identb], [PS[0]])
                    QT = QT_r.get()
                    fw.cp(act, QT[:].rearrange("p a b -> p (a b)"), pb[:, 0:512], [PS[0]], [QT])
                    nkb = b + 1
                    ob = ob_r.get()
                    for hl in range(4):
                        for c in range(2):
                            acc = PS[6 + c]
                            for kb0 in range(0, nkb, 4):
                                n = min(4, nkb - kb0)
                                sbank = PS[4 + cnt % 2]; cnt += 1
                                for i in range(n):
                                    kb = kb0 + i
                                    fw.mm(sbank[:, i * 128:(i + 1) * 128], KT[c * 64:(c + 1) * 64, hl, kb * 128:(kb + 1) * 128],
                                          QT[c * 64:(c + 1) * 64, hl, :], True, True, [KT, QT], [sbank])
                                PT = PT_r.get()
                                fw.actv(PT[:].rearrange("p a b -> p (a b)")[:, 0:n * 128], sbank[:, 0:n * 128], AF.Exp, [sbank], [PT], scale=0.125)
                                for i in range(n):
                                    kb = kb0 + i
                                    if kb == nkb - 2:
                                        fw.tt(pool, PT[:, i, :], PT[:, i, :], maskA[:], ALU.mult, [PT, maskA], [PT])
                                    if kb == nkb - 1:
                                        fw.tt(pool, PT[:, i, :], PT[:, i, :], maskB[:], ALU.mult, [PT, maskB], [PT])
                                for i in range(n):
                                    kb = kb0 + i
                                    fw.mm(acc[:, 0:129], PT[:, i, :], V[:, kb, hl, 0:129], kb == 0, kb == nkb - 1, [PT, V], [acc])
                        rs = sring.get()
                        fw.recip(rs[:, 0:1], PS[6][:, 128:129], [PS[6]], [rs])
                        fw.recip(rs[:, 1:2], PS[7][:, 128:129], [PS[7]], [rs])
                        fw.tt(dve, rs[:, 2:3], rs[:, 1:2], lam[:], ALU.mult, [rs, lam], [rs])
                        fw.ts(dve, o1[:], PS[7][:, 0:128], rs[:, 2:3], None, ALU.mult, ALU.bypass, [PS[7], rs], [o1])
                        fw.stt(dve, oh[:], PS[6][:, 0:128], rs[:, 0:1], o1[:], ALU.mult, ALU.subtract, [PS[6], rs, o1], [oh])
                        st2 = sring.get()
                        fw.actv(junk[:, 0:128], oh[:], AF.Square, [oh], [junk, st2], accum=st2[:, 0:1])
                        r = rstd_from(st2[:, 0:1], st2, 128)
                        fw.stt(dve, ob[:, hl * 128:(hl + 1) * 128], oh[:], r, subB[:], ALU.mult, ALU.mult, [oh, st2, subB], [ob])
                    fw.dma(sp, oscr[m * 128:(m + 1) * 128, hg * 512:(hg + 1) * 512], ob[:], reads=[ob])
            fw.barrier()
        fw.ckpt(50, stop_after)

        with contextlib.ExitStack() as s5:
            pt_i = fw.sb("pt_i", [128, 256], I32, s5); pt_f = fw.sb("pt_f", [128, 256], F32, s5); idx = fw.sb("idx", [128, 256], I32, s5)
            iot = fw.sb("iot", [128, 1], F32, s5)
            fw.dma(sp, pt_i[:], ptab.partition_broadcast(128), writes=[pt_i])
            fw.dma(sp, iot[:], c_iota, writes=[iot])
            fw.cp(dve, pt_f[:], pt_i[:], [pt_i], [pt_f])
            fw.ts(dve, pt_f[:], pt_f[:], 128.0, iot[:, 0:1], ALU.mult, ALU.add, [pt_f, iot], [pt_f])
            fw.cp(dve, idx[:], pt_f[:], [pt_f], [idx])
            selb_t = fw.sb("selb_t", [16, 16, 128], F32, s5)
            fw.dma(sp, selb_t[:].rearrange("p a b -> p (a b)"), c_selb, writes=[selb_t])
            Kp_r = Ring([fw.sb("Kp%d" % i, [128, 1024], F32, s5) for i in range(2)])
            Vp_r = Ring([fw.sb("Vp%d" % i, [128, 1024], F32, s5) for i in range(2)])
            Vb_r = Ring([fw.sb("Vb%d" % i, [128, 8, 130], BF16, s5) for i in range(2)])
            for t in Vb_r.tiles:
                fw.mset(pool, t[:], 1.0, [t])
            prod = fw.sb("prod", [128, 1024], F32, s5); sc = fw.sb("sc", [128, 16], F32, s5)
            Pz_r = Ring([fw.sb("Pz%d" % i, [128, 16, 16], BF16, s5) for i in range(2)])
            qB = fw.sb("qB", [128, 1024], F32, s5)
            zer = fw.sb("zer", [128, 16], BF16, s5)
            fw.mset(dve, zer[:], 0.0, [zer])
            fw.mset(dve, junk[:], 0.0, [junk])
            for k in range(6):
                fw.mm(PS[k][0:16, :], zer[:, 0:16], junk[:, 0:512], True, False, [zer, junk], [PS[k]])
            for bs in range(16):
                for hf in range(2):
                    fw.mm(PS[6 + hf][:], selb_t[:, bs, :], qs_t[0:16, hf * 512:(hf + 1) * 512], True, True, [selb_t, qs_t], [PS[6 + hf]])
                    fw.cp(act, qB[:, hf * 512:(hf + 1) * 512], PS[6 + hf][:], [PS[6 + hf]], [qB])
                for pg in range(16):
                    col = bs * 16 + pg
                    Kp = Kp_r.get(); Vp = Vp_r.get()
                    fw.dma(pool, Kp[:], cache_k, reads=[idx], writes=[Kp], indirect=idx[:, col:col + 1])
                    fw.dma(pool, Vp[:], cache_v, reads=[idx], writes=[Vp], indirect=idx[:, col:col + 1])
                    fw.tt(dve, prod[:], Kp[:], qB[:], ALU.mult, [Kp, qB], [prod])
                    fw.red(dve, sc[:], prod[:].rearrange("p (a d) -> p a d", a=16), [prod], [sc])
                    Pz = Pz_r.get()
                    fw.mset(pool, Pz[:], 0.0, [Pz])
                    fw.actv(Pz[:, :, bs], sc[:], AF.Exp, [sc], [Pz], scale=0.125)
                    Vb = Vb_r.get()
                    fw.cp(pool, Vb[:, :, 0:128], Vp[:].rearrange("p (a b) -> p a b", a=8), [Vp], [Vb])
                    for hc in range(16):
                        bk = PS[hc // 3]; off = (hc % 3) * 129
                        fw.mm(bk[0:16, off:off + 129], Pz[:, hc, :], Vb[:, hc // 2, 0:129], False, False, [Pz, Vb], [bk])
            psf = fw.sb("psf", [16, 16], F32, s5); Ot = fw.sb("Ot", [16, 16, 128], F32, s5); St = fw.sb("St", [16, 16], F32, s5)
            obs = fw.sb("obs", [128, 1024], BF16, s5)
            fw.mset(pool, obs[:], 0.0, [obs])
            fw.tt(dve, prod[0:16, :], qs_t[0:16, :], ks_t[0:16, :], ALU.mult, [qs_t, ks_t], [prod])
            fw.red(dve, sc[0:16, :], prod[0:16, :].rearrange("p (a d) -> p a d", a=16), [prod], [sc])
            fw.actv(psf[:], sc[0:16, :], AF.Exp, [sc], [psf], scale=0.125)
            for hc in range(16):
                bk = PS[hc // 3]; off = (hc % 3) * 129; h_ = hc // 2
                fw.stt(dve, Ot[:, hc, :], vs_t[0:16, h_ * 128:(h_ + 1) * 128], psf[:, hc:hc + 1], bk[0:16, off:off + 128], ALU.mult, ALU.add, [vs_t, psf, bk], [Ot])
                fw.tt(dve, St[:, hc:hc + 1], bk[0:16, off + 128:off + 129], psf[:, hc:hc + 1], ALU.add, [bk, psf], [St])
            fw.recip(St[:], St[:], [St], [St])
            l1 = fw.sb("l1", [16, 4], F32, s5); o1s = fw.sb("o1s", [16, 128], F32, s5); ohs = fw.sb("ohs", [16, 128], F32, s5)
            for h_ in range(8):
                fw.tt(dve, l1[:, 0:1], St[:, 2 * h_ + 1:2 * h_ + 2], lam[0:16, :], ALU.mult, [St, lam], [l1])
                fw.ts(dve, o1s[:], Ot[:, 2 * h_ + 1, :], l1[:, 0:1], None, ALU.mult, ALU.bypass, [Ot, l1], [o1s])
                fw.stt(dve, ohs[:], Ot[:, 2 * h_, :], St[:, 2 * h_:2 * h_ + 1], o1s[:], ALU.mult, ALU.subtract, [Ot, St, o1s], [ohs])
                fw.actv(junk[0:16, 0:128], ohs[:], AF.Square, [ohs], [junk, l1], accum=l1[:, 1:2])
                fw.ts(dve, l1[:, 2:3], l1[:, 1:2], 1.0 / 128, EPS, ALU.mult, ALU.add, [l1], [l1])
                fw.actv(l1[:, 3:4], l1[:, 2:3], AF.Sqrt, [l1], [l1])
                fw.recip(l1[:, 3:4], l1[:, 3:4], [l1], [l1])
                fw.stt(dve, obs[0:16, h_ * 128:(h_ + 1) * 128], ohs[:], l1[:, 3:4], subB[0:16, :], ALU.mult, ALU.mult, [ohs, l1, subB], [obs])
            fw.dma(sp, oscr[16 * 128:17 * 128, :], obs[:], reads=[obs])
        fw.barrier()
        fw.ckpt(60, stop_after)

        with contextlib.ExitStack() as s3:
            Wgs = fw.sb("Wgs", [128, 8, 1024], BF16, s3); Wga_ = fw.sb("Wga", [128, 8, 1024], BF16, s3)
            Wa = fw.sb("Wa", [128, 4, 1024], BF16, s3); Wb = fw.sb("Wb", [128, 4, 1024], BF16, s3); Wo = fw.sb("Wo", [128, 8, 1024], BF16, s3)
            load_w(Wgs, w_in, 8, 3584, 4608); load_w(Wga_, w_in, 8, 4608, 5632)
            load_w(Wa, wga, 4, 0, 1024); load_w(Wb, wgb, 4, 0, 1024); load_w(Wo, w_o, 8, 0, 1024)
            sgs = fw.sb("sgs", [128, 1024], F32, s3); sga = fw.sb("sga", [128, 1024], F32, s3)
            sgb = fw.sb("sgb", [128, 1024], F32, s3); t1 = fw.sb("t1", [128, 1024], F32, s3); t2 = fw.sb("t2", [128, 1024], F32, s3)
            obl = fw.sb("obl", [128, 1024], BF16, s3); mixb = fw.sb("mixb", [128, 1024], BF16, s3)
            mixT = fw.sb("mixT", [128, 8, 128], BF16, s3)
            for m in range(17):
                xt = xring.get()
                fw.dma(sp, xt[:], xo[m * 128:(m + 1) * 128, :], writes=[xt])
                fw.dma(sp, obl[:], oscr[m * 128:(m + 1) * 128, :], writes=[obl])
                hT = norm_h(xt, gB["pre"])
                for hf in range(2):
                    for kt in range(8):
                        fw.mm(PS[1 + hf][:], hT[:, kt, :], Wgs[:, kt, hf * 512:(hf + 1) * 512], kt == 0, kt == 7, [hT, Wgs], [PS[1 + hf]])
                    for kt in range(8):
                        fw.mm(PS[3 + hf][:], hT[:, kt, :], Wga_[:, kt, hf * 512:(hf + 1) * 512], kt == 0, kt == 7, [hT, Wga_], [PS[3 + hf]])
                for hf in range(2):
                    fw.actv(sgs[:, hf * 512:(hf + 1) * 512], PS[1 + hf][:], AF.Sigmoid, [PS[1 + hf]], [sgs])
                    fw.actv(sga[:, hf * 512:(hf + 1) * 512], PS[3 + hf][:], AF.Sigmoid, [PS[3 + hf]], [sga])
                for hf in range(2):
                    for ct in range(4):
                        fw.mm(PS[1 + hf][:], ysown[:, m, ct, :], Wa[:, ct, hf * 512:(hf + 1) * 512], ct == 0, ct == 3, [ysown, Wa], [PS[1 + hf]])
                    for ct in range(4):
                        fw.mm(PS[3 + hf][:], ysown[:, m, ct, :], Wb[:, ct, hf * 512:(hf + 1) * 512], ct == 0, ct == 3, [ysown, Wb], [PS[3 + hf]])
                for hf in range(2):
                    sl = slice(hf * 512, (hf + 1) * 512)
                    fw.actv(sgb[:, sl], PS[3 + hf][:], AF.Sigmoid, [PS[3 + hf]], [sgb])
                    fw.tt(dve, t1[:, sl], PS[1 + hf][:], sgb[:, sl], ALU.mult, [PS[1 + hf], sgb], [t1])
                fw.tt(pool, t1[:], t1[:], sgs[:], ALU.mult, [t1, sgs], [t1])
                fw.tt(pool, t2[:], sga[:], obl[:], ALU.mult, [sga, obl], [t2])
                fw.tt(pool, mixb[:], t1[:], t2[:], ALU.add, [t1, t2], [mixb])
                pb = psbf(PS[0])
                for kt in range(8):
                    fw.tr(pb[:, kt * 128:(kt + 1) * 128], mixb[:, kt * 128:(kt + 1) * 128], identb[:], [mixb, identb], [PS[0]])
                fw.cp(act, mixT[:].rearrange("p a b -> p (a b)"), pb, [PS[0]], [mixT])
                for hf in range(2):
                    for kt in range(8):
                        fw.mm(PS[5 + hf][:], mixT[:, kt, :], Wo[:, kt, hf * 512:(hf + 1) * 512], kt == 0, kt == 7, [mixT, Wo], [PS[5 + hf]])
                    fw.cp(dve, t1[:, hf * 512:(hf + 1) * 512], PS[5 + hf][:], [PS[5 + hf]], [t1])
                stt_ = sring.get()
                fw.actv(junk[:], t1[:], AF.Square, [t1], [junk, stt_], accum=stt_[:, 0:1])
                r = rstd_from(stt_[:, 0:1], stt_, D)
                fw.stt(dve, t2[:], t1[:], r, gB["post"][:], ALU.mult, ALU.mult, [t1, stt_, gB["post"]], [t2])
                fw.tt(pool, t2[:], t2[:], xt[:], ALU.add, [t2, xt], [t2])
                fw.dma(sp, x1scr[m * 128:(m + 1) * 128, :], t2[:], reads=[t2])
        fw.barrier()
        fw.ckpt(70, stop_after)
        if dbg:
            o_dbg = dout('o_dbg', [128, 17 * 512])
            fw.dma(pool, o_dbg, ysown[:].rearrange('p a b c -> p (a b c)'), reads=[ysown], is_out=True)
        sm.close()
        with contextlib.ExitStack() as s4:
            Wg = fw.sb("Wg", [128, 8, 2816], BF16, s4); Wup = fw.sb("Wup", [128, 8, 2816], BF16, s4); Wd = fw.sb("Wd", [128, 22, 1024], BF16, s4)
            load_w(Wg, w_gate, 8, 0, 2816); load_w(Wup, w_up, 8, 0, 2816); load_w(Wd, w_down, 22, 0, 1024)
            actT = fw.sb("actT", [128, 22, 128], BF16, s4)
            sg_r = Ring([fw.sb("sg%d" % i, [128, 512], F32, s4) for i in range(2)])
            tg_r = Ring([fw.sb("tg%d" % i, [128, 512], F32, s4) for i in range(2)])
            fsb = fw.sb("fsb", [128, 1024], F32, s4); ysb = fw.sb("ysb", [128, 1024], F32, s4)
            cnt = 0
            for m in range(17):
                xt = xring.get()
                fw.dma(sp, xt[:], x1scr[m * 128:(m + 1) * 128, :], writes=[xt])
                hT = norm_h(xt, gB["pf"])
                for ft0 in range(0, 22, 4):
                    n = min(4, 22 - ft0)
                    bg = PS[1 + 2 * (cnt % 2)]; bu = PS[2 + 2 * (cnt % 2)]; cnt += 1
                    for i in range(n):
                        ft = ft0 + i
                        for kt in range(8):
                            fw.mm(bg[:, i * 128:(i + 1) * 128], Wg[:, kt, ft * 128:(ft + 1) * 128], hT[:, kt, :], kt == 0, kt == 7, [Wg, hT], [bg])
                        for kt in range(8):
                            fw.mm(bu[:, i * 128:(i + 1) * 128], Wup[:, kt, ft * 128:(ft + 1) * 128], hT[:, kt, :], kt == 0, kt == 7, [Wup, hT], [bu])
                    sg = sg_r.get(); tg = tg_r.get()
                    fw.actv(sg[:, 0:n * 128], bg[:, 0:n * 128], AF.Sigmoid, [bg], [sg])
                    fw.tt(dve, tg[:, 0:n * 128], bg[:, 0:n * 128], sg[:, 0:n * 128], ALU.mult, [bg, sg], [tg])
                    fw.tt(dve, actT[:, ft0:ft0 + n, :].rearrange("p a b -> p (a b)"), bu[:, 0:n * 128], tg[:, 0:n * 128], ALU.mult, [bu, tg], [actT])
                for hf in range(2):
                    for ft in range(22):
                        fw.mm(PS[5 + hf][:], actT[:, ft, :], Wd[:, ft, hf * 512:(hf + 1) * 512], ft == 0, ft == 21, [actT, Wd], [PS[5 + hf]])
                    fw.cp(act, fsb[:, hf * 512:(hf + 1) * 512], PS[5 + hf][:], [PS[5 + hf]], [fsb])
                stt_ = sring.get()
                fw.actv(junk[:], fsb[:], AF.Square, [fsb], [junk, stt_], accum=stt_[:, 0:1])
                r = rstd_from(stt_[:, 0:1], stt_, D)
                fw.stt(dve, ysb[:], fsb[:], r, gB["postf"][:], ALU.mult, ALU.mult, [fsb, stt_, gB["postf"]], [ysb])
                fw.tt(pool, ysb[:], ysb[:], xt[:], ALU.add, [ysb, xt], [ysb])
                fw.dma(sp, o_y[m * 128:(m + 1) * 128, :], ysb[:], reads=[ysb], is_out=True)
        fw.finish()
    return nc


def _consts(j):
    c = {}
    c["c_ident"] = np.eye(128, dtype=np.float32)
    s = np.arange(128)
    c["c_tri"] = (s[:, None] <= s[None, :]).astype(np.float32)
    tri = (s[:, None] <= s[None, :]).astype(np.float32)
    if j == 0:
        c["c_maskA"] = tri; c["c_maskB"] = np.zeros((128, 128), np.float32)
    else:
        c["c_maskA"] = np.ones((128, 128), np.float32); c["c_maskB"] = tri
    mz = np.zeros((128, 4, 128), np.float32)
    for g2 in range(2):
        for gpl in range(4):
            gl = 2 * gpl + g2
            mz[g2 * 64:(g2 + 1) * 64, gpl, gl * 16:(gl + 1) * 16] = 1.0
    c["c_maskZ"] = mz.reshape(128, 512)
    c["c_maskC"] = np.ascontiguousarray(mz.transpose(2, 1, 0)).reshape(128, 512)
    sel = np.zeros((128, 2), np.float32); sel[:, j] = 1.0
    c["c_sel"] = sel
    half = 8
    inv = (500000.0 ** (-np.arange(half, dtype=np.float32) * 2.0 / 16)).astype(np.float32)
    posA = np.concatenate([np.arange(4096), np.full(128, 2048)]).astype(np.float32).reshape(33, 128)
    angA = posA[:, :, None] * inv[None, None, :]
    c["c_cosA"] = np.ascontiguousarray(np.cos(angA).transpose(1, 0, 2)).reshape(128, 33 * 8).astype(np.float32)
    c["c_sinA"] = np.ascontiguousarray(np.sin(angA).transpose(1, 0, 2)).reshape(128, 33 * 8).astype(np.float32)
    blocks = [2 * m + j for m in range(16)] + [32]
    c["c_cosO"] = np.ascontiguousarray(np.cos(angA[blocks]).transpose(1, 0, 2)).reshape(128, 17 * 8).astype(np.float32)
    c["c_sinO"] = np.ascontiguousarray(np.sin(angA[blocks]).transpose(1, 0, 2)).reshape(128, 17 * 8).astype(np.float32)
    sb = np.zeros((16, 16, 128), np.float32)
    for b in range(16):
        sb[b, b, :] = 1.0
    c["c_selb"] = sb.reshape(16, 2048)
    c["c_iota"] = np.arange(128, dtype=np.float32).reshape(128, 1)
    return c


def make_in_maps(inp):
    f = lambda a: np.ascontiguousarray(np.asarray(a))
    maps = []
    ck = f(inp["cache_k"]).reshape(-1, 1024)
    cv = f(inp["cache_v"]).reshape(-1, 1024)
    xp = f(inp["x_prompt"]); xs = f(inp["x_sample"]).reshape(128, 1024)
    shared = {
        "w_in": f(inp["w_in"])[0], "w_glu_a": f(inp["w_glu_a"])[0], "w_glu_b": f(inp["w_glu_b"])[0],
        "w_o": f(inp["w_o"])[0], "w_gate": f(inp["w_gate"])[0], "w_up": f(inp["w_up"])[0], "w_down": f(inp["w_down"])[0],
        "n_pre": f(inp["norm_pre_mix"]), "n_post": f(inp["norm_post_mix"]), "n_pf": f(inp["norm_pre_ffn"]),
        "n_postf": f(inp["norm_post_ffn"]), "subln": f(inp["subln_gain"]),
        "lamq": np.concatenate([f(inp["lambda_q1"]), f(inp["lambda_k1"]), f(inp["lambda_q2"]), f(inp["lambda_k2"])], axis=1),
        "lamre": f(inp["ssm_lambda_re"]).reshape(16, 128), "lamim": f(inp["ssm_lambda_im"]).reshape(16, 128),
        "logdt": f(inp["ssm_log_dt"]).reshape(16, 2),
        "bre": f(inp["ssm_b_re"]).reshape(16, 128, 16), "bim": f(inp["ssm_b_im"]).reshape(16, 128, 16),
        "cre": f(inp["ssm_c_re"]).reshape(4, 128, 64), "cim": f(inp["ssm_c_im"]).reshape(4, 128, 64),
        "dsk": np.ascontiguousarray(f(inp["ssm_d"]).reshape(4, 128).T),
        "cache_k": ck, "cache_v": cv,
    }
    for core in range(8):
        s, j = core // 2, core % 2
        xsb = np.zeros((128, 1024), np.float32)
        xsb[:16] = xs[core * 16:(core + 1) * 16]
        xa = np.concatenate([xp[s], xsb], axis=0)
        own = xp[s].reshape(32, 128, 1024)[j::2].reshape(2048, 1024)
        xo = np.concatenate([own, xsb], axis=0)
        m = dict(shared)
        m.update(_consts(j))
        m["xa"] = xa; m["xo"] = xo
        m["ptab"] = f(inp["page_table"])[core * 16:(core + 1) * 16].reshape(1, 256).astype(np.int32)
        m["s0re"] = f(inp["state_ssm_re"])[0, core * 16:(core + 1) * 16].reshape(16, 2048)
        m["s0im"] = f(inp["state_ssm_im"])[0, core * 16:(core + 1) * 16].reshape(16, 2048)
        maps.append(m)
    return maps


_NC = None


def kernel(**inp):
    global _NC
    if _NC is None:
        _NC = build_nc()
    maps = make_in_maps(inp)
    res = run_bass_kernel_spmd(_NC, maps, core_ids=list(range(8))).results
    return assemble(res)


def assemble(res):
    yp = np.zeros((4, 32, 128, 1024), np.float32)
    ys = np.zeros((128, 1, 1024), np.float32)
    kp = np.zeros((1, 4, 4096, 8, 128), np.float32); vp = np.zeros((1, 4, 4096, 8, 128), np.float32)
    hre = np.zeros((1, 4, 32, 64), np.float32); him = np.zeros((1, 4, 32, 64), np.float32)
    ksm = np.zeros((1, 128, 1, 8, 128), np.float32); vsm = np.zeros((1, 128, 1, 8, 128), np.float32)
    sre = np.zeros((1, 128, 32, 64), np.float32); sim = np.zeros((1, 128, 32, 64), np.float32)
    for core in range(8):
        r = res[core]
        s, j = core // 2, core % 2
        yp[s, j::2] = r["o_y"][:2048].reshape(16, 128, 1024)
        ys[core * 16:(core + 1) * 16, 0] = r["o_y"][2048:2048 + 16]
        if j == 0:
            kp[0, s] = r["o_k"][:4096].reshape(4096, 8, 128)
            vp[0, s] = r["o_v"][:4096].reshape(4096, 8, 128)
            hre[0, s] = r["o_hre"].reshape(2, 64, 16).transpose(2, 0, 1).reshape(32, 64)
            him[0, s] = r["o_him"].reshape(2, 64, 16).transpose(2, 0, 1).reshape(32, 64)
        ksm[0, core * 16:(core + 1) * 16, 0] = r["o_k"][4096:4096 + 16].reshape(16, 8, 128)
        vsm[0, core * 16:(core + 1) * 16, 0] = r["o_v"][4096:4096 + 16].reshape(16, 8, 128)
        sre[0, core * 16:(core + 1) * 16] = r["o_sre"][:16].reshape(16, 32, 64)
        sim[0, core * 16:(core + 1) * 16] = r["o_sim"][:16].reshape(16, 32, 64)
    return (yp.reshape(4, 4096, 1024), ys, kp, vp, hre, him, ksm, vsm, sre, sim)
```

```python
import contextlib
import math
import numpy as np
import concourse.bass as bass
import concourse.mybir as mybir
from concourse.bass_utils import run_bass_kernel_spmd

F32 = mybir.dt.float32
BF16 = mybir.dt.bfloat16
I32 = mybir.dt.int32
AF = mybir.ActivationFunctionType
ALU = mybir.AluOpType
AX = mybir.AxisListType

D = 1024
NB = 32
NOWN = 16
EPS = 1e-6
LAM_INIT = 0.2


class T:
    __slots__ = ("ap", "w", "r", "name", "psum")

    def __init__(self, ap, name="", psum=False):
        self.psum = psum
        self.ap = ap
        self.w = None
        self.r = {}
        self.name = name

    def __getitem__(self, idx):
        return self.ap[idx]


class Eng:
    def __init__(self, fw, name, raw, selfsync=True):
        self.name = name
        self.raw = raw
        self.sem = fw.new_sem("e_" + name)
        self.n = 0
        self.seen = {}
        self.ops = []
        self.pending = []
        self.selfsync = selfsync


class FW:
    NDMA = 40

    def __init__(self, nc, stack):
        self.nc = nc
        self.stack = stack
        self.sems = {}
        self.pe = Eng(self, "pe", nc.tensor, selfsync=False)
        self.act = Eng(self, "act", nc.scalar)
        self.dve = Eng(self, "dve", nc.vector)
        self.pool = Eng(self, "pool", nc.gpsimd)
        self.sp = Eng(self, "sp", nc.sync)
        self.dma_sems = [self.new_sem("d%d" % i) for i in range(self.NDMA)]
        self.dma_cnt = [0] * self.NDMA
        self.dma_rr = 0
        self.out_events = []

    def new_sem(self, name):
        s = self.stack.enter_context(self.nc.semaphore(name))
        self.sems[id(s)] = s
        return s

    def sb(self, name, shape, dt, stack=None):
        st = stack or self.stack
        self.uid = getattr(self, 'uid', 0) + 1
        name = '%s_%d' % (name, self.uid)
        return T(st.enter_context(self.nc.sbuf_tensor(name, list(shape), dt)), name)

    def ps(self, name, shape, dt=F32):
        return T(self.stack.enter_context(self.nc.psum_tensor(name, list(shape), dt)), name, psum=True)

    def _deps(self, eng, reads, writes):
        need = {}
        for t in reads:
            if t.w is not None and need.get(t.w[0], 0) < t.w[1]:
                need[t.w[0]] = t.w[1]
            if t.psum:
                for s, v in t.r.items():
                    if s != id(eng.sem) and need.get(s, 0) < v:
                        need[s] = v
        for t in writes:
            if t.w is not None and need.get(t.w[0], 0) < t.w[1]:
                need[t.w[0]] = t.w[1]
            for s, v in t.r.items():
                if need.get(s, 0) < v:
                    need[s] = v
        waits = []
        for s, v in need.items():
            if s == id(eng.sem) and not eng.selfsync:
                continue
            if eng.seen.get(s, 0) >= v:
                continue
            eng.seen[s] = v
            waits.append((self.sems[s], v))
        if eng.pending:
            waits = eng.pending + waits
            eng.pending = []
        return waits

    def _mark(self, ev, reads, writes):
        for t in reads:
            if t.r.get(ev[0], 0) < ev[1]:
                t.r[ev[0]] = ev[1]
        for t in writes:
            t.w = ev
            t.r = {}

    stopped = False

    def barrier(self):
        engs = [self.pe, self.act, self.dve, self.pool, self.sp]
        for e in engs:
            for o in engs:
                if o is not e and o.n > 0 and e.seen.get(id(o.sem), 0) < o.n:
                    e.seen[id(o.sem)] = o.n
                    e.pending.append((o.sem, o.n))
            for k, sem in enumerate(self.dma_sems):
                v = self.dma_cnt[k]
                if v and e.seen.get(id(sem), 0) < v:
                    e.seen[id(sem)] = v
                    e.pending.append((sem, v))

    def ckpt(self, k, stop_after):
        if stop_after <= k:
            self.stopped = True

    def op(self, eng, fn, reads=(), writes=()):
        if self.stopped:
            return None
        waits = self._deps(eng, reads, writes)
        eng.n += 1
        ev = (id(eng.sem), eng.n)
        eng.ops.append((waits, fn, (eng.sem, 1)))
        self._mark(ev, reads, writes)
        return ev

    def dma(self, eng, out, in_, reads=(), writes=(), is_out=False, indirect=None):
        if self.stopped:
            return None
        k = self.dma_rr
        self.dma_rr = (k + 1) % self.NDMA
        sem = self.dma_sems[k]
        waits = self._deps(eng, reads, writes)
        prev = self.dma_cnt[k]
        if prev and eng.seen.get(id(sem), 0) < prev:
            eng.seen[id(sem)] = prev
            waits.append((sem, prev))
        self.dma_cnt[k] = prev + 16
        ev = (id(sem), prev + 16)
        if indirect is None:
            fn = lambda e: e.dma_start(out=out, in_=in_)
        else:
            fn = lambda e: e.indirect_dma_start(out=out, out_offset=None, in_=in_,
                                                in_offset=bass.IndirectOffsetOnAxis(ap=indirect, axis=0))
        eng.ops.append((waits, fn, (sem, 16)))
        self._mark(ev, reads, writes)
        if is_out:
            self.out_events.append(ev)
        return ev

    def mm(self, out, lhsT, rhs, start, stop, reads, writes):
        return self.op(self.pe, lambda e: e.matmul(out, lhsT=lhsT, rhs=rhs, start=start, stop=stop,
                                                   skip_group_check=True), reads, writes)

    def tr(self, out, in_, ident, reads, writes):
        return self.op(self.pe, lambda e: e.transpose(out, in_, ident), reads, writes)

    def actv(self, out, in_, func, reads, writes, scale=1.0, bias=None, accum=None, eng=None):
        kw = {}
        if bias is not None:
            kw["bias"] = bias
        if accum is not None:
            kw["accum_out"] = accum
        return self.op(self.act, lambda e: e.activation(out=out, in_=in_, func=func, scale=scale, **kw),
                       reads, writes)

    def tt(self, eng, out, in0, in1, op, reads, writes):
        return self.op(eng, lambda e: e.tensor_tensor(out=out, in0=in0, in1=in1, op=op), reads, writes)

    def stt(self, eng, out, in0, scalar, in1, op0, op1, reads, writes):
        return self.op(eng, lambda e: e.scalar_tensor_tensor(out=out, in0=in0, scalar=scalar, in1=in1,
                                                             op0=op0, op1=op1), reads, writes)

    def ts(self, eng, out, in0, s1, s2, op0, op1, reads, writes):
        return self.op(eng, lambda e: e.tensor_scalar(out=out, in0=in0, scalar1=s1, scalar2=s2, op0=op0,
                                                      op1=op1), reads, writes)

    def cp(self, eng, out, in_, reads, writes):
        if eng is self.act:
            return self.op(eng, lambda e: e.copy(out=out, in_=in_), reads, writes)
        return self.op(eng, lambda e: e.tensor_copy(out=out, in_=in_), reads, writes)

    def red(self, eng, out, in_, reads, writes, op=ALU.add):
        return self.op(eng, lambda e: e.tensor_reduce(out=out, in_=in_, axis=AX.X, op=op), reads, writes)

    def mset(self, eng, out, val, writes):
        return self.op(eng, lambda e: e.memset(out, val), (), writes)

    def recip(self, out, in_, reads, writes):
        return self.op(self.dve, lambda e: e.reciprocal(out=out, in_=in_), reads, writes)

    def finish(self):
        final = {}
        for s, v in self.out_events:
            if final.get(s, 0) < v:
                final[s] = v
        nc = self.nc
        with nc.Block() as block:
            def emit(e, last=False):
                def body(raw):
                    for waits, fn, inc in e.ops:
                        for s, v in waits:
                            raw.wait_ge(s, v)
                        fn(raw).then_inc(inc[0], inc[1])
                    if last:
                        for s, v in final.items():
                            raw.wait_ge(self.sems[s], v)
                return body
            block.tensor(emit(self.pe))
            block.scalar(emit(self.act))
            block.vector(emit(self.dve))
            block.gpsimd(emit(self.pool))
            block.sync(emit(self.sp, True))


class Stop(Exception):
    pass


class Ring:
    def __init__(self, tiles):
        self.tiles = tiles
        self.i = 0

    def get(self):
        t = self.tiles[self.i % len(self.tiles)]
        self.i += 1
        return t


def build_nc(stop_after=99, cache_rows=2560 * 128, dbg=False):
    nc = bass.Bass("TRN2", target_bir_lowering=False)

    def din(name, shape, dt=F32):
        return nc.dram_tensor(name, list(shape), dt, kind="ExternalInput").ap()

    def dout(name, shape, dt=F32):
        return nc.dram_tensor(name, list(shape), dt, kind="ExternalOutput").ap()

    xa = din("xa", [33 * 128, D])
    xo = din("xo", [17 * 128, D])
    w_in = din("w_in", [D, 5632])
    wga = din("w_glu_a", [512, D]); wgb = din("w_glu_b", [512, D])
    w_o = din("w_o", [D, D])
    w_gate = din("w_gate", [D, 2816]); w_up = din("w_up", [D, 2816]); w_down = din("w_down", [2816, D])
    n_pre = din("n_pre", [1, D]); n_post = din("n_post", [1, D]); n_pf = din("n_pf", [1, D]); n_postf = din("n_postf", [1, D])
    subln = din("subln", [1, 128])
    lamq = din("lamq", [1, 256])
    lamre = din("lamre", [16, 128]); lamim = din("lamim", [16, 128]); logdt = din("logdt", [16, 2])
    bre = din("bre", [16, 128, 16]); bim = din("bim", [16, 128, 16])
    cre = din("cre", [4, 128, 64]); cim = din("cim", [4, 128, 64])
    dsk = din("dsk", [128, 4])
    cache_k = din("cache_k", [cache_rows, D]); cache_v = din("cache_v", [cache_rows, D])
    ptab = din("ptab", [1, 256], I32)
    s0re = din("s0re", [16, 2048]); s0im = din("s0im", [16, 2048])
    c_ident = din("c_ident", [128, 128]); c_tri = din("c_tri", [128, 128])
    c_maskA = din("c_maskA", [128, 128]); c_maskB = din("c_maskB", [128, 128])
    c_maskZ = din("c_maskZ", [128, 512]); c_maskC = din("c_maskC", [128, 512])
    c_sel = din("c_sel", [128, 2])
    c_cosA = din("c_cosA", [128, 33 * 8]); c_sinA = din("c_sinA", [128, 33 * 8])
    c_cosO = din("c_cosO", [128, 17 * 8]); c_sinO = din("c_sinO", [128, 17 * 8])
    c_selb = din("c_selb", [16, 16 * 128])
    c_iota = din("c_iota", [128, 1])

    o_y = dout("o_y", [17 * 128, D])
    o_k = dout("o_k", [33 * 128, D]); o_v = dout("o_v", [33 * 128, D])
    o_hre = dout("o_hre", [128, 16]); o_him = dout("o_him", [128, 16])
    o_sre = dout("o_sre", [128, 2048]); o_sim = dout("o_sim", [128, 2048])

    x1scr = nc.dram_tensor("x1scr", [17 * 128, D], F32, kind="Internal").ap()
    oscr = nc.dram_tensor("oscr", [17 * 128, D], BF16, kind="Internal").ap()

    with contextlib.ExitStack() as st:
        fw = FW(nc, st)
        sm = contextlib.ExitStack()
        pe, act, dve, pool, sp = fw.pe, fw.act, fw.dve, fw.pool, fw.sp
        PS = [fw.ps("bank%d" % i, [128, 512]) for i in range(8)]

        def psbf(t):
            return t.ap[:].bitcast(BF16)

        ident = fw.sb("ident", [128, 128], F32)
        identb = fw.sb("identb", [128, 128], BF16)
        trib = fw.sb("trib", [128, 128], BF16)
        maskA = fw.sb("maskA", [128, 128], BF16); maskB = fw.sb("maskB", [128, 128], BF16)
        sel = fw.sb("sel", [128, 2], F32)
        lam = fw.sb("lam", [128, 1], F32)
        xring = Ring([fw.sb("x%d" % i, [128, D], F32) for i in range(2)])
        hring = Ring([fw.sb("h%d" % i, [128, D], BF16) for i in range(2)])
        hTring = Ring([fw.sb("hT%d" % i, [128, 8, 128], BF16) for i in range(2)])
        junk = fw.sb("junk", [128, D], BF16)
        sring = Ring([fw.sb("st%d" % i, [128, 4], F32) for i in range(4)])

        gB = {}
        for nm, src in (("pf", n_pf), ("postf", n_postf), ("pre", n_pre), ("post", n_post)):
            gB[nm] = fw.sb("g_" + nm, [128, D], F32, sm if nm in ("pre", "post") else None)
            fw.dma(sp, gB[nm][:], src.partition_broadcast(128), writes=[gB[nm]])
        sublnB = fw.sb("sublnB", [128, 128], F32, sm)
        fw.dma(sp, sublnB[:], subln.partition_broadcast(128), writes=[sublnB])
        fw.dma(sp, ident[:], c_ident, writes=[ident])
        fw.dma(pool, identb[:], c_ident, writes=[identb])
        fw.dma(pool, trib[:], c_tri, writes=[trib])
        fw.dma(pool, maskA[:], c_maskA, writes=[maskA])
        fw.dma(pool, maskB[:], c_maskB, writes=[maskB])
        fw.dma(sp, sel[:], c_sel, writes=[sel])
        cosA = fw.sb("cosA", [128, 33, 8], F32, sm); sinA = fw.sb("sinA", [128, 33, 8], F32, sm)
        cosO = fw.sb("cosO", [128, 17, 8], F32, sm); sinO = fw.sb("sinO", [128, 17, 8], F32, sm)
        fw.dma(sp, cosA[:].rearrange("p a b -> p (a b)"), c_cosA, writes=[cosA])
        fw.dma(sp, sinA[:].rearrange("p a b -> p (a b)"), c_sinA, writes=[sinA])
        fw.dma(sp, cosO[:].rearrange("p a b -> p (a b)"), c_cosO, writes=[cosO])
        fw.dma(sp, sinO[:].rearrange("p a b -> p (a b)"), c_sinO, writes=[sinO])
        lq = fw.sb("lq", [128, 256], F32, sm)
        fw.dma(sp, lq[:], lamq.partition_broadcast(128), writes=[lq])
        lprod = fw.sb("lprod", [128, 2, 64], F32, sm)
        lsum = fw.sb("lsum", [128, 2], F32, sm)
        lqv = lq[:].rearrange("p (a b c) -> p a b c", a=2, b=2)
        fw.tt(dve, lprod[:], lqv[:, :, 0, :], lqv[:, :, 1, :], ALU.mult, [lq], [lprod])
        fw.red(dve, lsum[:], lprod[:], [lprod], [lsum])
        fw.actv(lsum[:], lsum[:], AF.Exp, [lsum], [lsum])
        fw.tt(dve, lam[:], lsum[:, 0:1], lsum[:, 1:2], ALU.subtract, [lsum], [lam])
        fw.ts(dve, lam[:], lam[:], LAM_INIT, None, ALU.add, ALU.bypass, [lam], [lam])

        ysown = fw.sb("ysown", [128, 17, 4, 128], BF16, sm)
        ks_t = fw.sb("ks_t", [128, D], F32, sm); vs_t = fw.sb("vs_t", [128, D], F32, sm); qs_t = fw.sb("qs_t", [128, D], F32, sm)

        def rstd_from(ssq_ap, stt, n):
            fw.ts(dve, stt[:, 1:2], ssq_ap, 1.0 / n, EPS, ALU.mult, ALU.add, [stt], [stt])
            fw.actv(stt[:, 2:3], stt[:, 1:2], AF.Sqrt, [stt], [stt])
            fw.recip(stt[:, 3:4], stt[:, 2:3], [stt], [stt])
            return stt[:, 3:4]

        def norm_h(xt, g):
            stt = sring.get()
            fw.actv(junk[:], xt[:], AF.Square, [xt], [junk, stt], accum=stt[:, 0:1])
            r = rstd_from(stt[:, 0:1], stt, D)
            h = hring.get()
            fw.stt(dve, h[:], xt[:], r, g[:], ALU.mult, ALU.mult, [xt, stt, g], [h])
            hT = hTring.get()
            pb = psbf(PS[0])
            for kt in range(8):
                fw.tr(pb[:, kt * 128:(kt + 1) * 128], h[:, kt * 128:(kt + 1) * 128], identb[:], [h, identb], [PS[0]])
            fw.cp(act, hT[:].rearrange("p a b -> p (a b)"), pb, [PS[0]], [hT])
            return hT

        def load_w(dst, src_ap, nk, c0, c1, rows0=0):
            for kt in range(nk):
                fw.dma(pool, dst[:, kt, :], src_ap[rows0 + kt * 128: rows0 + (kt + 1) * 128, c0:c1], writes=[dst])

        fw.ckpt(0, stop_after)
        with contextlib.ExitStack() as s1:
            Wu = fw.sb("Wu", [128, 8, 512], BF16, s1)
            load_w(Wu, w_in, 8, 0, 512)
            BbT = fw.sb("BbT", [128, 4, 1024], BF16, s1)
            Cpad = fw.sb("Cpad", [128, 16, 2, 128], BF16, s1)
            Wt = fw.sb("Wt", [128, 16, 2, 128], F32, s1)
            Atr = fw.sb("Atr", [128, 16, 128], F32, s1); Ati = fw.sb("Ati", [128, 16, 128], F32, s1)
            a_r = fw.sb("a_r", [128, 16], F32, s1); a_i = fw.sb("a_i", [128, 16], F32, s1)
            Dcol = fw.sb("Dcol", [128, 4], F32, s1)
            with contextlib.ExitStack() as s0:
                nat = fw.sb("nat", [16, 3, 128], F32, s0)
                fw.dma(sp, nat[:, 0, :], lamre, writes=[nat])
                fw.dma(sp, nat[:, 1, :], lamim, writes=[nat])
                ldt = fw.sb("ldt", [16, 2], F32, s0)
                fw.dma(sp, ldt[:], logdt, writes=[ldt])
                fw.cp(dve, nat[:, 2, :].rearrange("p (a b) -> p a b", a=2), ldt[:].unsqueeze(2).to_broadcast([16, 2, 64]), [ldt, nat], [nat])
                L = fw.sb("L", [128, 3, 16], F32, s0)
                for i in range(3):
                    fw.tr(PS[1][:, i * 16:(i + 1) * 16], nat[:, i, :], ident[0:16, 0:16], [nat, ident], [PS[1]])
                fw.cp(dve, L[:].rearrange("p a b -> p (a b)"), PS[1][:, 0:48], [PS[1]], [L])
                lr = L[:, 0, :]; li = L[:, 1, :]
                w = fw.sb("wk", [128, 12, 16], F32, s0)
                W = lambda i: w[:, i, :]
                def horner(dst, x, coefs, rd):
                    q = W(11)
                    fw.ts(dve, q, x, float(coefs[-1]), None, ALU.mult, ALU.bypass, rd + [w], [w])
                    for c in coefs[-2:0:-1]:
                        fw.stt(dve, q, q, float(c), x, ALU.add, ALU.mult, rd + [w], [w])
                    fw.ts(dve, dst, q, float(coefs[0]), None, ALU.add, ALU.bypass, [w], [w])
                ecoef = [1.0 / math.factorial(k) for k in range(10)]
                fw.ts(dve, W(2), L[:, 2, :], 1.0 / 16, None, ALU.mult, ALU.bypass, [L], [w])
                horner(W(0), W(2), ecoef, [])
                for _ in range(4):
                    fw.tt(dve, W(0), W(0), W(0), ALU.mult, [w], [w])
                fw.tt(dve, W(3), lr, W(0), ALU.mult, [L, w], [w])
                horner(W(1), W(3), ecoef, [])
                fw.ts(dve, W(3), W(3), -1.0, None, ALU.mult, ALU.bypass, [w], [w])
                horner(W(9), W(3), ecoef, [])
                fw.tt(dve, W(2), li, W(0), ALU.mult, [L, w], [w])
                ki = fw.sb("ki", [128, 16], I32, s0)
                C1 = 6.28125
                C2 = 2 * math.pi - C1
                fw.ts(dve, ki[:], W(2), 1.0 / (2 * math.pi), None, ALU.mult, ALU.bypass, [w], [ki])
                fw.cp(dve, W(3), ki[:], [ki], [w])
                fw.stt(dve, W(4), W(3), -C1, W(2), ALU.mult, ALU.add, [w], [w])
                fw.stt(dve, W(4), W(3), -C2, W(4), ALU.mult, ALU.add, [w], [w])
                fw.ts(dve, W(3), W(4), math.pi, -2 * math.pi, ALU.is_gt, ALU.mult, [w], [w])
                fw.tt(dve, W(4), W(4), W(3), ALU.add, [w], [w])
                fw.ts(dve, W(3), W(4), -math.pi, 2 * math.pi, ALU.is_lt, ALU.mult, [w], [w])
                fw.tt(dve, W(4), W(4), W(3), ALU.add, [w], [w])
                fw.tt(dve, W(5), W(4), W(4), ALU.mult, [w], [w])
                scoef = [(-1.0) ** k / math.factorial(2 * k + 1) for k in range(12)]
                ccoef = [(-1.0) ** k / math.factorial(2 * k) for k in range(13)]
                horner(W(6), W(5), ccoef, [])
                horner(W(3), W(5), scoef, [])
                fw.tt(dve, W(4), W(3), W(4), ALU.mult, [w], [w])
                fw.tt(dve, a_r[:], W(1), W(6), ALU.mult, [w], [a_r])
                fw.tt(dve, a_i[:], W(1), W(4), ALU.mult, [w], [a_i])
                fw.tt(dve, W(10), W(9), W(4), ALU.mult, [w], [w])
                fw.ts(dve, W(10), W(10), -1.0, None, ALU.mult, ALU.bypass, [w], [w])
                fw.tt(dve, W(9), W(9), W(6), ALU.mult, [w], [w])
                fw.tt(dve, W(0), lr, lr, ALU.mult, [L], [w])
                fw.tt(dve, W(1), li, li, ALU.mult, [L], [w])
                fw.tt(dve, W(0), W(0), W(1), ALU.add, [w], [w])
                fw.recip(W(0), W(0), [w], [w])
                fw.ts(dve, W(1), a_r[:], -1.0, None, ALU.add, ALU.bypass, [a_r], [w])
                fw.tt(dve, W(2), W(1), lr, ALU.mult, [w, L], [w])
                fw.tt(dve, W(3), a_i[:], li, ALU.mult, [a_i, L], [w])
                fw.tt(dve, W(2), W(2), W(3), ALU.add, [w], [w])
                fw.tt(dve, W(7), W(2), W(0), ALU.mult, [w], [w])
                fw.tt(dve, W(2), a_i[:], lr, ALU.mult, [a_i, L], [w])
                fw.tt(dve, W(3), W(1), li, ALU.mult, [w, L], [w])
                fw.tt(dve, W(2), W(2), W(3), ALU.subtract, [w], [w])
                fw.tt(dve, W(8), W(2), W(0), ALU.mult, [w], [w])
                bL = fw.sb("bL", [128, 2, 16, 16], F32, s0)
                fw.dma(sp, bL[:, 0, :, :], bre.rearrange("g q c -> q g c"), writes=[bL])
                fw.dma(sp, bL[:, 1, :, :], bim.rearrange("g q c -> q g c"), writes=[bL])
                bb = fw.sb("bb", [128, 2, 16, 16], F32, s0)
                tmpb = fw.sb("tmpb", [128, 16, 16], F32, s0)
                frb = W(7).unsqueeze(2).to_broadcast([128, 16, 16]); fib = W(8).unsqueeze(2).to_broadcast([128, 16, 16])
                fw.tt(dve, bb[:, 0], bL[:, 0], frb, ALU.mult, [bL, w], [bb])
                fw.tt(dve, tmpb[:], bL[:, 1], fib, ALU.mult, [bL, w], [tmpb])
                fw.tt(dve, bb[:, 0], bb[:, 0], tmpb[:], ALU.subtract, [bb, tmpb], [bb])
                fw.tt(dve, bb[:, 1], bL[:, 1], frb, ALU.mult, [bL, w], [bb])
                fw.tt(dve, tmpb[:], bL[:, 0], fib, ALU.mult, [bL, w], [tmpb])
                fw.tt(dve, bb[:, 1], bb[:, 1], tmpb[:], ALU.add, [bb, tmpb], [bb])
                mZ = fw.sb("mZ", [128, 4, 128], F32, s0); mC = fw.sb("mC", [128, 4, 128], F32, s0)
                fw.dma(sp, mZ[:].rearrange("p a b -> p (a b)"), c_maskZ, writes=[mZ])
                fw.dma(sp, mC[:].rearrange("p a b -> p (a b)"), c_maskC, writes=[mC])
                Zr = Ring([fw.sb("Z%d" % i, [128, 128], F32, s0) for i in range(2)])
                for ct in range(4):
                    for gpl in range(4):
                        gp = 4 * ct + gpl
                        for ri in range(2):
                            Z = Zr.get()
                            fw.tt(dve, Z[:].rearrange("p (a b) -> p a b", a=8), mZ[:, gpl, :].rearrange("p (a b) -> p a b", a=8),
                                  bb[:, ri, gp, :].unsqueeze(1).to_broadcast([128, 8, 16]), ALU.mult, [mZ, bb], [Z])
                            bk = PS[2 + (gp * 2 + ri) % 2]
                            fw.tr(bk[:, 0:128], Z[:], ident[:], [Z, ident], [bk])
                            fw.cp(act, BbT[:, ct, (gpl * 2 + ri) * 128:(gpl * 2 + ri + 1) * 128], bk[:, 0:128], [bk], [BbT])
                Cn = fw.sb("Cn", [128, 2, 4, 64], F32, s0)
                fw.dma(sp, Cn[:, 0], cre.rearrange("c q p -> q c p"), writes=[Cn])
                fw.dma(sp, Cn[:, 1], cim.rearrange("c q p -> q c p"), writes=[Cn])
                for ct in range(4):
                    for gpl in range(4):
                        gp = 4 * ct + gpl
                        for ri in range(2):
                            Z = Zr.get()
                            fw.tt(dve, Z[:].rearrange("p (a b) -> p a b", a=2), mC[:, gpl, :].rearrange("p (a b) -> p a b", a=2),
                                  Cn[:, ri, ct, :].unsqueeze(1).to_broadcast([128, 2, 64]), ALU.mult, [mC, Cn], [Z])
                            bk = PS[2 + (gp * 2 + ri) % 2]
                            fw.tr(bk[:, 0:128], Z[:], ident[:], [Z, ident], [bk])
                            fw.actv(Cpad[:, gp, ri, :], bk[:, 0:128], AF.Copy, [bk], [Cpad], scale=(1.0 if ri == 0 else -1.0))
                fw.dma(sp, Dcol[:], dsk, writes=[Dcol])
                Avr = fw.sb("Avr", [128, 16, 128], F32, s0); Avi = fw.sb("Avi", [128, 16, 128], F32, s0)
                pt1 = fw.sb("pt1", [128, 16, 64], F32, s0); pt2 = fw.sb("pt2", [128, 16, 64], F32, s0)

                def cpow(Pr, Pi, br, bi, rd):
                    fw.cp(dve, Pr[:, :, 0:1], br.unsqueeze(2), rd, [Pr])
                    fw.cp(dve, Pi[:, :, 0:1], bi.unsqueeze(2), rd, [Pi])
                    k = 1
                    while k < 128:
                        akr = Pr[:, :, k - 1:k].to_broadcast([128, 16, k]); aki = Pi[:, :, k - 1:k].to_broadcast([128, 16, k])
                        fw.tt(dve, pt1[:, :, 0:k], Pr[:, :, 0:k], akr, ALU.mult, [Pr], [pt1])
                        fw.tt(dve, pt2[:, :, 0:k], Pi[:, :, 0:k], aki, ALU.mult, [Pi], [pt2])
                        fw.tt(dve, Pr[:, :, k:2 * k], pt1[:, :, 0:k], pt2[:, :, 0:k], ALU.subtract, [pt1, pt2, Pr], [Pr])
                        fw.tt(dve, pt1[:, :, 0:k], Pr[:, :, 0:k], aki, ALU.mult, [Pr, Pi], [pt1])
                        fw.tt(dve, pt2[:, :, 0:k], Pi[:, :, 0:k], akr, ALU.mult, [Pr, Pi], [pt2])
                        fw.tt(dve, Pi[:, :, k:2 * k], pt1[:, :, 0:k], pt2[:, :, 0:k], ALU.add, [pt1, pt2, Pi], [Pi])
                        k *= 2
                cpow(Atr, Ati, a_r[:], a_i[:], [a_r, a_i])
                cpow(Avr, Avi, W(9), W(10), [w])
                for gp in range(16):
                    for ri, Av in ((0, Avr), (1, Avi)):
                        bk = PS[2 + (gp * 2 + ri) % 2]
                        fw.tr(bk[:, 0:128], Av[:, gp, :], ident[:], [Av, ident], [bk])
                        fw.cp(act, Wt[:, gp, ri, :], bk[:, 0:128], [bk], [Wt])

            fw.barrier()
            fw.ckpt(1, stop_after)
            uTb_r = Ring([fw.sb("uTb%d" % i, [128, 4, 128], BF16, s1) for i in range(2)])
            uTf_r = Ring([fw.sb("uTf%d" % i, [128, 4, 128], F32, s1) for i in range(2)])
            X_r = Ring([fw.sb("X%d" % i, [128, 4, 2, 128], BF16, s1) for i in range(2)])
            scr_r = {nm: Ring([fw.sb(nm + str(i), [128, 4, 128], F32, s1) for i in range(2)]) for nm in ("e1", "e2", "e3", "e4", "Gr", "Gi", "Hr", "Hi")}
            Hb_r = Ring([fw.sb("Hb%d" % i, [128, 2, 4, 128], BF16, s1) for i in range(2)])
            Hc = fw.sb("Hc", [128, 2, 16], F32, s1)
            fw.mset(dve, Hc[:], 0.0, [Hc])
            yall = fw.sb("yall", [128, 2, 4, 128], BF16, s1)
            yt1 = fw.sb("yt1", [128, 4, 128], F32, s1); yt2 = fw.sb("yt2", [128, 4, 128], F32, s1); yt3 = fw.sb("yt3", [128, 4, 128], F32, s1)
            S0 = fw.sb("S0", [128, 2048], F32, s1)
            H0 = fw.sb("H0", [128, 2, 16, 128], F32, s1)

            blk = {}

            def Pre(b):
                samp = (b == 32)
                xt = xring.get()
                fw.dma(sp, xt[:], xa[b * 128:(b + 1) * 128, :], writes=[xt])
                hT = norm_h(xt, gB["pre"])
                for ct in range(4):
                    for kt in range(8):
                        fw.mm(PS[1][:, ct * 128:(ct + 1) * 128], Wu[:, kt, ct * 128:(ct + 1) * 128], hT[:, kt, :], kt == 0, kt == 7, [Wu, hT], [PS[1]])
                uTb = uTb_r.get(); uTf = uTf_r.get()
                fw.cp(act, uTb[:].rearrange("p a b -> p (a b)"), PS[1][:], [PS[1]], [uTb])
                fw.cp(dve, uTf[:].rearrange("p a b -> p (a b)"), PS[1][:], [PS[1]], [uTf])
                blk[b] = (uTb, uTf)
                if samp:
                    for ri, src in ((0, s0re), (1, s0im)):
                        fw.mset(dve, S0[:], 0.0, [S0])
                        fw.dma(sp, S0[0:16, :], src, writes=[S0])
                        for gp in range(16):
                            fw.tr(PS[2][:, 0:128], S0[:, gp * 128:(gp + 1) * 128], ident[:], [S0, ident], [PS[2]])
                            fw.cp(act, H0[:, ri, gp, :], PS[2][:, 0:128], [PS[2]], [H0])

            def A(b, ct, k):
                samp = (b == 32)
                uTb, uTf = blk[b]
                CB = (PS[4], PS[5]) if k % 2 == 0 else (PS[6], PS[7])
                if samp:
                    for gpl in range(4):
                        for ri in range(2):
                            fw.mm(CB[ri][:, gpl * 128:(gpl + 1) * 128], BbT[:, ct, (gpl * 2 + ri) * 128:(gpl * 2 + ri + 1) * 128],
                                  uTb[:, ct, :], True, True, [uTb, BbT], [CB[ri]])
                    return CB
                e1, e2, e3, e4 = (scr_r[nm].get() for nm in ("e1", "e2", "e3", "e4"))
                X = X_r.get()
                for hf in range(2):
                    fw.mm(PS[2][:], uTb[:, ct, :], BbT[:, ct, hf * 512:(hf + 1) * 512], True, True, [uTb, BbT], [PS[2]])
                    Bv = PS[2][:].rearrange("p (a r q) -> p a r q", a=2, r=2)
                    gsl = slice(4 * ct + 2 * hf, 4 * ct + 2 * hf + 2)
                    sl = slice(2 * hf, 2 * hf + 2)
                    fw.tt(dve, e1[:, sl, :], Bv[:, :, 0, :], Wt[:, gsl, 0, :], ALU.mult, [PS[2], Wt], [e1])
                    fw.tt(dve, e2[:, sl, :], Bv[:, :, 1, :], Wt[:, gsl, 1, :], ALU.mult, [PS[2], Wt], [e2])
                    fw.tt(dve, e3[:, sl, :], Bv[:, :, 0, :], Wt[:, gsl, 1, :], ALU.mult, [PS[2], Wt], [e3])
                    fw.tt(dve, e4[:, sl, :], Bv[:, :, 1, :], Wt[:, gsl, 0, :], ALU.mult, [PS[2], Wt], [e4])
                fw.tt(dve, X[:, :, 0, :], e1[:], e2[:], ALU.subtract, [e1, e2], [X])
                fw.tt(dve, X[:, :, 1, :], e3[:], e4[:], ALU.add, [e3, e4], [X])
                for gpl in range(4):
                    for ri in range(2):
                        fw.mm(CB[ri][:, gpl * 128:(gpl + 1) * 128], X[:, gpl, ri, :], trib[:], True, True, [X, trib], [CB[ri]])
                return CB

            def B(b, ct, CB):
                samp = (b == 32)
                gps = slice(4 * ct, 4 * ct + 4)
                e1, e2, e3, e4, Gr, Gi, Hr, Hi = (scr_r[nm].get() for nm in ("e1", "e2", "e3", "e4", "Gr", "Gi", "Hr", "Hi"))
                cr = CB[0][:].rearrange("p (a b) -> p a b", a=4); ci = CB[1][:].rearrange("p (a b) -> p a b", a=4)
                if not samp:
                    fw.tt(dve, Gr[:], cr, Hc[:, 0, gps].unsqueeze(2).to_broadcast([128, 4, 128]), ALU.add, [CB[0], Hc], [Gr])
                    fw.tt(dve, Gi[:], ci, Hc[:, 1, gps].unsqueeze(2).to_broadcast([128, 4, 128]), ALU.add, [CB[1], Hc], [Gi])
                    fw.tt(dve, e1[:], Gr[:], Atr[:, gps, :], ALU.mult, [Gr, Atr], [e1])
                    fw.tt(dve, e2[:], Gi[:], Ati[:, gps, :], ALU.mult, [Gi, Ati], [e2])
                    fw.tt(dve, e3[:], Gr[:], Ati[:, gps, :], ALU.mult, [Gr, Ati], [e3])
                    fw.tt(dve, e4[:], Gi[:], Atr[:, gps, :], ALU.mult, [Gi, Atr], [e4])
                    fw.tt(dve, Hr[:], e1[:], e2[:], ALU.subtract, [e1, e2], [Hr])
                    fw.tt(dve, Hi[:], e3[:], e4[:], ALU.add, [e3, e4], [Hi])
                    fw.cp(dve, Hc[:, 0, gps], Hr[:, :, 127], [Hr], [Hc])
                    fw.cp(dve, Hc[:, 1, gps], Hi[:, :, 127], [Hi], [Hc])
                else:
                    arb = a_r[:, gps].unsqueeze(2).to_broadcast([128, 4, 128]); aib = a_i[:, gps].unsqueeze(2).to_broadcast([128, 4, 128])
                    fw.tt(dve, e1[:], H0[:, 0, gps, :], arb, ALU.mult, [H0, a_r], [e1])
                    fw.tt(dve, e2[:], H0[:, 1, gps, :], aib, ALU.mult, [H0, a_i], [e2])
                    fw.tt(dve, e3[:], H0[:, 0, gps, :], aib, ALU.mult, [H0, a_i], [e3])
                    fw.tt(dve, e4[:], H0[:, 1, gps, :], arb, ALU.mult, [H0, a_r], [e4])
                    fw.tt(dve, e1[:], e1[:], e2[:], ALU.subtract, [e1, e2], [e1])
                    fw.tt(dve, e3[:], e3[:], e4[:], ALU.add, [e3, e4], [e3])
                    fw.tt(dve, Hr[:], e1[:], cr, ALU.add, [e1, CB[0]], [Hr])
                    fw.tt(dve, Hi[:], e3[:], ci, ALU.add, [e3, CB[1]], [Hi])
                Hb = Hb_r.get()
                fw.cp(act, Hb[:, 0], Hr[:], [Hr], [Hb])
                fw.cp(act, Hb[:, 1], Hi[:], [Hi], [Hb])
                if samp:
                    for ri, Hx in ((0, Hr), (1, Hi)):
                        for gpl in range(4):
                            fw.tr(PS[0][:, gpl * 128:(gpl + 1) * 128], Hx[:, gpl, :], ident[:], [Hx, ident], [PS[0]])
                        fw.cp(dve, H0[:, ri, gps, :].rearrange("p a b -> p (a b)"), PS[0][:], [PS[0]], [H0])
                for gpl in range(4):
                    for ri in range(2):
                        fw.mm(PS[3][:, ct * 128:(ct + 1) * 128], Cpad[:, 4 * ct + gpl, ri, :], Hb[:, ri, gpl, :],
                              gpl == 0 and ri == 0, gpl == 3 and ri == 1, [Cpad, Hb], [PS[3]])

            def Post(b):
                samp = (b == 32)
                uTb, uTf = blk[b]
                fw.tt(pool, yt1[:], uTf[:], Dcol[:].unsqueeze(2).to_broadcast([128, 4, 128]), ALU.mult, [uTf, Dcol], [yt1])
                fw.tt(dve, yt1[:], yt1[:], PS[3][:].rearrange("p (a b) -> p a b", a=4), ALU.add, [yt1, PS[3]], [yt1])
                fw.tt(pool, yt2[:], yt1[:], yt1[:], ALU.mult, [yt1], [yt2])
                fw.ts(pool, yt2[:], yt2[:], 0.044715, 1.0, ALU.mult, ALU.add, [yt2], [yt2])
                fw.tt(pool, yt2[:], yt2[:], yt1[:], ALU.mult, [yt2, yt1], [yt2])
                fw.actv(yt3[:], yt2[:], AF.Sigmoid, [yt2], [yt3], scale=2.0 * math.sqrt(2.0 / math.pi))
                if samp:
                    fw.tt(dve, ysown[:, 16], yt1[:], yt3[:], ALU.mult, [yt1, yt3], [ysown])
                    fw.dma(sp, o_sre, H0[:, 0].rearrange("p a b -> p (a b)"), reads=[H0], is_out=True)
                    fw.dma(sp, o_sim, H0[:, 1].rearrange("p a b -> p (a b)"), reads=[H0], is_out=True)
                else:
                    fw.tt(dve, yall[:, b % 2], yt1[:], yt3[:], ALU.mult, [yt1, yt3], [yall])
                    if b % 2 == 1:
                        m = b // 2
                        fw.ts(pool, yt2[:], yall[:, 0], sel[:, 0:1], None, ALU.mult, ALU.bypass, [yall, sel], [yt2])
                        fw.stt(dve, ysown[:, m], yall[:, 1], sel[:, 1:2], yt2[:], ALU.mult, ALU.add, [yall, sel, yt2], [ysown])
                if b == 31:
                    fw.dma(sp, o_hre, Hc[:, 0, :], reads=[Hc], is_out=True)
                    fw.dma(sp, o_him, Hc[:, 1, :], reads=[Hc], is_out=True)

            items = [(b, ct) for b in range(33) for ct in range(4)]
            Pre(0)
            cb = A(0, 0, 0)
            for k, (b, ct) in enumerate(items):
                cbn = None
                if k + 1 < len(items):
                    nb, nct = items[k + 1]
                    if nct == 0:
                        Pre(nb)
                    cbn = A(nb, nct, k + 1)
                B(b, ct, cb)
                if ct == 3:
                    Post(b)
                    fw.ckpt(2 + b, stop_after)
                cb = cbn

        fw.barrier()
        fw.ckpt(40, stop_after)
        subB = fw.sb("subB", [128, 128], F32, sm)
        fw.ts(dve, subB[:], sublnB[:], 1.0 - LAM_INIT, None, ALU.mult, ALU.bypass, [sublnB], [subB])
        rt = [fw.sb("rt%d" % i, [128, 8, 8], F32, sm) for i in range(4)]

        def rope(Xf, cs, sn):
            v = Xf[:].rearrange("p (a d) -> p a d", a=8)
            x1 = v[:, :, 0:8]; x2 = v[:, :, 8:16]
            cb = cs.unsqueeze(1).to_broadcast([128, 8, 8]); sb_ = sn.unsqueeze(1).to_broadcast([128, 8, 8])
            fw.tt(dve, rt[0][:], x1, cb, ALU.mult, [Xf], [rt[0]])
            fw.tt(dve, rt[1][:], x2, sb_, ALU.mult, [Xf], [rt[1]])
            fw.tt(dve, rt[2][:], x2, cb, ALU.mult, [Xf], [rt[2]])
            fw.tt(dve, rt[3][:], x1, sb_, ALU.mult, [Xf], [rt[3]])
            fw.tt(dve, x1, rt[0][:], rt[1][:], ALU.subtract, [rt[0], rt[1]], [Xf])
            fw.tt(dve, x2, rt[2][:], rt[3][:], ALU.add, [rt[2], rt[3]], [Xf])

        for hg in range(2):
            with contextlib.ExitStack() as s2:
                Wq = fw.sb("Wq", [128, 8, 512], BF16, s2); Wk = fw.sb("Wk", [128, 8, 512], BF16, s2); Wv = fw.sb("Wv", [128, 8, 512], BF16, s2)
                load_w(Wk, w_in, 8, 1536 + hg * 512, 1536 + hg * 512 + 512)
                load_w(Wv, w_in, 8, 2560 + hg * 512, 2560 + hg * 512 + 512)
                load_w(Wq, w_in, 8, 512 + hg * 512, 512 + hg * 512 + 512)
                KT = fw.sb("KT", [128, 4, 4096], BF16, s2)
                V = fw.sb("V", [128, 32, 4, 130], BF16, s2)
                fw.mset(pool, V[:], 1.0, [V])
                Kf_r = Ring([fw.sb("Kf%d" % i, [128, 512], F32, s2) for i in range(2)])
                Vf_r = Ring([fw.sb("Vf%d" % i, [128, 512], F32, s2) for i in range(2)])
                Qf_r = Ring([fw.sb("Qf%d" % i, [128, 512], F32, s2) for i in range(2)])
                Kb_r = Ring([fw.sb("Kb%d" % i, [128, 512], BF16, s2) for i in range(2)])
                QT_r = Ring([fw.sb("QT%d" % i, [128, 4, 128], BF16, s2) for i in range(2)])
                PT_r = Ring([fw.sb("PT%d" % i, [128, 4, 128], BF16, s2) for i in range(4)])
                o1 = fw.sb("o1", [128, 128], F32, s2); oh = fw.sb("oh", [128, 128], F32, s2)
                ob_r = Ring([fw.sb("ob%d" % i, [128, 512], BF16, s2) for i in range(2)])
                nonlocal_cnt = [0]
                for b in range(33):
                    samp = (b == 32)
                    xt = xring.get()
                    fw.dma(sp, xt[:], xa[b * 128:(b + 1) * 128, :], writes=[xt])
                    hT = norm_h(xt, gB["pre"])
                    for kt in range(8):
                        fw.mm(PS[1][:], hT[:, kt, :], Wk[:, kt, :], kt == 0, kt == 7, [hT, Wk], [PS[1]])
                    for kt in range(8):
                        fw.mm(PS[2][:], hT[:, kt, :], Wv[:, kt, :], kt == 0, kt == 7, [hT, Wv], [PS[2]])
                    Kf = Kf_r.get(); Vf = Vf_r.get()
                    fw.cp(act, Kf[:], PS[1][:], [PS[1]], [Kf])
                    rope(Kf, cosA[:, b, :], sinA[:, b, :])
                    fw.cp(act, Vf[:], PS[2][:], [PS[2]], [Vf])
                    fw.dma(sp, o_k[b * 128:(b + 1) * 128, hg * 512:(hg + 1) * 512], Kf[:], reads=[Kf], is_out=True)
                    fw.dma(sp, o_v[b * 128:(b + 1) * 128, hg * 512:(hg + 1) * 512], Vf[:], reads=[Vf], is_out=True)
                    if samp:
                        fw.cp(pool, ks_t[:, hg * 512:(hg + 1) * 512], Kf[:], [Kf], [ks_t])
                        fw.cp(pool, vs_t[:, hg * 512:(hg + 1) * 512], Vf[:], [Vf], [vs_t])
                    else:
                        Kb = Kb_r.get()
                        fw.cp(pool, Kb[:], Kf[:], [Kf], [Kb])
                        pb = psbf(PS[0])
                        for hl in range(4):
                            fw.tr(pb[:, hl * 128:(hl + 1) * 128], Kb[:, hl * 128:(hl + 1) * 128], identb[:], [Kb, identb], [PS[0]])
                        fw.cp(act, KT[:, :, b * 128:(b + 1) * 128], pb[:, 0:512].rearrange("p (a b) -> p a b", a=4), [PS[0]], [KT])
                        fw.cp(pool, V[:, b, :, 0:128], Vf[:].rearrange("p (a b) -> p a b", a=4), [Vf], [V])
                    if not (samp or b % 2 == 1):
                        continue
                    m = 16 if samp else b // 2
                    if samp:
                        hTo = hT
                    else:
                        xo_t = xring.get()
                        fw.dma(sp, xo_t[:], xo[m * 128:(m + 1) * 128, :], writes=[xo_t])
                        hTo = norm_h(xo_t, gB["pre"])
                    for kt in range(8):
                        fw.mm(PS[3][:], hTo[:, kt, :], Wq[:, kt, :], kt == 0, kt == 7, [hTo, Wq], [PS[3]])
                    Qf = Qf_r.get()
                    fw.cp(act, Qf[:], PS[3][:], [PS[3]], [Qf])
                    rope(Qf, cosO[:, m, :], sinO[:, m, :])
                    if samp:
                        fw.cp(pool, qs_t[:, hg * 512:(hg + 1) * 512], Qf[:], [Qf], [qs_t])
                        continue
                    Qb = Kb_r.get()
                    fw.cp(pool, Qb[:], Qf[:], [Qf], [Qb])
                    pb = psbf(PS[0])
                    for hl in range(4):
                        fw.tr(pb[:, hl * 128:(hl + 1) * 128], Qb[:, hl * 128:(hl + 1) * 128], identb[:], [Qb, identb], [PS[0]])
                    QT = QT_r.get()
                    fw.cp(act, QT[:].rearrange("p a b -> p (a b)"), pb[:, 0:512], [PS[0]], [QT])
                    nkb = b + 1
                    ob = ob_r.get()
                    items = []
                    for hl in range(4):
                        for c in range(2):
                            for kb0 in range(0, nkb, 4):
                                items.append((hl, c, kb0, min(4, nkb - kb0)))

                    def emit_qk(it):
                        hl, c, kb0, n = it
                        nonlocal_cnt[0] += 1
                        sbank = PS[4 + nonlocal_cnt[0] % 2]
                        for i in range(n):
                            kb = kb0 + i
                            fw.mm(sbank[:, i * 128:(i + 1) * 128], KT[c * 64:(c + 1) * 64, hl, kb * 128:(kb + 1) * 128],
                                  QT[c * 64:(c + 1) * 64, hl, :], True, True, [KT, QT], [sbank])
                        return sbank
                    pend = emit_qk(items[0])
                    for ii, it in enumerate(items):
                        hl, c, kb0, n = it
                        sbank = pend
                        if ii + 1 < len(items):
                            pend = emit_qk(items[ii + 1])
                        accb = (PS[6], PS[7]) if hl % 2 == 0 else (PS[2], PS[3])
                        acc = accb[c]
                        PT = PT_r.get()
                        fw.actv(PT[:].rearrange("p a b -> p (a b)")[:, 0:n * 128], sbank[:, 0:n * 128], AF.Exp, [sbank], [PT], scale=0.125)
                        for i in range(n):
                            kb = kb0 + i
                            if kb == nkb - 2:
                                fw.tt(pool, PT[:, i, :], PT[:, i, :], maskA[:], ALU.mult, [PT, maskA], [PT])
                            if kb == nkb - 1:
                                fw.tt(pool, PT[:, i, :], PT[:, i, :], maskB[:], ALU.mult, [PT, maskB], [PT])
                        for i in range(n):
                            kb = kb0 + i
                            fw.mm(acc[:, 0:129], PT[:, i, :], V[:, kb, hl, 0:129], kb == 0, kb == nkb - 1, [PT, V], [acc])
                        if not (c == 1 and kb0 + n == nkb):
                            continue
                        A0, A1 = accb
                        rs = sring.get()
                        fw.recip(rs[:, 0:1], A0[:, 128:129], [A0], [rs])
                        fw.recip(rs[:, 1:2], A1[:, 128:129], [A1], [rs])
                        fw.tt(dve, rs[:, 2:3], rs[:, 1:2], lam[:], ALU.mult, [rs, lam], [rs])
                        fw.ts(dve, o1[:], A1[:, 0:128], rs[:, 2:3], None, ALU.mult, ALU.bypass, [A1, rs], [o1])
                        fw.stt(dve, oh[:], A0[:, 0:128], rs[:, 0:1], o1[:], ALU.mult, ALU.subtract, [A0, rs, o1], [oh])
                        st2 = sring.get()
                        fw.actv(junk[:, 0:128], oh[:], AF.Square, [oh], [junk, st2], accum=st2[:, 0:1])
                        r = rstd_from(st2[:, 0:1], st2, 128)
                        fw.stt(dve, ob[:, hl * 128:(hl + 1) * 128], oh[:], r, subB[:], ALU.mult, ALU.mult, [oh, st2, subB], [ob])
                    fw.dma(sp, oscr[m * 128:(m + 1) * 128, hg * 512:(hg + 1) * 512], ob[:], reads=[ob])
            fw.barrier()
        fw.ckpt(50, stop_after)

        with contextlib.ExitStack() as s5:
            pt_i = fw.sb("pt_i", [128, 256], I32, s5); pt_f = fw.sb("pt_f", [128, 256], F32, s5); idx = fw.sb("idx", [128, 256], I32, s5)
            iot = fw.sb("iot", [128, 1], F32, s5)
            fw.dma(sp, pt_i[:], ptab.partition_broadcast(128), writes=[pt_i])
            fw.dma(sp, iot[:], c_iota, writes=[iot])
            fw.cp(dve, pt_f[:], pt_i[:], [pt_i], [pt_f])
            fw.ts(dve, pt_f[:], pt_f[:], 128.0, iot[:, 0:1], ALU.mult, ALU.add, [pt_f, iot], [pt_f])
            fw.cp(dve, idx[:], pt_f[:], [pt_f], [idx])
            selb_t = fw.sb("selb_t", [16, 16, 128], F32, s5)
            fw.dma(sp, selb_t[:].rearrange("p a b -> p (a b)"), c_selb, writes=[selb_t])
            Kp_r = Ring([fw.sb("Kp%d" % i, [128, 1024], F32, s5) for i in range(3)])
            Vp_r = Ring([fw.sb("Vp%d" % i, [128, 1024], F32, s5) for i in range(3)])
            Vb_r = Ring([fw.sb("Vb%d" % i, [128, 8, 130], BF16, s5) for i in range(3)])
            for t in Vb_r.tiles:
                fw.mset(pool, t[:], 1.0, [t])
            prod = fw.sb("prod", [128, 1024], F32, s5); sc = fw.sb("sc", [128, 16], F32, s5)
            Pz_r = Ring([fw.sb("Pz%d" % i, [128, 16, 16], BF16, s5) for i in range(3)])
            qB = fw.sb("qB", [128, 1024], F32, s5)
            zer = fw.sb("zer", [128, 16], BF16, s5)
            fw.mset(dve, zer[:], 0.0, [zer])
            fw.mset(dve, junk[:], 0.0, [junk])
            for k in range(6):
                fw.mm(PS[k][0:16, :], zer[:, 0:16], junk[:, 0:512], True, False, [zer, junk], [PS[k]])
            for bs in range(16):
                for hf in range(2):
                    fw.mm(PS[6 + hf][:], selb_t[:, bs, :], qs_t[0:16, hf * 512:(hf + 1) * 512], True, True, [selb_t, qs_t], [PS[6 + hf]])
                    fw.cp(act, qB[:, hf * 512:(hf + 1) * 512], PS[6 + hf][:], [PS[6 + hf]], [qB])
                for pg in range(16):
                    col = bs * 16 + pg
                    Kp = Kp_r.get(); Vp = Vp_r.get()
                    fw.dma(pool, Kp[:], cache_k, reads=[idx], writes=[Kp], indirect=idx[:, col:col + 1])
                    fw.dma(pool, Vp[:], cache_v, reads=[idx], writes=[Vp], indirect=idx[:, col:col + 1])
                    fw.tt(dve, prod[:], Kp[:], qB[:], ALU.mult, [Kp, qB], [prod])
                    fw.red(dve, sc[:], prod[:].rearrange("p (a d) -> p a d", a=16), [prod], [sc])
                    Pz = Pz_r.get()
                    fw.mset(pool, Pz[:], 0.0, [Pz])
                    fw.actv(Pz[:, :, bs], sc[:], AF.Exp, [sc], [Pz], scale=0.125)
                    Vb = Vb_r.get()
                    fw.cp(act, Vb[:, :, 0:128], Vp[:].rearrange("p (a b) -> p a b", a=8), [Vp], [Vb])
                    for hc in range(16):
                        bk = PS[hc // 3]; off = (hc % 3) * 129
                        fw.mm(bk[0:16, off:off + 129], Pz[:, hc, :], Vb[:, hc // 2, 0:129], False, False, [Pz, Vb], [bk])
            psf = fw.sb("psf", [16, 16], F32, s5); Ot = fw.sb("Ot", [16, 16, 128], F32, s5); St = fw.sb("St", [16, 16], F32, s5)
            obs = fw.sb("obs", [128, 1024], BF16, s5)
            fw.mset(pool, obs[:], 0.0, [obs])
            fw.tt(dve, prod[0:16, :], qs_t[0:16, :], ks_t[0:16, :], ALU.mult, [qs_t, ks_t], [prod])
            fw.red(dve, sc[0:16, :], prod[0:16, :].rearrange("p (a d) -> p a d", a=16), [prod], [sc])
            fw.actv(psf[:], sc[0:16, :], AF.Exp, [sc], [psf], scale=0.125)
            for hc in range(16):
                bk = PS[hc // 3]; off = (hc % 3) * 129; h_ = hc // 2
                fw.stt(dve, Ot[:, hc, :], vs_t[0:16, h_ * 128:(h_ + 1) * 128], psf[:, hc:hc + 1], bk[0:16, off:off + 128], ALU.mult, ALU.add, [vs_t, psf, bk], [Ot])
                fw.tt(dve, St[:, hc:hc + 1], bk[0:16, off + 128:off + 129], psf[:, hc:hc + 1], ALU.add, [bk, psf], [St])
            fw.recip(St[:], St[:], [St], [St])
            l1 = fw.sb("l1", [16, 4], F32, s5); o1s = fw.sb("o1s", [16, 128], F32, s5); ohs = fw.sb("ohs", [16, 128], F32, s5)
            for h_ in range(8):
                fw.tt(dve, l1[:, 0:1], St[:, 2 * h_ + 1:2 * h_ + 2], lam[0:16, :], ALU.mult, [St, lam], [l1])
                fw.ts(dve, o1s[:], Ot[:, 2 * h_ + 1, :], l1[:, 0:1], None, ALU.mult, ALU.bypass, [Ot, l1], [o1s])
                fw.stt(dve, ohs[:], Ot[:, 2 * h_, :], St[:, 2 * h_:2 * h_ + 1], o1s[:], ALU.mult, ALU.subtract, [Ot, St, o1s], [ohs])
                fw.actv(junk[0:16, 0:128], ohs[:], AF.Square, [ohs], [junk, l1], accum=l1[:, 1:2])
                fw.ts(dve, l1[:, 2:3], l1[:, 1:2], 1.0 / 128, EPS, ALU.mult, ALU.add, [l1], [l1])
                fw.actv(l1[:, 3:4], l1[:, 2:3], AF.Sqrt, [l1], [l1])
                fw.recip(l1[:, 3:4], l1[:, 3:4], [l1], [l1])
                fw.stt(dve, obs[0:16, h_ * 128:(h_ + 1) * 128], ohs[:], l1[:, 3:4], subB[0:16, :], ALU.mult, ALU.mult, [ohs, l1, subB], [obs])
            fw.dma(sp, oscr[16 * 128:17 * 128, :], obs[:], reads=[obs])
        fw.barrier()
        fw.ckpt(60, stop_after)

        with contextlib.ExitStack() as s3:
            Wgs = fw.sb("Wgs", [128, 8, 1024], BF16, s3); Wga_ = fw.sb("Wga", [128, 8, 1024], BF16, s3)
            Wa = fw.sb("Wa", [128, 4, 1024], BF16, s3); Wb = fw.sb("Wb", [128, 4, 1024], BF16, s3); Wo = fw.sb("Wo", [128, 8, 1024], BF16, s3)
            load_w(Wgs, w_in, 8, 3584, 4608); load_w(Wga_, w_in, 8, 4608, 5632)
            load_w(Wa, wga, 4, 0, 1024); load_w(Wb, wgb, 4, 0, 1024); load_w(Wo, w_o, 8, 0, 1024)
            sgs = fw.sb("sgs", [128, 1024], F32, s3); sga = fw.sb("sga", [128, 1024], F32, s3)
            sgb = fw.sb("sgb", [128, 1024], F32, s3); t1 = fw.sb("t1", [128, 1024], F32, s3); t2 = fw.sb("t2", [128, 1024], F32, s3)
            obl = fw.sb("obl", [128, 1024], BF16, s3); mixb = fw.sb("mixb", [128, 1024], BF16, s3)
            mixT = fw.sb("mixT", [128, 8, 128], BF16, s3)
            for m in range(17):
                xt = xring.get()
                fw.dma(sp, xt[:], xo[m * 128:(m + 1) * 128, :], writes=[xt])
                fw.dma(sp, obl[:], oscr[m * 128:(m + 1) * 128, :], writes=[obl])
                hT = norm_h(xt, gB["pre"])
                for hf in range(2):
                    for kt in range(8):
                        fw.mm(PS[1 + hf][:], hT[:, kt, :], Wgs[:, kt, hf * 512:(hf + 1) * 512], kt == 0, kt == 7, [hT, Wgs], [PS[1 + hf]])
                    for kt in range(8):
                        fw.mm(PS[3 + hf][:], hT[:, kt, :], Wga_[:, kt, hf * 512:(hf + 1) * 512], kt == 0, kt == 7, [hT, Wga_], [PS[3 + hf]])
                for hf in range(2):
                    fw.actv(sgs[:, hf * 512:(hf + 1) * 512], PS[1 + hf][:], AF.Sigmoid, [PS[1 + hf]], [sgs])
                    fw.actv(sga[:, hf * 512:(hf + 1) * 512], PS[3 + hf][:], AF.Sigmoid, [PS[3 + hf]], [sga])
                for hf in range(2):
                    for ct in range(4):
                        fw.mm(PS[1 + hf][:], ysown[:, m, ct, :], Wa[:, ct, hf * 512:(hf + 1) * 512], ct == 0, ct == 3, [ysown, Wa], [PS[1 + hf]])
                    for ct in range(4):
                        fw.mm(PS[3 + hf][:], ysown[:, m, ct, :], Wb[:, ct, hf * 512:(hf + 1) * 512], ct == 0, ct == 3, [ysown, Wb], [PS[3 + hf]])
                for hf in range(2):
                    sl = slice(hf * 512, (hf + 1) * 512)
                    fw.actv(sgb[:, sl], PS[3 + hf][:], AF.Sigmoid, [PS[3 + hf]], [sgb])
                    fw.tt(dve, t1[:, sl], PS[1 + hf][:], sgb[:, sl], ALU.mult, [PS[1 + hf], sgb], [t1])
                fw.tt(pool, t1[:], t1[:], sgs[:], ALU.mult, [t1, sgs], [t1])
                fw.tt(pool, t2[:], sga[:], obl[:], ALU.mult, [sga, obl], [t2])
                fw.tt(pool, mixb[:], t1[:], t2[:], ALU.add, [t1, t2], [mixb])
                pb = psbf(PS[0])
                for kt in range(8):
                    fw.tr(pb[:, kt * 128:(kt + 1) * 128], mixb[:, kt * 128:(kt + 1) * 128], identb[:], [mixb, identb], [PS[0]])
                fw.cp(act, mixT[:].rearrange("p a b -> p (a b)"), pb, [PS[0]], [mixT])
                for hf in range(2):
                    for kt in range(8):
                        fw.mm(PS[5 + hf][:], mixT[:, kt, :], Wo[:, kt, hf * 512:(hf + 1) * 512], kt == 0, kt == 7, [mixT, Wo], [PS[5 + hf]])
                    fw.cp(dve, t1[:, hf * 512:(hf + 1) * 512], PS[5 + hf][:], [PS[5 + hf]], [t1])
                stt_ = sring.get()
                fw.actv(junk[:], t1[:], AF.Square, [t1], [junk, stt_], accum=stt_[:, 0:1])
                r = rstd_from(stt_[:, 0:1], stt_, D)
                fw.stt(dve, t2[:], t1[:], r, gB["post"][:], ALU.mult, ALU.mult, [t1, stt_, gB["post"]], [t2])
                fw.tt(pool, t2[:], t2[:], xt[:], ALU.add, [t2, xt], [t2])
                fw.dma(sp, x1scr[m * 128:(m + 1) * 128, :], t2[:], reads=[t2])
        fw.barrier()
        fw.ckpt(70, stop_after)
        if dbg:
            o_dbg = dout('o_dbg', [128, 17 * 512])
            fw.dma(pool, o_dbg, ysown[:].rearrange('p a b c -> p (a b c)'), reads=[ysown], is_out=True)
        sm.close()
        with contextlib.ExitStack() as s4:
            Wg = fw.sb("Wg", [128, 8, 2816], BF16, s4); Wup = fw.sb("Wup", [128, 8, 2816], BF16, s4); Wd = fw.sb("Wd", [128, 22, 1024], BF16, s4)
            load_w(Wg, w_gate, 8, 0, 2816); load_w(Wup, w_up, 8, 0, 2816); load_w(Wd, w_down, 22, 0, 1024)
            actT = fw.sb("actT", [128, 22, 128], BF16, s4)
            sg_r = Ring([fw.sb("sg%d" % i, [128, 512], F32, s4) for i in range(2)])
            tg_r = Ring([fw.sb("tg%d" % i, [128, 512], F32, s4) for i in range(2)])
            fsb = fw.sb("fsb", [128, 1024], F32, s4); ysb = fw.sb("ysb", [128, 1024], F32, s4)
            cnt = 0
            for m in range(17):
                xt = xring.get()
                fw.dma(sp, xt[:], x1scr[m * 128:(m + 1) * 128, :], writes=[xt])
                hT = norm_h(xt, gB["pf"])
                for ft0 in range(0, 22, 4):
                    n = min(4, 22 - ft0)
                    bg = PS[1 + 2 * (cnt % 2)]; bu = PS[2 + 2 * (cnt % 2)]; cnt += 1
                    for i in range(n):
                        ft = ft0 + i
                        for kt in range(8):
                            fw.mm(bg[:, i * 128:(i + 1) * 128], Wg[:, kt, ft * 128:(ft + 1) * 128], hT[:, kt, :], kt == 0, kt == 7, [Wg, hT], [bg])
                        for kt in range(8):
                            fw.mm(bu[:, i * 128:(i + 1) * 128], Wup[:, kt, ft * 128:(ft + 1) * 128], hT[:, kt, :], kt == 0, kt == 7, [Wup, hT], [bu])
                    sg = sg_r.get(); tg = tg_r.get()
                    fw.actv(sg[:, 0:n * 128], bg[:, 0:n * 128], AF.Sigmoid, [bg], [sg])
                    fw.tt(dve, tg[:, 0:n * 128], bg[:, 0:n * 128], sg[:, 0:n * 128], ALU.mult, [bg, sg], [tg])
                    fw.tt(dve, actT[:, ft0:ft0 + n, :].rearrange("p a b -> p (a b)"), bu[:, 0:n * 128], tg[:, 0:n * 128], ALU.mult, [bu, tg], [actT])
                for hf in range(2):
                    for ft in range(22):
                        fw.mm(PS[5 + hf][:], actT[:, ft, :], Wd[:, ft, hf * 512:(hf + 1) * 512], ft == 0, ft == 21, [actT, Wd], [PS[5 + hf]])
                    fw.cp(act, fsb[:, hf * 512:(hf + 1) * 512], PS[5 + hf][:], [PS[5 + hf]], [fsb])
                stt_ = sring.get()
                fw.actv(junk[:], fsb[:], AF.Square, [fsb], [junk, stt_], accum=stt_[:, 0:1])
                r = rstd_from(stt_[:, 0:1], stt_, D)
                fw.stt(dve, ysb[:], fsb[:], r, gB["postf"][:], ALU.mult, ALU.mult, [fsb, stt_, gB["postf"]], [ysb])
                fw.tt(pool, ysb[:], ysb[:], xt[:], ALU.add, [ysb, xt], [ysb])
                fw.dma(sp, o_y[m * 128:(m + 1) * 128, :], ysb[:], reads=[ysb], is_out=True)
        fw.finish()
    return nc


def _consts(j):
    c = {}
    c["c_ident"] = np.eye(128, dtype=np.float32)
    s = np.arange(128)
    c["c_tri"] = (s[:, None] <= s[None, :]).astype(np.float32)
    tri = (s[:, None] <= s[None, :]).astype(np.float32)
    if j == 0:
        c["c_maskA"] = tri; c["c_maskB"] = np.zeros((128, 128), np.float32)
    else:
        c["c_maskA"] = np.ones((128, 128), np.float32); c["c_maskB"] = tri
    mz = np.zeros((128, 4, 128), np.float32)
    for g2 in range(2):
        for gpl in range(4):
            gl = 2 * gpl + g2
            mz[g2 * 64:(g2 + 1) * 64, gpl, gl * 16:(gl + 1) * 16] = 1.0
    c["c_maskZ"] = mz.reshape(128, 512)
    c["c_maskC"] = np.ascontiguousarray(mz.transpose(2, 1, 0)).reshape(128, 512)
    sel = np.zeros((128, 2), np.float32); sel[:, j] = 1.0
    c["c_sel"] = sel
    half = 8
    inv = (500000.0 ** (-np.arange(half, dtype=np.float32) * 2.0 / 16)).astype(np.float32)
    posA = np.concatenate([np.arange(4096), np.full(128, 2048)]).astype(np.float32).reshape(33, 128)
    angA = posA[:, :, None] * inv[None, None, :]
    c["c_cosA"] = np.ascontiguousarray(np.cos(angA).transpose(1, 0, 2)).reshape(128, 33 * 8).astype(np.float32)
    c["c_sinA"] = np.ascontiguousarray(np.sin(angA).transpose(1, 0, 2)).reshape(128, 33 * 8).astype(np.float32)
    blocks = [2 * m + j for m in range(16)] + [32]
    c["c_cosO"] = np.ascontiguousarray(np.cos(angA[blocks]).transpose(1, 0, 2)).reshape(128, 17 * 8).astype(np.float32)
    c["c_sinO"] = np.ascontiguousarray(np.sin(angA[blocks]).transpose(1, 0, 2)).reshape(128, 17 * 8).astype(np.float32)
    sb = np.zeros((16, 16, 128), np.float32)
    for b in range(16):
        sb[b, b, :] = 1.0
    c["c_selb"] = sb.reshape(16, 2048)
    c["c_iota"] = np.arange(128, dtype=np.float32).reshape(128, 1)
    return c


def make_in_maps(inp):
    f = lambda a: np.ascontiguousarray(np.asarray(a))
    maps = []
    ck = f(inp["cache_k"]).reshape(-1, 1024)
    cv = f(inp["cache_v"]).reshape(-1, 1024)
    xp = f(inp["x_prompt"]); xs = f(inp["x_sample"]).reshape(128, 1024)
    shared = {
        "w_in": f(inp["w_in"])[0], "w_glu_a": f(inp["w_glu_a"])[0], "w_glu_b": f(inp["w_glu_b"])[0],
        "w_o": f(inp["w_o"])[0], "w_gate": f(inp["w_gate"])[0], "w_up": f(inp["w_up"])[0], "w_down": f(inp["w_down"])[0],
        "n_pre": f(inp["norm_pre_mix"]), "n_post": f(inp["norm_post_mix"]), "n_pf": f(inp["norm_pre_ffn"]),
        "n_postf": f(inp["norm_post_ffn"]), "subln": f(inp["subln_gain"]),
        "lamq": np.concatenate([f(inp["lambda_q1"]), f(inp["lambda_k1"]), f(inp["lambda_q2"]), f(inp["lambda_k2"])], axis=1),
        "lamre": f(inp["ssm_lambda_re"]).reshape(16, 128), "lamim": f(inp["ssm_lambda_im"]).reshape(16, 128),
        "logdt": f(inp["ssm_log_dt"]).reshape(16, 2),
        "bre": f(inp["ssm_b_re"]).reshape(16, 128, 16), "bim": f(inp["ssm_b_im"]).reshape(16, 128, 16),
        "cre": f(inp["ssm_c_re"]).reshape(4, 128, 64), "cim": f(inp["ssm_c_im"]).reshape(4, 128, 64),
        "dsk": np.ascontiguousarray(f(inp["ssm_d"]).reshape(4, 128).T),
        "cache_k": ck, "cache_v": cv,
    }
    for core in range(8):
        s, j = core // 2, core % 2
        xsb = np.zeros((128, 1024), np.float32)
        xsb[:16] = xs[core * 16:(core + 1) * 16]
        xa = np.concatenate([xp[s], xsb], axis=0)
        own = xp[s].reshape(32, 128, 1024)[j::2].reshape(2048, 1024)
        xo = np.concatenate([own, xsb], axis=0)
        m = dict(shared)
        m.update(_consts(j))
        m["xa"] = xa; m["xo"] = xo
        m["ptab"] = f(inp["page_table"])[core * 16:(core + 1) * 16].reshape(1, 256).astype(np.int32)
        m["s0re"] = f(inp["state_ssm_re"])[0, core * 16:(core + 1) * 16].reshape(16, 2048)
        m["s0im"] = f(inp["state_ssm_im"])[0, core * 16:(core + 1) * 16].reshape(16, 2048)
        maps.append(m)
    return maps


_NC = None


def kernel(**inp):
    global _NC
    if _NC is None:
        _NC = build_nc()
    maps = make_in_maps(inp)
    res = run_bass_kernel_spmd(_NC, maps, core_ids=list(range(8))).results
    return assemble(res)


def assemble(res):
    yp = np.zeros((4, 32, 128, 1024), np.float32)
    ys = np.zeros((128, 1, 1024), np.float32)
    kp = np.zeros((1, 4, 4096, 8, 128), np.float32); vp = np.zeros((1, 4, 4096, 8, 128), np.float32)
    hre = np.zeros((1, 4, 32, 64), np.float32); him = np.zeros((1, 4, 32, 64), np.float32)
    ksm = np.zeros((1, 128, 1, 8, 128), np.float32); vsm = np.zeros((1, 128, 1, 8, 128), np.float32)
    sre = np.zeros((1, 128, 32, 64), np.float32); sim = np.zeros((1, 128, 32, 64), np.float32)
    for core in range(8):
        r = res[core]
        s, j = core // 2, core % 2
        yp[s, j::2] = r["o_y"][:2048].reshape(16, 128, 1024)
        ys[core * 16:(core + 1) * 16, 0] = r["o_y"][2048:2048 + 16]
        if j == 0:
            kp[0, s] = r["o_k"][:4096].reshape(4096, 8, 128)
            vp[0, s] = r["o_v"][:4096].reshape(4096, 8, 128)
            hre[0, s] = r["o_hre"].reshape(2, 64, 16).transpose(2, 0, 1).reshape(32, 64)
            him[0, s] = r["o_him"].reshape(2, 64, 16).transpose(2, 0, 1).reshape(32, 64)
        ksm[0, core * 16:(core + 1) * 16, 0] = r["o_k"][4096:4096 + 16].reshape(16, 8, 128)
        vsm[0, core * 16:(core + 1) * 16, 0] = r["o_v"][4096:4096 + 16].reshape(16, 8, 128)
        sre[0, core * 16:(core + 1) * 16] = r["o_sre"][:16].reshape(16, 32, 64)
        sim[0, core * 16:(core + 1) * 16] = r["o_sim"][:16].reshape(16, 32, 64)
    return (yp.reshape(4, 4096, 1024), ys, kp, vp, hre, him, ksm, vsm, sre, sim)
```

```python
import contextlib
import math
import numpy as np
import concourse.bass as bass
import concourse.mybir as mybir
from concourse.bass_utils import run_bass_kernel_spmd

F32 = mybir.dt.float32
BF16 = mybir.dt.bfloat16
I32 = mybir.dt.int32
AF = mybir.ActivationFunctionType
ALU = mybir.AluOpType
AX = mybir.AxisListType

D = 1024
NB = 32
NOWN = 16
EPS = 1e-6
LAM_INIT = 0.2


class T:
    __slots__ = ("ap", "w", "r", "name", "psum")

    def __init__(self, ap, name="", psum=False):
        self.psum = psum
        self.ap = ap
        self.w = None
        self.r = {}
        self.name = name

    def __getitem__(self, idx):
        return self.ap[idx]


class Eng:
    def __init__(self, fw, name, raw, selfsync=True):
        self.name = name
        self.raw = raw
        self.sem = fw.new_sem("e_" + name)
        self.n = 0
        self.seen = {}
        self.ops = []
        self.pending = []
        self.selfsync = selfsync


class FW:
    NDMA = 40

    def __init__(self, nc, stack):
        self.nc = nc
        self.stack = stack
        self.sems = {}
        self.pe = Eng(self, "pe", nc.tensor, selfsync=False)
        self.act = Eng(self, "act", nc.scalar)
        self.dve = Eng(self, "dve", nc.vector)
        self.pool = Eng(self, "pool", nc.gpsimd)
        self.sp = Eng(self, "sp", nc.sync)
        self.dma_sems = [self.new_sem("d%d" % i) for i in range(self.NDMA)]
        self.dma_cnt = [0] * self.NDMA
        self.dma_rr = 0
        self.out_events = []

    def new_sem(self, name):
        s = self.stack.enter_context(self.nc.semaphore(name))
        self.sems[id(s)] = s
        return s

    def sb(self, name, shape, dt, stack=None):
        st = stack or self.stack
        self.uid = getattr(self, 'uid', 0) + 1
        name = '%s_%d' % (name, self.uid)
        return T(st.enter_context(self.nc.sbuf_tensor(name, list(shape), dt)), name)

    def ps(self, name, shape, dt=F32):
        return T(self.stack.enter_context(self.nc.psum_tensor(name, list(shape), dt)), name, psum=True)

    def _deps(self, eng, reads, writes):
        need = {}
        for t in reads:
            if t.w is not None and need.get(t.w[0], 0) < t.w[1]:
                need[t.w[0]] = t.w[1]
            if t.psum:
                for s, v in t.r.items():
                    if s != id(eng.sem) and need.get(s, 0) < v:
                        need[s] = v
        for t in writes:
            if t.w is not None and need.get(t.w[0], 0) < t.w[1]:
                need[t.w[0]] = t.w[1]
            for s, v in t.r.items():
                if need.get(s, 0) < v:
                    need[s] = v
        waits = []
        for s, v in need.items():
            if s == id(eng.sem) and not eng.selfsync:
                continue
            if eng.seen.get(s, 0) >= v:
                continue
            eng.seen[s] = v
            waits.append((self.sems[s], v))
        if eng.pending:
            waits = eng.pending + waits
            eng.pending = []
        return waits

    def _mark(self, ev, reads, writes):
        for t in reads:
            if t.r.get(ev[0], 0) < ev[1]:
                t.r[ev[0]] = ev[1]
        for t in writes:
            t.w = ev
            t.r = {}

    stopped = False

    def barrier(self):
        engs = [self.pe, self.act, self.dve, self.pool, self.sp]
        for e in engs:
            for o in engs:
                if o is not e and o.n > 0 and e.seen.get(id(o.sem), 0) < o.n:
                    e.seen[id(o.sem)] = o.n
                    e.pending.append((o.sem, o.n))
            for k, sem in enumerate(self.dma_sems):
                v = self.dma_cnt[k]
                if v and e.seen.get(id(sem), 0) < v:
                    e.seen[id(sem)] = v
                    e.pending.append((sem, v))

    def ckpt(self, k, stop_after):
        if stop_after <= k:
            self.stopped = True

    def op(self, eng, fn, reads=(), writes=()):
        if self.stopped:
            return None
        waits = self._deps(eng, reads, writes)
        eng.n += 1
        ev = (id(eng.sem), eng.n)
        eng.ops.append((waits, fn, (eng.sem, 1)))
        self._mark(ev, reads, writes)
        return ev

    def dma(self, eng, out, in_, reads=(), writes=(), is_out=False, indirect=None):
        if self.stopped:
            return None
        k = self.dma_rr
        self.dma_rr = (k + 1) % self.NDMA
        sem = self.dma_sems[k]
        waits = self._deps(eng, reads, writes)
        prev = self.dma_cnt[k]
        if prev and eng.seen.get(id(sem), 0) < prev:
            eng.seen[id(sem)] = prev
            waits.append((sem, prev))
        self.dma_cnt[k] = prev + 16
        ev = (id(sem), prev + 16)
        if indirect is None:
            fn = lambda e: e.dma_start(out=out, in_=in_)
        else:
            fn = lambda e: e.indirect_dma_start(out=out, out_offset=None, in_=in_,
                                                in_offset=bass.IndirectOffsetOnAxis(ap=indirect, axis=0))
        eng.ops.append((waits, fn, (sem, 16)))
        self._mark(ev, reads, writes)
        if is_out:
            self.out_events.append(ev)
        return ev

    def mm(self, out, lhsT, rhs, start, stop, reads, writes):
        return self.op(self.pe, lambda e: e.matmul(out, lhsT=lhsT, rhs=rhs, start=start, stop=stop,
                                                   skip_group_check=True), reads, writes)

    def tr(self, out, in_, ident, reads, writes):
        return self.op(self.pe, lambda e: e.transpose(out, in_, ident), reads, writes)

    def actv(self, out, in_, func, reads, writes, scale=1.0, bias=None, accum=None, eng=None):
        kw = {}
        if bias is not None:
            kw["bias"] = bias
        if accum is not None:
            kw["accum_out"] = accum
        return self.op(self.act, lambda e: e.activation(out=out, in_=in_, func=func, scale=scale, **kw),
                       reads, writes)

    def tt(self, eng, out, in0, in1, op, reads, writes):
        return self.op(eng, lambda e: e.tensor_tensor(out=out, in0=in0, in1=in1, op=op), reads, writes)

    def stt(self, eng, out, in0, scalar, in1, op0, op1, reads, writes):
        return self.op(eng, lambda e: e.scalar_tensor_tensor(out=out, in0=in0, scalar=scalar, in1=in1,
                                                             op0=op0, op1=op1), reads, writes)

    def ts(self, eng, out, in0, s1, s2, op0, op1, reads, writes):
        return self.op(eng, lambda e: e.tensor_scalar(out=out, in0=in0, scalar1=s1, scalar2=s2, op0=op0,
                                                      op1=op1), reads, writes)

    def cp(self, eng, out, in_, reads, writes):
        if eng is self.act:
            return self.op(eng, lambda e: e.copy(out=out, in_=in_), reads, writes)
        return self.op(eng, lambda e: e.tensor_copy(out=out, in_=in_), reads, writes)

    def red(self, eng, out, in_, reads, writes, op=ALU.add):
        return self.op(eng, lambda e: e.tensor_reduce(out=out, in_=in_, axis=AX.X, op=op), reads, writes)

    def mset(self, eng, out, val, writes):
        return self.op(eng, lambda e: e.memset(out, val), (), writes)

    def recip(self, out, in_, reads, writes):
        return self.op(self.dve, lambda e: e.reciprocal(out=out, in_=in_), reads, writes)

    def finish(self):
        final = {}
        for s, v in self.out_events:
            if final.get(s, 0) < v:
                final[s] = v
        nc = self.nc
        with nc.Block() as block:
            def emit(e, last=False):
                def body(raw):
                    for waits, fn, inc in e.ops:
                        for s, v in waits:
                            raw.wait_ge(s, v)
                        fn(raw).then_inc(inc[0], inc[1])
                    if last:
                        for s, v in final.items():
                            raw.wait_ge(self.sems[s], v)
                return body
            block.tensor(emit(self.pe))
            block.scalar(emit(self.act))
            block.vector(emit(self.dve))
            block.gpsimd(emit(self.pool))
            block.sync(emit(self.sp, True))


class Stop(Exception):
    pass


class Ring:
    def __init__(self, tiles):
        self.tiles = tiles
        self.i = 0

    def get(self):
        t = self.tiles[self.i % len(self.tiles)]
        self.i += 1
        return t


def build_nc(stop_after=99, cache_rows=2560 * 128, dbg=False):
    nc = bass.Bass("TRN2", target_bir_lowering=False)

    def din(name, shape, dt=F32):
        return nc.dram_tensor(name, list(shape), dt, kind="ExternalInput").ap()

    def dout(name, shape, dt=F32):
        return nc.dram_tensor(name, list(shape), dt, kind="ExternalOutput").ap()

    xa = din("xa", [33 * 128, D])
    xo = din("xo", [17 * 128, D])
    w_in = din("w_in", [D, 5632])
    wga = din("w_glu_a", [512, D]); wgb = din("w_glu_b", [512, D])
    w_o = din("w_o", [D, D])
    w_gate = din("w_gate", [D, 2816]); w_up = din("w_up", [D, 2816]); w_down = din("w_down", [2816, D])
    n_pre = din("n_pre", [1, D]); n_post = din("n_post", [1, D]); n_pf = din("n_pf", [1, D]); n_postf = din("n_postf", [1, D])
    subln = din("subln", [1, 128])
    lamq = din("lamq", [1, 256])
    lamre = din("lamre", [16, 128]); lamim = din("lamim", [16, 128]); logdt = din("logdt", [16, 2])
    bre = din("bre", [16, 128, 16]); bim = din("bim", [16, 128, 16])
    cre = din("cre", [4, 128, 64]); cim = din("cim", [4, 128, 64])
    dsk = din("dsk", [128, 4])
    cache_k = din("cache_k", [cache_rows, D]); cache_v = din("cache_v", [cache_rows, D])
    ptab = din("ptab", [1, 256], I32)
    s0re = din("s0re", [16, 2048]); s0im = din("s0im", [16, 2048])
    c_ident = din("c_ident", [128, 128]); c_tri = din("c_tri", [128, 128])
    c_maskA = din("c_maskA", [128, 128]); c_maskB = din("c_maskB", [128, 128])
    c_maskZ = din("c_maskZ", [128, 512]); c_maskC = din("c_maskC", [128, 512])
    c_sel = din("c_sel", [128, 2])
    c_cosA = din("c_cosA", [128, 33 * 8]); c_sinA = din("c_sinA", [128, 33 * 8])
    c_cosO = din("c_cosO", [128, 17 * 8]); c_sinO = din("c_sinO", [128, 17 * 8])
    c_selb = din("c_selb", [16, 16 * 128])
    c_iota = din("c_iota", [128, 1])

    o_y = dout("o_y", [17 * 128, D])
    o_k = dout("o_k", [33 * 128, D]); o_v = dout("o_v", [33 * 128, D])
    o_hre = dout("o_hre", [128, 16]); o_him = dout("o_him", [128, 16])
    o_sre = dout("o_sre", [128, 2048]); o_sim = dout("o_sim", [128, 2048])

    x1scr = nc.dram_tensor("x1scr", [17 * 128, D], F32, kind="Internal").ap()
    oscr = nc.dram_tensor("oscr", [17 * 128, D], BF16, kind="Internal").ap()

    with contextlib.ExitStack() as st:
        fw = FW(nc, st)
        sm = contextlib.ExitStack()
        pe, act, dve, pool, sp = fw.pe, fw.act, fw.dve, fw.pool, fw.sp
        PS = [fw.ps("bank%d" % i, [128, 512]) for i in range(8)]

        def psbf(t):
            return t.ap[:].bitcast(BF16)

        ident = fw.sb("ident", [128, 128], F32)
        identb = fw.sb("identb", [128, 128], BF16)
        trib = fw.sb("trib", [128, 128], BF16)
        maskA = fw.sb("maskA", [128, 128], BF16); maskB = fw.sb("maskB", [128, 128], BF16)
        sel = fw.sb("sel", [128, 2], F32)
        lam = fw.sb("lam", [128, 1], F32)
        xring = Ring([fw.sb("x%d" % i, [128, D], F32) for i in range(2)])
        hring = Ring([fw.sb("h%d" % i, [128, D], BF16) for i in range(2)])
        hTring = Ring([fw.sb("hT%d" % i, [128, 8, 128], BF16) for i in range(2)])
        junk = fw.sb("junk", [128, D], BF16)
        sring = Ring([fw.sb("st%d" % i, [128, 4], F32) for i in range(4)])

        gB = {}
        for nm, src in (("pf", n_pf), ("postf", n_postf), ("pre", n_pre), ("post", n_post)):
            gB[nm] = fw.sb("g_" + nm, [128, D], F32, sm if nm in ("pre", "post") else None)
            fw.dma(sp, gB[nm][:], src.partition_broadcast(128), writes=[gB[nm]])
        sublnB = fw.sb("sublnB", [128, 128], F32, sm)
        fw.dma(sp, sublnB[:], subln.partition_broadcast(128), writes=[sublnB])
        fw.dma(sp, ident[:], c_ident, writes=[ident])
        fw.dma(pool, identb[:], c_ident, writes=[identb])
        fw.dma(pool, trib[:], c_tri, writes=[trib])
        fw.dma(pool, maskA[:], c_maskA, writes=[maskA])
        fw.dma(pool, maskB[:], c_maskB, writes=[maskB])
        fw.dma(sp, sel[:], c_sel, writes=[sel])
        cosA = fw.sb("cosA", [128, 33, 8], F32, sm); sinA = fw.sb("sinA", [128, 33, 8], F32, sm)
        cosO = fw.sb("cosO", [128, 17, 8], F32, sm); sinO = fw.sb("sinO", [128, 17, 8], F32, sm)
        fw.dma(sp, cosA[:].rearrange("p a b -> p (a b)"), c_cosA, writes=[cosA])
        fw.dma(sp, sinA[:].rearrange("p a b -> p (a b)"), c_sinA, writes=[sinA])
        fw.dma(sp, cosO[:].rearrange("p a b -> p (a b)"), c_cosO, writes=[cosO])
        fw.dma(sp, sinO[:].rearrange("p a b -> p (a b)"), c_sinO, writes=[sinO])
        lq = fw.sb("lq", [128, 256], F32, sm)
        fw.dma(sp, lq[:], lamq.partition_broadcast(128), writes=[lq])
        lprod = fw.sb("lprod", [128, 2, 64], F32, sm)
        lsum = fw.sb("lsum", [128, 2], F32, sm)
        lqv = lq[:].rearrange("p (a b c) -> p a b c", a=2, b=2)
        fw.tt(dve, lprod[:], lqv[:, :, 0, :], lqv[:, :, 1, :], ALU.mult, [lq], [lprod])
        fw.red(dve, lsum[:], lprod[:], [lprod], [lsum])
        fw.actv(lsum[:], lsum[:], AF.Exp, [lsum], [lsum])
        fw.tt(dve, lam[:], lsum[:, 0:1], lsum[:, 1:2], ALU.subtract, [lsum], [lam])
        fw.ts(dve, lam[:], lam[:], LAM_INIT, None, ALU.add, ALU.bypass, [lam], [lam])

        ysown = fw.sb("ysown", [128, 17, 4, 128], BF16, sm)
        ks_t = fw.sb("ks_t", [128, D], F32, sm); vs_t = fw.sb("vs_t", [128, D], F32, sm); qs_t = fw.sb("qs_t", [128, D], F32, sm)

        def rstd_from(ssq_ap, stt, n):
            fw.ts(dve, stt[:, 1:2], ssq_ap, 1.0 / n, EPS, ALU.mult, ALU.add, [stt], [stt])
            fw.actv(stt[:, 2:3], stt[:, 1:2], AF.Ln, [stt], [stt])
            fw.actv(stt[:, 3:4], stt[:, 2:3], AF.Exp, [stt], [stt], scale=-0.5)
            return stt[:, 3:4]

        def norm_h(xt, g, ring=None):
            stt = sring.get()
            fw.op(dve, lambda e, o=junk[:], i=xt[:], a=stt[:, 0:1]: e.scalar_tensor_tensor(out=o, in0=i, scalar=1.0, in1=i, op0=ALU.mult, op1=ALU.mult, accum_out=a), [xt], [junk, stt])
            r = rstd_from(stt[:, 0:1], stt, D)
            h = hring.get()
            fw.stt(dve, h[:], xt[:], r, g[:], ALU.mult, ALU.mult, [xt, stt, g], [h])
            hT = (ring or hTring).get()
            pb = psbf(PS[0])
            for kt in range(8):
                fw.tr(pb[:, kt * 128:(kt + 1) * 128], h[:, kt * 128:(kt + 1) * 128], identb[:], [h, identb], [PS[0]])
            fw.cp(act, hT[:].rearrange("p a b -> p (a b)"), pb, [PS[0]], [hT])
            return hT

        def load_w(dst, src_ap, nk, c0, c1, rows0=0):
            for kt in range(nk):
                fw.dma(pool, dst[:, kt, :], src_ap[rows0 + kt * 128: rows0 + (kt + 1) * 128, c0:c1], writes=[dst])

        fw.ckpt(0, stop_after)
        with contextlib.ExitStack() as s1:
            Wu = fw.sb("Wu", [128, 8, 512], BF16, s1)
            load_w(Wu, w_in, 8, 0, 512)
            BbT = fw.sb("BbT", [128, 4, 1024], BF16, s1)
            Cpad = fw.sb("Cpad", [128, 16, 2, 128], BF16, s1)
            Wt = fw.sb("Wt", [128, 16, 2, 128], F32, s1)
            Atr = fw.sb("Atr", [128, 16, 128], F32, s1); Ati = fw.sb("Ati", [128, 16, 128], F32, s1)
            a_r = fw.sb("a_r", [128, 16], F32, s1); a_i = fw.sb("a_i", [128, 16], F32, s1)
            Dcol = fw.sb("Dcol", [128, 4], F32, s1)
            with contextlib.ExitStack() as s0:
                nat = fw.sb("nat", [16, 3, 128], F32, s0)
                fw.dma(sp, nat[:, 0, :], lamre, writes=[nat])
                fw.dma(sp, nat[:, 1, :], lamim, writes=[nat])
                ldt = fw.sb("ldt", [16, 2], F32, s0)
                fw.dma(sp, ldt[:], logdt, writes=[ldt])
                fw.cp(dve, nat[:, 2, :].rearrange("p (a b) -> p a b", a=2), ldt[:].unsqueeze(2).to_broadcast([16, 2, 64]), [ldt, nat], [nat])
                L = fw.sb("L", [128, 3, 16], F32, s0)
                for i in range(3):
                    fw.tr(PS[1][:, i * 16:(i + 1) * 16], nat[:, i, :], ident[0:16, 0:16], [nat, ident], [PS[1]])
                fw.cp(dve, L[:].rearrange("p a b -> p (a b)"), PS[1][:, 0:48], [PS[1]], [L])
                lr = L[:, 0, :]; li = L[:, 1, :]
                w = fw.sb("wk", [128, 12, 16], F32, s0)
                W = lambda i: w[:, i, :]
                def horner(dst, x, coefs, rd):
                    q = W(11)
                    fw.ts(dve, q, x, float(coefs[-1]), None, ALU.mult, ALU.bypass, rd + [w], [w])
                    for c in coefs[-2:0:-1]:
                        fw.stt(dve, q, q, float(c), x, ALU.add, ALU.mult, rd + [w], [w])
                    fw.ts(dve, dst, q, float(coefs[0]), None, ALU.add, ALU.bypass, [w], [w])
                ecoef = [1.0 / math.factorial(k) for k in range(10)]
                fw.ts(dve, W(2), L[:, 2, :], 1.0 / 16, None, ALU.mult, ALU.bypass, [L], [w])
                horner(W(0), W(2), ecoef, [])
                for _ in range(4):
                    fw.tt(dve, W(0), W(0), W(0), ALU.mult, [w], [w])
                fw.tt(dve, W(3), lr, W(0), ALU.mult, [L, w], [w])
                horner(W(1), W(3), ecoef, [])
                fw.ts(dve, W(3), W(3), -1.0, None, ALU.mult, ALU.bypass, [w], [w])
                horner(W(9), W(3), ecoef, [])
                fw.tt(dve, W(2), li, W(0), ALU.mult, [L, w], [w])
                ki = fw.sb("ki", [128, 16], I32, s0)
                C1 = 6.28125
                C2 = 2 * math.pi - C1
                fw.ts(dve, ki[:], W(2), 1.0 / (2 * math.pi), None, ALU.mult, ALU.bypass, [w], [ki])
                fw.cp(dve, W(3), ki[:], [ki], [w])
                fw.stt(dve, W(4), W(3), -C1, W(2), ALU.mult, ALU.add, [w], [w])
                fw.stt(dve, W(4), W(3), -C2, W(4), ALU.mult, ALU.add, [w], [w])
                fw.ts(dve, W(3), W(4), math.pi, -2 * math.pi, ALU.is_gt, ALU.mult, [w], [w])
                fw.tt(dve, W(4), W(4), W(3), ALU.add, [w], [w])
                fw.ts(dve, W(3), W(4), -math.pi, 2 * math.pi, ALU.is_lt, ALU.mult, [w], [w])
                fw.tt(dve, W(4), W(4), W(3), ALU.add, [w], [w])
                fw.tt(dve, W(5), W(4), W(4), ALU.mult, [w], [w])
                scoef = [(-1.0) ** k / math.factorial(2 * k + 1) for k in range(12)]
                ccoef = [(-1.0) ** k / math.factorial(2 * k) for k in range(13)]
                horner(W(6), W(5), ccoef, [])
                horner(W(3), W(5), scoef, [])
                fw.tt(dve, W(4), W(3), W(4), ALU.mult, [w], [w])
                fw.tt(dve, a_r[:], W(1), W(6), ALU.mult, [w], [a_r])
                fw.tt(dve, a_i[:], W(1), W(4), ALU.mult, [w], [a_i])
                fw.tt(dve, W(10), W(9), W(4), ALU.mult, [w], [w])
                fw.ts(dve, W(10), W(10), -1.0, None, ALU.mult, ALU.bypass, [w], [w])
                fw.tt(dve, W(9), W(9), W(6), ALU.mult, [w], [w])
                fw.tt(dve, W(0), lr, lr, ALU.mult, [L], [w])
                fw.tt(dve, W(1), li, li, ALU.mult, [L], [w])
                fw.tt(dve, W(0), W(0), W(1), ALU.add, [w], [w])
                fw.recip(W(0), W(0), [w], [w])
                fw.ts(dve, W(1), a_r[:], -1.0, None, ALU.add, ALU.bypass, [a_r], [w])
                fw.tt(dve, W(2), W(1), lr, ALU.mult, [w, L], [w])
                fw.tt(dve, W(3), a_i[:], li, ALU.mult, [a_i, L], [w])
                fw.tt(dve, W(2), W(2), W(3), ALU.add, [w], [w])
                fw.tt(dve, W(7), W(2), W(0), ALU.mult, [w], [w])
                fw.tt(dve, W(2), a_i[:], lr, ALU.mult, [a_i, L], [w])
                fw.tt(dve, W(3), W(1), li, ALU.mult, [w, L], [w])
                fw.tt(dve, W(2), W(2), W(3), ALU.subtract, [w], [w])
                fw.tt(dve, W(8), W(2), W(0), ALU.mult, [w], [w])
                bL = fw.sb("bL", [128, 2, 16, 16], F32, s0)
                fw.dma(sp, bL[:, 0, :, :], bre.rearrange("g q c -> q g c"), writes=[bL])
                fw.dma(sp, bL[:, 1, :, :], bim.rearrange("g q c -> q g c"), writes=[bL])
                bb = fw.sb("bb", [128, 2, 16, 16], F32, s0)
                tmpb = fw.sb("tmpb", [128, 16, 16], F32, s0)
                frb = W(7).unsqueeze(2).to_broadcast([128, 16, 16]); fib = W(8).unsqueeze(2).to_broadcast([128, 16, 16])
                fw.tt(dve, bb[:, 0], bL[:, 0], frb, ALU.mult, [bL, w], [bb])
                fw.tt(dve, tmpb[:], bL[:, 1], fib, ALU.mult, [bL, w], [tmpb])
                fw.tt(dve, bb[:, 0], bb[:, 0], tmpb[:], ALU.subtract, [bb, tmpb], [bb])
                fw.tt(dve, bb[:, 1], bL[:, 1], frb, ALU.mult, [bL, w], [bb])
                fw.tt(dve, tmpb[:], bL[:, 0], fib, ALU.mult, [bL, w], [tmpb])
                fw.tt(dve, bb[:, 1], bb[:, 1], tmpb[:], ALU.add, [bb, tmpb], [bb])
                mZ = fw.sb("mZ", [128, 4, 128], F32, s0); mC = fw.sb("mC", [128, 4, 128], F32, s0)
                fw.dma(sp, mZ[:].rearrange("p a b -> p (a b)"), c_maskZ, writes=[mZ])
                fw.dma(sp, mC[:].rearrange("p a b -> p (a b)"), c_maskC, writes=[mC])
                Zr = Ring([fw.sb("Z%d" % i, [128, 128], F32, s0) for i in range(2)])
                for ct in range(4):
                    for gpl in range(4):
                        gp = 4 * ct + gpl
                        for ri in range(2):
                            Z = Zr.get()
                            fw.tt(dve, Z[:].rearrange("p (a b) -> p a b", a=8), mZ[:, gpl, :].rearrange("p (a b) -> p a b", a=8),
                                  bb[:, ri, gp, :].unsqueeze(1).to_broadcast([128, 8, 16]), ALU.mult, [mZ, bb], [Z])
                            bk = PS[2 + (gp * 2 + ri) % 2]
                            fw.tr(bk[:, 0:128], Z[:], ident[:], [Z, ident], [bk])
                            fw.cp(act, BbT[:, ct, (gpl * 2 + ri) * 128:(gpl * 2 + ri + 1) * 128], bk[:, 0:128], [bk], [BbT])
                Cn = fw.sb("Cn", [128, 2, 4, 64], F32, s0)
                fw.dma(sp, Cn[:, 0], cre.rearrange("c q p -> q c p"), writes=[Cn])
                fw.dma(sp, Cn[:, 1], cim.rearrange("c q p -> q c p"), writes=[Cn])
                for ct in range(4):
                    for gpl in range(4):
                        gp = 4 * ct + gpl
                        for ri in range(2):
                            Z = Zr.get()
                            fw.tt(dve, Z[:].rearrange("p (a b) -> p a b", a=2), mC[:, gpl, :].rearrange("p (a b) -> p a b", a=2),
                                  Cn[:, ri, ct, :].unsqueeze(1).to_broadcast([128, 2, 64]), ALU.mult, [mC, Cn], [Z])
                            bk = PS[2 + (gp * 2 + ri) % 2]
                            fw.tr(bk[:, 0:128], Z[:], ident[:], [Z, ident], [bk])
                            fw.actv(Cpad[:, gp, ri, :], bk[:, 0:128], AF.Copy, [bk], [Cpad], scale=(1.0 if ri == 0 else -1.0))
                fw.dma(sp, Dcol[:], dsk, writes=[Dcol])
                Avr = fw.sb("Avr", [128, 16, 128], F32, s0); Avi = fw.sb("Avi", [128, 16, 128], F32, s0)
                pt1 = fw.sb("pt1", [128, 16, 64], F32, s0); pt2 = fw.sb("pt2", [128, 16, 64], F32, s0)

                def cpow(Pr, Pi, br, bi, rd):
                    fw.cp(dve, Pr[:, :, 0:1], br.unsqueeze(2), rd, [Pr])
                    fw.cp(dve, Pi[:, :, 0:1], bi.unsqueeze(2), rd, [Pi])
                    k = 1
                    while k < 128:
                        akr = Pr[:, :, k - 1:k].to_broadcast([128, 16, k]); aki = Pi[:, :, k - 1:k].to_broadcast([128, 16, k])
                        fw.tt(dve, pt1[:, :, 0:k], Pr[:, :, 0:k], akr, ALU.mult, [Pr], [pt1])
                        fw.tt(dve, pt2[:, :, 0:k], Pi[:, :, 0:k], aki, ALU.mult, [Pi], [pt2])
                        fw.tt(dve, Pr[:, :, k:2 * k], pt1[:, :, 0:k], pt2[:, :, 0:k], ALU.subtract, [pt1, pt2, Pr], [Pr])
                        fw.tt(dve, pt1[:, :, 0:k], Pr[:, :, 0:k], aki, ALU.mult, [Pr, Pi], [pt1])
                        fw.tt(dve, pt2[:, :, 0:k], Pi[:, :, 0:k], akr, ALU.mult, [Pr, Pi], [pt2])
                        fw.tt(dve, Pi[:, :, k:2 * k], pt1[:, :, 0:k], pt2[:, :, 0:k], ALU.add, [pt1, pt2, Pi], [Pi])
                        k *= 2
                cpow(Atr, Ati, a_r[:], a_i[:], [a_r, a_i])
                cpow(Avr, Avi, W(9), W(10), [w])
                for gp in range(16):
                    for ri, Av in ((0, Avr), (1, Avi)):
                        bk = PS[2 + (gp * 2 + ri) % 2]
                        fw.tr(bk[:, 0:128], Av[:, gp, :], ident[:], [Av, ident], [bk])
                        fw.cp(act, Wt[:, gp, ri, :], bk[:, 0:128], [bk], [Wt])

            fw.barrier()
            fw.ckpt(1, stop_after)
            uTb_r = Ring([fw.sb("uTb%d" % i, [128, 4, 128], BF16, s1) for i in range(2)])
            uTf_r = Ring([fw.sb("uTf%d" % i, [128, 4, 128], F32, s1) for i in range(2)])
            X_r = Ring([fw.sb("X%d" % i, [128, 4, 2, 128], BF16, s1) for i in range(3)])
            scr_r = {nm: Ring([fw.sb(nm + str(i), [128, 4, 128], F32, s1) for i in range(2)]) for nm in ("e1", "e2", "e3", "e4", "Gr", "Gi", "Hr", "Hi")}
            Hb_r = Ring([fw.sb("Hb%d" % i, [128, 2, 4, 128], BF16, s1) for i in range(2)])
            Hc = fw.sb("Hc", [128, 2, 16], F32, s1)
            fw.mset(dve, Hc[:], 0.0, [Hc])
            yall = fw.sb("yall", [128, 2, 4, 128], BF16, s1)
            yt1 = fw.sb("yt1", [128, 4, 128], F32, s1); yt2 = fw.sb("yt2", [128, 4, 128], F32, s1); yt3 = fw.sb("yt3", [128, 4, 128], F32, s1)
            S0 = fw.sb("S0", [128, 2048], F32, s1)
            H0 = fw.sb("H0", [128, 2, 16, 128], F32, s1)

            blk = {}

            def Pre(b):
                samp = (b == 32)
                xt = xring.get()
                fw.dma(sp, xt[:], xa[b * 128:(b + 1) * 128, :], writes=[xt])
                hT = norm_h(xt, gB["pre"])
                for ct in range(4):
                    for kt in range(8):
                        fw.mm(PS[1][:, ct * 128:(ct + 1) * 128], Wu[:, kt, ct * 128:(ct + 1) * 128], hT[:, kt, :], kt == 0, kt == 7, [Wu, hT], [PS[1]])
                uTb = uTb_r.get(); uTf = uTf_r.get()
                fw.cp(act, uTb[:].rearrange("p a b -> p (a b)"), PS[1][:], [PS[1]], [uTb])
                fw.cp(dve, uTf[:].rearrange("p a b -> p (a b)"), PS[1][:], [PS[1]], [uTf])
                blk[b] = (uTb, uTf)
                if samp:
                    for ri, src in ((0, s0re), (1, s0im)):
                        fw.mset(dve, S0[:], 0.0, [S0])
                        fw.dma(sp, S0[0:16, :], src, writes=[S0])
                        for gp in range(16):
                            fw.tr(PS[2][:, 0:128], S0[:, gp * 128:(gp + 1) * 128], ident[:], [S0, ident], [PS[2]])
                            fw.cp(act, H0[:, ri, gp, :], PS[2][:, 0:128], [PS[2]], [H0])

            def A1(b, ct, k):
                samp = (b == 32)
                uTb, uTf = blk[b]
                if samp:
                    return None
                e1, e2, e3, e4 = (scr_r[nm].get() for nm in ("e1", "e2", "e3", "e4"))
                X = X_r.get()
                for hf in range(2):
                    BK = PS[2] if hf == 0 else PS[0]
                    fw.mm(BK[:], uTb[:, ct, :], BbT[:, ct, hf * 512:(hf + 1) * 512], True, True, [uTb, BbT], [BK])
                    Bv = BK[:].rearrange("p (a r q) -> p a r q", a=2, r=2)
                    gsl = slice(4 * ct + 2 * hf, 4 * ct + 2 * hf + 2)
                    sl = slice(2 * hf, 2 * hf + 2)
                    fw.tt(dve, e1[:, sl, :], Bv[:, :, 0, :], Wt[:, gsl, 0, :], ALU.mult, [BK, Wt], [e1])
                    fw.tt(dve, e2[:, sl, :], Bv[:, :, 1, :], Wt[:, gsl, 1, :], ALU.mult, [BK, Wt], [e2])
                    fw.tt(dve, e3[:, sl, :], Bv[:, :, 0, :], Wt[:, gsl, 1, :], ALU.mult, [BK, Wt], [e3])
                    fw.tt(dve, e4[:, sl, :], Bv[:, :, 1, :], Wt[:, gsl, 0, :], ALU.mult, [BK, Wt], [e4])
                fw.tt(dve, X[:, :, 0, :], e1[:], e2[:], ALU.subtract, [e1, e2], [X])
                fw.tt(dve, X[:, :, 1, :], e3[:], e4[:], ALU.add, [e3, e4], [X])
                return X

            def A2(b, ct, k, X):
                samp = (b == 32)
                uTb, uTf = blk[b]
                CB = (PS[4], PS[5]) if k % 2 == 0 else (PS[6], PS[7])
                for gpl in range(4):
                    for ri in range(2):
                        if samp:
                            fw.mm(CB[ri][:, gpl * 128:(gpl + 1) * 128], BbT[:, ct, (gpl * 2 + ri) * 128:(gpl * 2 + ri + 1) * 128],
                                  uTb[:, ct, :], True, True, [uTb, BbT], [CB[ri]])
                        else:
                            fw.mm(CB[ri][:, gpl * 128:(gpl + 1) * 128], X[:, gpl, ri, :], trib[:], True, True, [X, trib], [CB[ri]])
                return CB

            def B(b, ct, CB):
                samp = (b == 32)
                gps = slice(4 * ct, 4 * ct + 4)
                e1, e2, e3, e4, Gr, Gi, Hr, Hi = (scr_r[nm].get() for nm in ("e1", "e2", "e3", "e4", "Gr", "Gi", "Hr", "Hi"))
                cr = CB[0][:].rearrange("p (a b) -> p a b", a=4); ci = CB[1][:].rearrange("p (a b) -> p a b", a=4)
                if not samp:
                    fw.tt(dve, Gr[:], cr, Hc[:, 0, gps].unsqueeze(2).to_broadcast([128, 4, 128]), ALU.add, [CB[0], Hc], [Gr])
                    fw.tt(dve, Gi[:], ci, Hc[:, 1, gps].unsqueeze(2).to_broadcast([128, 4, 128]), ALU.add, [CB[1], Hc], [Gi])
                    fw.tt(dve, e1[:], Gr[:], Atr[:, gps, :], ALU.mult, [Gr, Atr], [e1])
                    fw.tt(dve, e2[:], Gi[:], Ati[:, gps, :], ALU.mult, [Gi, Ati], [e2])
                    fw.tt(dve, e3[:], Gr[:], Ati[:, gps, :], ALU.mult, [Gr, Ati], [e3])
                    fw.tt(dve, e4[:], Gi[:], Atr[:, gps, :], ALU.mult, [Gi, Atr], [e4])
                    fw.tt(dve, Hr[:], e1[:], e2[:], ALU.subtract, [e1, e2], [Hr])
                    fw.tt(dve, Hi[:], e3[:], e4[:], ALU.add, [e3, e4], [Hi])
                    fw.cp(dve, Hc[:, 0, gps], Hr[:, :, 127], [Hr], [Hc])
                    fw.cp(dve, Hc[:, 1, gps], Hi[:, :, 127], [Hi], [Hc])
                else:
                    arb = a_r[:, gps].unsqueeze(2).to_broadcast([128, 4, 128]); aib = a_i[:, gps].unsqueeze(2).to_broadcast([128, 4, 128])
                    fw.tt(dve, e1[:], H0[:, 0, gps, :], arb, ALU.mult, [H0, a_r], [e1])
                    fw.tt(dve, e2[:], H0[:, 1, gps, :], aib, ALU.mult, [H0, a_i], [e2])
                    fw.tt(dve, e3[:], H0[:, 0, gps, :], aib, ALU.mult, [H0, a_i], [e3])
                    fw.tt(dve, e4[:], H0[:, 1, gps, :], arb, ALU.mult, [H0, a_r], [e4])
                    fw.tt(dve, e1[:], e1[:], e2[:], ALU.subtract, [e1, e2], [e1])
                    fw.tt(dve, e3[:], e3[:], e4[:], ALU.add, [e3, e4], [e3])
                    fw.tt(dve, Hr[:], e1[:], cr, ALU.add, [e1, CB[0]], [Hr])
                    fw.tt(dve, Hi[:], e3[:], ci, ALU.add, [e3, CB[1]], [Hi])
                Hb = Hb_r.get()
                fw.cp(act, Hb[:, 0], Hr[:], [Hr], [Hb])
                fw.cp(act, Hb[:, 1], Hi[:], [Hi], [Hb])
                if samp:
                    for ri, Hx in ((0, Hr), (1, Hi)):
                        for gpl in range(4):
                            fw.tr(PS[0][:, gpl * 128:(gpl + 1) * 128], Hx[:, gpl, :], ident[:], [Hx, ident], [PS[0]])
                        fw.cp(dve, H0[:, ri, gps, :].rearrange("p a b -> p (a b)"), PS[0][:], [PS[0]], [H0])
                for gpl in range(4):
                    for ri in range(2):
                        fw.mm(PS[3][:, ct * 128:(ct + 1) * 128], Cpad[:, 4 * ct + gpl, ri, :], Hb[:, ri, gpl, :],
                              gpl == 0 and ri == 0, gpl == 3 and ri == 1, [Cpad, Hb], [PS[3]])

            def Post(b):
                samp = (b == 32)
                uTb, uTf = blk[b]
                fw.tt(pool, yt1[:], uTf[:], Dcol[:].unsqueeze(2).to_broadcast([128, 4, 128]), ALU.mult, [uTf, Dcol], [yt1])
                fw.tt(dve, yt1[:], yt1[:], PS[3][:].rearrange("p (a b) -> p a b", a=4), ALU.add, [yt1, PS[3]], [yt1])
                fw.tt(pool, yt2[:], yt1[:], yt1[:], ALU.mult, [yt1], [yt2])
                fw.ts(pool, yt2[:], yt2[:], 0.044715, 1.0, ALU.mult, ALU.add, [yt2], [yt2])
                fw.tt(pool, yt2[:], yt2[:], yt1[:], ALU.mult, [yt2, yt1], [yt2])
                fw.actv(yt3[:], yt2[:], AF.Sigmoid, [yt2], [yt3], scale=2.0 * math.sqrt(2.0 / math.pi))
                if samp:
                    fw.tt(dve, ysown[:, 16], yt1[:], yt3[:], ALU.mult, [yt1, yt3], [ysown])
                    fw.dma(sp, o_sre, H0[:, 0].rearrange("p a b -> p (a b)"), reads=[H0], is_out=True)
                    fw.dma(sp, o_sim, H0[:, 1].rearrange("p a b -> p (a b)"), reads=[H0], is_out=True)
                else:
                    fw.tt(dve, yall[:, b % 2], yt1[:], yt3[:], ALU.mult, [yt1, yt3], [yall])
                    if b % 2 == 1:
                        m = b // 2
                        fw.ts(pool, yt2[:], yall[:, 0], sel[:, 0:1], None, ALU.mult, ALU.bypass, [yall, sel], [yt2])
                        fw.stt(dve, ysown[:, m], yall[:, 1], sel[:, 1:2], yt2[:], ALU.mult, ALU.add, [yall, sel, yt2], [ysown])
                if b == 31:
                    fw.dma(sp, o_hre, Hc[:, 0, :], reads=[Hc], is_out=True)
                    fw.dma(sp, o_him, Hc[:, 1, :], reads=[Hc], is_out=True)

            items = [(b, ct) for b in range(33) for ct in range(4)]
            NI = len(items)
            Pre(0)
            xs = {0: A1(0, 0, 0), 1: A1(0, 1, 1)}
            cbs = {0: A2(0, 0, 0, xs[0])}
            for k, (b, ct) in enumerate(items):
                if k + 2 < NI:
                    nb, nct = items[k + 2]
                    if nct == 0:
                        Pre(nb)
                    xs[k + 2] = A1(nb, nct, k + 2)
                if k + 1 < NI:
                    nb, nct = items[k + 1]
                    cbs[k + 1] = A2(nb, nct, k + 1, xs.pop(k + 1))
                B(b, ct, cbs.pop(k))
                if ct == 3:
                    Post(b)
                    fw.ckpt(2 + b, stop_after)

        fw.barrier()
        fw.ckpt(40, stop_after)
        subB = fw.sb("subB", [128, 128], F32, sm)
        fw.ts(dve, subB[:], sublnB[:], 1.0 - LAM_INIT, None, ALU.mult, ALU.bypass, [sublnB], [subB])
        rt = [fw.sb("rt%d" % i, [128, 8, 8], F32, sm) for i in range(4)]

        def rope(Xf, cs, sn):
            v = Xf[:].rearrange("p (a d) -> p a d", a=8)
            x1 = v[:, :, 0:8]; x2 = v[:, :, 8:16]
            cb = cs.unsqueeze(1).to_broadcast([128, 8, 8]); sb_ = sn.unsqueeze(1).to_broadcast([128, 8, 8])
            fw.tt(dve, rt[0][:], x1, cb, ALU.mult, [Xf], [rt[0]])
            fw.tt(dve, rt[1][:], x2, sb_, ALU.mult, [Xf], [rt[1]])
            fw.tt(dve, rt[2][:], x2, cb, ALU.mult, [Xf], [rt[2]])
            fw.tt(dve, rt[3][:], x1, sb_, ALU.mult, [Xf], [rt[3]])
            fw.tt(dve, x1, rt[0][:], rt[1][:], ALU.subtract, [rt[0], rt[1]], [Xf])
            fw.tt(dve, x2, rt[2][:], rt[3][:], ALU.add, [rt[2], rt[3]], [Xf])

        for hg in range(2):
            with contextlib.ExitStack() as s2:
                Wq = fw.sb("Wq", [128, 8, 512], BF16, s2); Wk = fw.sb("Wk", [128, 8, 512], BF16, s2); Wv = fw.sb("Wv", [128, 8, 512], BF16, s2)
                load_w(Wk, w_in, 8, 1536 + hg * 512, 1536 + hg * 512 + 512)
                load_w(Wv, w_in, 8, 2560 + hg * 512, 2560 + hg * 512 + 512)
                load_w(Wq, w_in, 8, 512 + hg * 512, 512 + hg * 512 + 512)
                KT = fw.sb("KT", [128, 4, 4096], BF16, s2)
                V = fw.sb("V", [128, 32, 4, 130], BF16, s2)
                fw.mset(pool, V[:], 1.0, [V])
                Kf_r = Ring([fw.sb("Kf%d" % i, [128, 512], F32, s2) for i in range(3)])
                Vf_r = Ring([fw.sb("Vf%d" % i, [128, 512], F32, s2) for i in range(4)])
                Qf_r = Ring([fw.sb("Qf%d" % i, [128, 512], F32, s2) for i in range(2)])
                Kb_r = Ring([fw.sb("Kb%d" % i, [128, 512], BF16, s2) for i in range(6)])
                QT_r = Ring([fw.sb("QT%d" % i, [128, 4, 128], BF16, s2) for i in range(2)])
                PT_r = Ring([fw.sb("PT%d" % i, [128, 4, 128], BF16, s2) for i in range(4)])
                o1 = fw.sb("o1", [128, 128], F32, s2); oh = fw.sb("oh", [128, 128], F32, s2)
                ob_r = Ring([fw.sb("ob%d" % i, [128, 512], BF16, s2) for i in range(2)])
                nonlocal_cnt = [0]

                xf_r = Ring([fw.sb("xf%d" % i, [128, D], F32, s2) for i in range(2)])
                hTf_r = Ring([fw.sb("hTf%d" % i, [128, 8, 128], BF16, s2) for i in range(3)])

                def front(b):
                    samp = (b == 32)
                    xt = xf_r.get()
                    fw.dma(sp, xt[:], xa[b * 128:(b + 1) * 128, :], writes=[xt])
                    hT = norm_h(xt, gB["pre"], hTf_r)
                    for kt in range(8):
                        fw.mm(PS[1][:], hT[:, kt, :], Wk[:, kt, :], kt == 0, kt == 7, [hT, Wk], [PS[1]])
                    for kt in range(8):
                        fw.mm(PS[2][:], hT[:, kt, :], Wv[:, kt, :], kt == 0, kt == 7, [hT, Wv], [PS[2]])
                    Kf = Kf_r.get(); Vf = Vf_r.get()
                    fw.cp(dve, Kf[:], PS[1][:], [PS[1]], [Kf])
                    rope(Kf, cosA[:, b, :], sinA[:, b, :])
                    fw.cp(dve, Vf[:], PS[2][:], [PS[2]], [Vf])
                    fw.dma(sp, o_k[b * 128:(b + 1) * 128, hg * 512:(hg + 1) * 512], Kf[:], reads=[Kf], is_out=True)
                    fw.dma(sp, o_v[b * 128:(b + 1) * 128, hg * 512:(hg + 1) * 512], Vf[:], reads=[Vf], is_out=True)
                    Kb = None
                    if samp:
                        fw.cp(pool, ks_t[:, hg * 512:(hg + 1) * 512], Kf[:], [Kf], [ks_t])
                        fw.cp(pool, vs_t[:, hg * 512:(hg + 1) * 512], Vf[:], [Vf], [vs_t])
                    else:
                        Kb = Kb_r.get()
                        fw.cp(pool, Kb[:], Kf[:], [Kf], [Kb])
                    return (hT, Vf, Kb)

                fq = [front(0), front(1)]
                for b in range(33):
                    samp = (b == 32)
                    if b + 2 < 33:
                        fq.append(front(b + 2))
                    hT, Vf, Kb = fq.pop(0)
                    if not samp:
                        pb = psbf(PS[0])
                        for hl in range(4):
                            fw.tr(pb[:, hl * 128:(hl + 1) * 128], Kb[:, hl * 128:(hl + 1) * 128], identb[:], [Kb, identb], [PS[0]])
                        fw.cp(act, KT[:, :, b * 128:(b + 1) * 128], pb[:, 0:512].rearrange("p (a b) -> p a b", a=4), [PS[0]], [KT])
                        fw.cp(pool, V[:, b, :, 0:128], Vf[:].rearrange("p (a b) -> p a b", a=4), [Vf], [V])
                    if not (samp or b % 2 == 1):
                        continue
                    m = 16 if samp else b // 2
                    if samp:
                        hTo = hT
                    else:
                        xo_t = xring.get()
                        fw.dma(sp, xo_t[:], xo[m * 128:(m + 1) * 128, :], writes=[xo_t])
                        hTo = norm_h(xo_t, gB["pre"])
                    for kt in range(8):
                        fw.mm(PS[3][:], hTo[:, kt, :], Wq[:, kt, :], kt == 0, kt == 7, [hTo, Wq], [PS[3]])
                    Qf = Qf_r.get()
                    fw.cp(dve, Qf[:], PS[3][:], [PS[3]], [Qf])
                    rope(Qf, cosO[:, m, :], sinO[:, m, :])
                    if samp:
                        fw.cp(pool, qs_t[:, hg * 512:(hg + 1) * 512], Qf[:], [Qf], [qs_t])
                        continue
                    Qb = Kb_r.get()
                    fw.cp(pool, Qb[:], Qf[:], [Qf], [Qb])
                    pb = psbf(PS[0])
                    for hl in range(4):
                        fw.tr(pb[:, hl * 128:(hl + 1) * 128], Qb[:, hl * 128:(hl + 1) * 128], identb[:], [Qb, identb], [PS[0]])
                    QT = QT_r.get()
                    fw.cp(act, QT[:].rearrange("p a b -> p (a b)"), pb[:, 0:512], [PS[0]], [QT])
                    nkb = b + 1
                    ob = ob_r.get()
                    items = []
                    for hl in range(4):
                        for c in range(2):
                            for kb0 in range(0, nkb, 4):
                                items.append((hl, c, kb0, min(4, nkb - kb0)))

                    def emit_qk(it):
                        hl, c, kb0, n = it
                        nonlocal_cnt[0] += 1
                        sbank = PS[4 + nonlocal_cnt[0] % 2]
                        for i in range(n):
                            kb = kb0 + i
                            fw.mm(sbank[:, i * 128:(i + 1) * 128], KT[c * 64:(c + 1) * 64, hl, kb * 128:(kb + 1) * 128],
                                  QT[c * 64:(c + 1) * 64, hl, :], True, True, [KT, QT], [sbank])
                        return sbank
                    pend = emit_qk(items[0])
                    for ii, it in enumerate(items):
                        hl, c, kb0, n = it
                        sbank = pend
                        if ii + 1 < len(items):
                            pend = emit_qk(items[ii + 1])
                        accb = (PS[6], PS[7]) if hl % 2 == 0 else (PS[2], PS[3])
                        acc = accb[c]
                        PT = PT_r.get()
                        fw.actv(PT[:].rearrange("p a b -> p (a b)")[:, 0:n * 128], sbank[:, 0:n * 128], AF.Exp, [sbank], [PT], scale=0.125)
                        for i in range(n):
                            kb = kb0 + i
                            if kb == nkb - 2:
                                fw.tt(dve, PT[:, i, :], PT[:, i, :], maskA[:], ALU.mult, [PT, maskA], [PT])
                            if kb == nkb - 1:
                                fw.tt(dve, PT[:, i, :], PT[:, i, :], maskB[:], ALU.mult, [PT, maskB], [PT])
                        for i in range(n):
                            kb = kb0 + i
                            fw.mm(acc[:, 0:129], PT[:, i, :], V[:, kb, hl, 0:129], kb == 0, kb == nkb - 1, [PT, V], [acc])
                        if not (c == 1 and kb0 + n == nkb):
                            continue
                        A0, A1 = accb
                        rs = sring.get()
                        fw.recip(rs[:, 0:1], A0[:, 128:129], [A0], [rs])
                        fw.recip(rs[:, 1:2], A1[:, 128:129], [A1], [rs])
                        fw.tt(dve, rs[:, 2:3], rs[:, 1:2], lam[:], ALU.mult, [rs, lam], [rs])
                        fw.ts(dve, o1[:], A1[:, 0:128], rs[:, 2:3], None, ALU.mult, ALU.bypass, [A1, rs], [o1])
                        fw.stt(dve, oh[:], A0[:, 0:128], rs[:, 0:1], o1[:], ALU.mult, ALU.subtract, [A0, rs, o1], [oh])
                        st2 = sring.get()
                        fw.op(dve, lambda e, o=junk[:, 0:128], i=oh[:], a=st2[:, 0:1]: e.scalar_tensor_tensor(out=o, in0=i, scalar=1.0, in1=i, op0=ALU.mult, op1=ALU.mult, accum_out=a), [oh], [junk, st2])
                        r = rstd_from(st2[:, 0:1], st2, 128)
                        fw.stt(dve, ob[:, hl * 128:(hl + 1) * 128], oh[:], r, subB[:], ALU.mult, ALU.mult, [oh, st2, subB], [ob])
                    fw.dma(sp, oscr[m * 128:(m + 1) * 128, hg * 512:(hg + 1) * 512], ob[:], reads=[ob])
            fw.barrier()
        fw.ckpt(50, stop_after)

        with contextlib.ExitStack() as s5:
            pt_i = fw.sb("pt_i", [128, 256], I32, s5); pt_f = fw.sb("pt_f", [128, 256], F32, s5); idx = fw.sb("idx", [128, 256], I32, s5)
            iot = fw.sb("iot", [128, 1], F32, s5)
            fw.dma(sp, pt_i[:], ptab.partition_broadcast(128), writes=[pt_i])
            fw.dma(sp, iot[:], c_iota, writes=[iot])
            fw.cp(dve, pt_f[:], pt_i[:], [pt_i], [pt_f])
            fw.ts(dve, pt_f[:], pt_f[:], 128.0, iot[:, 0:1], ALU.mult, ALU.add, [pt_f, iot], [pt_f])
            fw.cp(dve, idx[:], pt_f[:], [pt_f], [idx])
            selb_t = fw.sb("selb_t", [16, 16, 128], F32, s5)
            fw.dma(sp, selb_t[:].rearrange("p a b -> p (a b)"), c_selb, writes=[selb_t])
            Kp_r = Ring([fw.sb("Kp%d" % i, [128, 1024], F32, s5) for i in range(3)])
            Vp_r = Ring([fw.sb("Vp%d" % i, [128, 1024], F32, s5) for i in range(3)])
            Vb_r = Ring([fw.sb("Vb%d" % i, [128, 8, 130], BF16, s5) for i in range(3)])
            for t in Vb_r.tiles:
                fw.mset(pool, t[:], 1.0, [t])
            prod = fw.sb("prod", [128, 1024], F32, s5); sc = fw.sb("sc", [128, 16], F32, s5)
            Pz_r = Ring([fw.sb("Pz%d" % i, [128, 16, 16], BF16, s5) for i in range(3)])
            qB = fw.sb("qB", [128, 1024], F32, s5)
            zer = fw.sb("zer", [128, 16], BF16, s5)
            fw.mset(dve, zer[:], 0.0, [zer])
            fw.mset(dve, junk[:], 0.0, [junk])
            for k in range(6):
                fw.mm(PS[k][0:16, :], zer[:, 0:16], junk[:, 0:512], True, False, [zer, junk], [PS[k]])
            for bs in range(16):
                for hf in range(2):
                    fw.mm(PS[6 + hf][:], selb_t[:, bs, :], qs_t[0:16, hf * 512:(hf + 1) * 512], True, True, [selb_t, qs_t], [PS[6 + hf]])
                    fw.cp(act, qB[:, hf * 512:(hf + 1) * 512], PS[6 + hf][:], [PS[6 + hf]], [qB])
                for pg in range(16):
                    col = bs * 16 + pg
                    Kp = Kp_r.get(); Vp = Vp_r.get()
                    fw.dma(pool, Kp[:], cache_k, reads=[idx], writes=[Kp], indirect=idx[:, col:col + 1])
                    fw.dma(pool, Vp[:], cache_v, reads=[idx], writes=[Vp], indirect=idx[:, col:col + 1])
                    fw.tt(dve, prod[:], Kp[:], qB[:], ALU.mult, [Kp, qB], [prod])
                    fw.red(dve, sc[:], prod[:].rearrange("p (a d) -> p a d", a=16), [prod], [sc])
                    Pz = Pz_r.get()
                    fw.mset(pool, Pz[:], 0.0, [Pz])
                    fw.actv(Pz[:, :, bs], sc[:], AF.Exp, [sc], [Pz], scale=0.125)
                    Vb = Vb_r.get()
                    fw.cp(act, Vb[:, :, 0:128], Vp[:].rearrange("p (a b) -> p a b", a=8), [Vp], [Vb])
                    for hc in range(16):
                        bk = PS[hc // 3]; off = (hc % 3) * 129
                        fw.mm(bk[0:16, off:off + 129], Pz[:, hc, :], Vb[:, hc // 2, 0:129], False, False, [Pz, Vb], [bk])
            psf = fw.sb("psf", [16, 16], F32, s5); Ot = fw.sb("Ot", [16, 16, 128], F32, s5); St = fw.sb("St", [16, 16], F32, s5)
            obs = fw.sb("obs", [128, 1024], BF16, s5)
            fw.mset(pool, obs[:], 0.0, [obs])
            fw.tt(dve, prod[0:16, :], qs_t[0:16, :], ks_t[0:16, :], ALU.mult, [qs_t, ks_t], [prod])
            fw.red(dve, sc[0:16, :], prod[0:16, :].rearrange("p (a d) -> p a d", a=16), [prod], [sc])
            fw.actv(psf[:], sc[0:16, :], AF.Exp, [sc], [psf], scale=0.125)
            for hc in range(16):
                bk = PS[hc // 3]; off = (hc % 3) * 129; h_ = hc // 2
                fw.stt(dve, Ot[:, hc, :], vs_t[0:16, h_ * 128:(h_ + 1) * 128], psf[:, hc:hc + 1], bk[0:16, off:off + 128], ALU.mult, ALU.add, [vs_t, psf, bk], [Ot])
                fw.tt(dve, St[:, hc:hc + 1], bk[0:16, off + 128:off + 129], psf[:, hc:hc + 1], ALU.add, [bk, psf], [St])
            fw.recip(St[:], St[:], [St], [St])
            l1 = fw.sb("l1", [16, 4], F32, s5); o1s = fw.sb("o1s", [16, 128], F32, s5); ohs = fw.sb("ohs", [16, 128], F32, s5)
            for h_ in range(8):
                fw.tt(dve, l1[:, 0:1], St[:, 2 * h_ + 1:2 * h_ + 2], lam[0:16, :], ALU.mult, [St, lam], [l1])
                fw.ts(dve, o1s[:], Ot[:, 2 * h_ + 1, :], l1[:, 0:1], None, ALU.mult, ALU.bypass, [Ot, l1], [o1s])
                fw.stt(dve, ohs[:], Ot[:, 2 * h_, :], St[:, 2 * h_:2 * h_ + 1], o1s[:], ALU.mult, ALU.subtract, [Ot, St, o1s], [ohs])
                fw.actv(junk[0:16, 0:128], ohs[:], AF.Square, [ohs], [junk, l1], accum=l1[:, 1:2])
                fw.ts(dve, l1[:, 2:3], l1[:, 1:2], 1.0 / 128, EPS, ALU.mult, ALU.add, [l1], [l1])
                fw.actv(l1[:, 3:4], l1[:, 2:3], AF.Sqrt, [l1], [l1])
                fw.recip(l1[:, 3:4], l1[:, 3:4], [l1], [l1])
                fw.stt(dve, obs[0:16, h_ * 128:(h_ + 1) * 128], ohs[:], l1[:, 3:4], subB[0:16, :], ALU.mult, ALU.mult, [ohs, l1, subB], [obs])
            fw.dma(sp, oscr[16 * 128:17 * 128, :], obs[:], reads=[obs])
        fw.barrier()
        fw.ckpt(60, stop_after)

        with contextlib.ExitStack() as s3:
            Wgs = fw.sb("Wgs", [128, 8, 1024], BF16, s3); Wga_ = fw.sb("Wga", [128, 8, 1024], BF16, s3)
            Wa = fw.sb("Wa", [128, 4, 1024], BF16, s3); Wb = fw.sb("Wb", [128, 4, 1024], BF16, s3); Wo = fw.sb("Wo", [128, 8, 1024], BF16, s3)
            load_w(Wgs, w_in, 8, 3584, 4608); load_w(Wga_, w_in, 8, 4608, 5632)
            load_w(Wa, wga, 4, 0, 1024); load_w(Wb, wgb, 4, 0, 1024); load_w(Wo, w_o, 8, 0, 1024)
            sgs = fw.sb("sgs", [128, 1024], F32, s3); sga = fw.sb("sga", [128, 1024], F32, s3)
            sgb = fw.sb("sgb", [128, 1024], F32, s3); t1 = fw.sb("t1", [128, 1024], F32, s3); t2 = fw.sb("t2", [128, 1024], F32, s3)
            obl = fw.sb("obl", [128, 1024], BF16, s3); mixb = fw.sb("mixb", [128, 1024], BF16, s3)
            mixT = fw.sb("mixT", [128, 8, 128], BF16, s3)
            for m in range(17):
                xt = xring.get()
                fw.dma(sp, xt[:], xo[m * 128:(m + 1) * 128, :], writes=[xt])
                fw.dma(sp, obl[:], oscr[m * 128:(m + 1) * 128, :], writes=[obl])
                hT = norm_h(xt, gB["pre"])
                for hf in range(2):
                    for kt in range(8):
                        fw.mm(PS[1 + hf][:], hT[:, kt, :], Wgs[:, kt, hf * 512:(hf + 1) * 512], kt == 0, kt == 7, [hT, Wgs], [PS[1 + hf]])
                    for kt in range(8):
                        fw.mm(PS[3 + hf][:], hT[:, kt, :], Wga_[:, kt, hf * 512:(hf + 1) * 512], kt == 0, kt == 7, [hT, Wga_], [PS[3 + hf]])
                for hf in range(2):
                    fw.actv(sgs[:, hf * 512:(hf + 1) * 512], PS[1 + hf][:], AF.Sigmoid, [PS[1 + hf]], [sgs])
                    fw.actv(sga[:, hf * 512:(hf + 1) * 512], PS[3 + hf][:], AF.Sigmoid, [PS[3 + hf]], [sga])
                for hf in range(2):
                    for ct in range(4):
                        fw.mm(PS[1 + hf][:], ysown[:, m, ct, :], Wa[:, ct, hf * 512:(hf + 1) * 512], ct == 0, ct == 3, [ysown, Wa], [PS[1 + hf]])
                    for ct in range(4):
                        fw.mm(PS[3 + hf][:], ysown[:, m, ct, :], Wb[:, ct, hf * 512:(hf + 1) * 512], ct == 0, ct == 3, [ysown, Wb], [PS[3 + hf]])
                for hf in range(2):
                    sl = slice(hf * 512, (hf + 1) * 512)
                    fw.actv(sgb[:, sl], PS[3 + hf][:], AF.Sigmoid, [PS[3 + hf]], [sgb])
                    fw.tt(dve, t1[:, sl], PS[1 + hf][:], sgb[:, sl], ALU.mult, [PS[1 + hf], sgb], [t1])
                fw.tt(dve, t1[:], t1[:], sgs[:], ALU.mult, [t1, sgs], [t1])
                fw.tt(dve, t2[:], sga[:], obl[:], ALU.mult, [sga, obl], [t2])
                fw.tt(dve, mixb[:], t1[:], t2[:], ALU.add, [t1, t2], [mixb])
                pb = psbf(PS[0])
                for kt in range(8):
                    fw.tr(pb[:, kt * 128:(kt + 1) * 128], mixb[:, kt * 128:(kt + 1) * 128], identb[:], [mixb, identb], [PS[0]])
                fw.cp(act, mixT[:].rearrange("p a b -> p (a b)"), pb, [PS[0]], [mixT])
                for hf in range(2):
                    for kt in range(8):
                        fw.mm(PS[5 + hf][:], mixT[:, kt, :], Wo[:, kt, hf * 512:(hf + 1) * 512], kt == 0, kt == 7, [mixT, Wo], [PS[5 + hf]])
                    fw.cp(dve, t1[:, hf * 512:(hf + 1) * 512], PS[5 + hf][:], [PS[5 + hf]], [t1])
                stt_ = sring.get()
                fw.actv(junk[:], t1[:], AF.Square, [t1], [junk, stt_], accum=stt_[:, 0:1])
                r = rstd_from(stt_[:, 0:1], stt_, D)
                fw.stt(dve, t2[:], t1[:], r, gB["post"][:], ALU.mult, ALU.mult, [t1, stt_, gB["post"]], [t2])
                fw.tt(dve, t2[:], t2[:], xt[:], ALU.add, [t2, xt], [t2])
                fw.dma(sp, x1scr[m * 128:(m + 1) * 128, :], t2[:], reads=[t2])
        fw.barrier()
        fw.ckpt(70, stop_after)
        if dbg:
            o_dbg = dout('o_dbg', [128, 17 * 512])
            fw.dma(pool, o_dbg, ysown[:].rearrange('p a b c -> p (a b c)'), reads=[ysown], is_out=True)
        sm.close()
        with contextlib.ExitStack() as s4:
            Wg = fw.sb("Wg", [128, 8, 2816], BF16, s4); Wup = fw.sb("Wup", [128, 8, 2816], BF16, s4); Wd = fw.sb("Wd", [128, 22, 1024], BF16, s4)
            load_w(Wg, w_gate, 8, 0, 2816); load_w(Wup, w_up, 8, 0, 2816); load_w(Wd, w_down, 22, 0, 1024)
            actT = fw.sb("actT", [128, 22, 128], BF16, s4)
            sg_r = Ring([fw.sb("sg%d" % i, [128, 512], F32, s4) for i in range(2)])
            tg_r = Ring([fw.sb("tg%d" % i, [128, 512], F32, s4) for i in range(2)])
            fsb = fw.sb("fsb", [128, 1024], F32, s4); ysb = fw.sb("ysb", [128, 1024], F32, s4)
            cnt = 0
            for m in range(17):
                xt = xring.get()
                fw.dma(sp, xt[:], x1scr[m * 128:(m + 1) * 128, :], writes=[xt])
                hT = norm_h(xt, gB["pf"])
                for ft0 in range(0, 22, 4):
                    n = min(4, 22 - ft0)
                    bg = PS[1 + 2 * (cnt % 2)]; bu = PS[2 + 2 * (cnt % 2)]; cnt += 1
                    for i in range(n):
                        ft = ft0 + i
                        for kt in range(8):
                            fw.mm(bg[:, i * 128:(i + 1) * 128], Wg[:, kt, ft * 128:(ft + 1) * 128], hT[:, kt, :], kt == 0, kt == 7, [Wg, hT], [bg])
                        for kt in range(8):
                            fw.mm(bu[:, i * 128:(i + 1) * 128], Wup[:, kt, ft * 128:(ft + 1) * 128], hT[:, kt, :], kt == 0, kt == 7, [Wup, hT], [bu])
                    sg = sg_r.get(); tg = tg_r.get()
                    fw.actv(sg[:, 0:n * 128], bg[:, 0:n * 128], AF.Sigmoid, [bg], [sg])
                    fw.tt(dve, tg[:, 0:n * 128], bg[:, 0:n * 128], sg[:, 0:n * 128], ALU.mult, [bg, sg], [tg])
                    fw.tt(dve, actT[:, ft0:ft0 + n, :].rearrange("p a b -> p (a b)"), bu[:, 0:n * 128], tg[:, 0:n * 128], ALU.mult, [bu, tg], [actT])
                for hf in range(2):
                    for ft in range(22):
                        fw.mm(PS[5 + hf][:], actT[:, ft, :], Wd[:, ft, hf * 512:(hf + 1) * 512], ft == 0, ft == 21, [actT, Wd], [PS[5 + hf]])
                    fw.cp(act, fsb[:, hf * 512:(hf + 1) * 512], PS[5 + hf][:], [PS[5 + hf]], [fsb])
                stt_ = sring.get()
                fw.actv(junk[:], fsb[:], AF.Square, [fsb], [junk, stt_], accum=stt_[:, 0:1])
                r = rstd_from(stt_[:, 0:1], stt_, D)
                fw.stt(dve, ysb[:], fsb[:], r, gB["postf"][:], ALU.mult, ALU.mult, [fsb, stt_, gB["postf"]], [ysb])
                fw.tt(pool, ysb[:], ysb[:], xt[:], ALU.add, [ysb, xt], [ysb])
                fw.dma(sp, o_y[m * 128:(m + 1) * 128, :], ysb[:], reads=[ysb], is_out=True)
        fw.finish()
    return nc


def _consts(j):
    c = {}
    c["c_ident"] = np.eye(128, dtype=np.float32)
    s = np.arange(128)
    c["c_tri"] = (s[:, None] <= s[None, :]).astype(np.float32)
    tri = (s[:, None] <= s[None, :]).astype(np.float32)
    if j == 0:
        c["c_maskA"] = tri; c["c_maskB"] = np.zeros((128, 128), np.float32)
    else:
        c["c_maskA"] = np.ones((128, 128), np.float32); c["c_maskB"] = tri
    mz = np.zeros((128, 4, 128), np.float32)
    for g2 in range(2):
        for gpl in range(4):
            gl = 2 * gpl + g2
            mz[g2 * 64:(g2 + 1) * 64, gpl, gl * 16:(gl + 1) * 16] = 1.0
    c["c_maskZ"] = mz.reshape(128, 512)
    c["c_maskC"] = np.ascontiguousarray(mz.transpose(2, 1, 0)).reshape(128, 512)
    sel = np.zeros((128, 2), np.float32); sel[:, j] = 1.0
    c["c_sel"] = sel
    half = 8
    inv = (500000.0 ** (-np.arange(half, dtype=np.float32) * 2.0 / 16)).astype(np.float32)
    posA = np.concatenate([np.arange(4096), np.full(128, 2048)]).astype(np.float32).reshape(33, 128)
    angA = posA[:, :, None] * inv[None, None, :]
    c["c_cosA"] = np.ascontiguousarray(np.cos(angA).transpose(1, 0, 2)).reshape(128, 33 * 8).astype(np.float32)
    c["c_sinA"] = np.ascontiguousarray(np.sin(angA).transpose(1, 0, 2)).reshape(128, 33 * 8).astype(np.float32)
    blocks = [2 * m + j for m in range(16)] + [32]
    c["c_cosO"] = np.ascontiguousarray(np.cos(angA[blocks]).transpose(1, 0, 2)).reshape(128, 17 * 8).astype(np.float32)
    c["c_sinO"] = np.ascontiguousarray(np.sin(angA[blocks]).transpose(1, 0, 2)).reshape(128, 17 * 8).astype(np.float32)
    sb = np.zeros((16, 16, 128), np.float32)
    for b in range(16):
        sb[b, b, :] = 1.0
    c["c_selb"] = sb.reshape(16, 2048)
    c["c_iota"] = np.arange(128, dtype=np.float32).reshape(128, 1)
    return c


def make_in_maps(inp):
    f = lambda a: np.ascontiguousarray(np.asarray(a))
    maps = []
    ck = f(inp["cache_k"]).reshape(-1, 1024)
    cv = f(inp["cache_v"]).reshape(-1, 1024)
    xp = f(inp["x_prompt"]); xs = f(inp["x_sample"]).reshape(128, 1024)
    shared = {
        "w_in": f(inp["w_in"])[0], "w_glu_a": f(inp["w_glu_a"])[0], "w_glu_b": f(inp["w_glu_b"])[0],
        "w_o": f(inp["w_o"])[0], "w_gate": f(inp["w_gate"])[0], "w_up": f(inp["w_up"])[0], "w_down": f(inp["w_down"])[0],
        "n_pre": f(inp["norm_pre_mix"]), "n_post": f(inp["norm_post_mix"]), "n_pf": f(inp["norm_pre_ffn"]),
        "n_postf": f(inp["norm_post_ffn"]), "subln": f(inp["subln_gain"]),
        "lamq": np.concatenate([f(inp["lambda_q1"]), f(inp["lambda_k1"]), f(inp["lambda_q2"]), f(inp["lambda_k2"])], axis=1),
        "lamre": f(inp["ssm_lambda_re"]).reshape(16, 128), "lamim": f(inp["ssm_lambda_im"]).reshape(16, 128),
        "logdt": f(inp["ssm_log_dt"]).reshape(16, 2),
        "bre": f(inp["ssm_b_re"]).reshape(16, 128, 16), "bim": f(inp["ssm_b_im"]).reshape(16, 128, 16),
        "cre": f(inp["ssm_c_re"]).reshape(4, 128, 64), "cim": f(inp["ssm_c_im"]).reshape(4, 128, 64),
        "dsk": np.ascontiguousarray(f(inp["ssm_d"]).reshape(4, 128).T),
        "cache_k": ck, "cache_v": cv,
    }
    for core in range(8):
        s, j = core // 2, core % 2
        xsb = np.zeros((128, 1024), np.float32)
        xsb[:16] = xs[core * 16:(core + 1) * 16]
        xa = np.concatenate([xp[s], xsb], axis=0)
        own = xp[s].reshape(32, 128, 1024)[j::2].reshape(2048, 1024)
        xo = np.concatenate([own, xsb], axis=0)
        m = dict(shared)
        m.update(_consts(j))
        m["xa"] = xa; m["xo"] = xo
        m["ptab"] = f(inp["page_table"])[core * 16:(core + 1) * 16].reshape(1, 256).astype(np.int32)
        m["s0re"] = f(inp["state_ssm_re"])[0, core * 16:(core + 1) * 16].reshape(16, 2048)
        m["s0im"] = f(inp["state_ssm_im"])[0, core * 16:(core + 1) * 16].reshape(16, 2048)
        maps.append(m)
    return maps


_NC = None


def kernel(**inp):
    global _NC
    if _NC is None:
        _NC = build_nc()
    maps = make_in_maps(inp)
    res = run_bass_kernel_spmd(_NC, maps, core_ids=list(range(8))).results
    return assemble(res)


def assemble(res):
    yp = np.zeros((4, 32, 128, 1024), np.float32)
    ys = np.zeros((128, 1, 1024), np.float32)
    kp = np.zeros((1, 4, 4096, 8, 128), np.float32); vp = np.zeros((1, 4, 4096, 8, 128), np.float32)
    hre = np.zeros((1, 4, 32, 64), np.float32); him = np.zeros((1, 4, 32, 64), np.float32)
    ksm = np.zeros((1, 128, 1, 8, 128), np.float32); vsm = np.zeros((1, 128, 1, 8, 128), np.float32)
    sre = np.zeros((1, 128, 32, 64), np.float32); sim = np.zeros((1, 128, 32, 64), np.float32)
    for core in range(8):
        r = res[core]
        s, j = core // 2, core % 2
        yp[s, j::2] = r["o_y"][:2048].reshape(16, 128, 1024)
        ys[core * 16:(core + 1) * 16, 0] = r["o_y"][2048:2048 + 16]
        if j == 0:
            kp[0, s] = r["o_k"][:4096].reshape(4096, 8, 128)
            vp[0, s] = r["o_v"][:4096].reshape(4096, 8, 128)
            hre[0, s] = r["o_hre"].reshape(2, 64, 16).transpose(2, 0, 1).reshape(32, 64)
            him[0, s] = r["o_him"].reshape(2, 64, 16).transpose(2, 0, 1).reshape(32, 64)
        ksm[0, core * 16:(core + 1) * 16, 0] = r["o_k"][4096:4096 + 16].reshape(16, 8, 128)
        vsm[0, core * 16:(core + 1) * 16, 0] = r["o_v"][4096:4096 + 16].reshape(16, 8, 128)
        sre[0, core * 16:(core + 1) * 16] = r["o_sre"][:16].reshape(16, 32, 64)
        sim[0, core * 16:(core + 1) * 16] = r["o_sim"][:16].reshape(16, 32, 64)
    return (yp.reshape(4, 4096, 1024), ys, kp, vp, hre, him, ksm, vsm, sre, sim)
```
